# Optimizing a Trainium2 kernel written in Bass

```python
import math
import jax
import jax.numpy as jnp
from jax import lax
import numpy as np


D_MODEL = 1024
BATCH = 8
SEQ = 2048
DEPTH = 2

CTX_LEN = 256
GRID_W = 64
HEAD_DIM = 64
ROPE_AXIS_DIM = HEAD_DIM // 2
ROPE_THETA = 10000.0
Q_BLOCK = 128
WINDOW = 128
A_Q_HEADS = 8
A_KV_HEADS = 2
A_GROUP = A_Q_HEADS // A_KV_HEADS
A_WIDTH = A_Q_HEADS * HEAD_DIM
B_HEADS = 4
B_V_DIM = 2 * HEAD_DIM
B_WIDTH = B_HEADS * B_V_DIM
IN_SIZES = (A_WIDTH, A_KV_HEADS * HEAD_DIM, A_KV_HEADS * HEAD_DIM,
            B_HEADS * 2 * HEAD_DIM, B_HEADS * 2 * HEAD_DIM, B_WIDTH)
RW_HEADS = D_MODEL // HEAD_DIM
DECAY_LORA = 64
ICLR_LORA = 64
GATE_LORA = 160
LNX_EPS = 64e-5
N_EXPERTS = 256
TOP_K = 8
N_GROUPS = 8
TOPK_GROUPS = 4
EXPERT_FF = 256
SHARED_FF = 256
ROUTED_SCALE = 2.5
MOE_BLOCK = 128
LN_EPS = 1e-5
SUBLN_EPS = 1e-5
N_ATT = (DEPTH + 1) // 2
N_RWKV = DEPTH // 2
DEEPNORM_ALPHA = (2 * DEPTH) ** 0.25
DEEPNORM_BETA = (8 * DEPTH) ** -0.25
NEG_INF = -1e30

kernel_name = 'hybrid_swa_diffattn_rwkv7_moe_dit'


def layer_norm(x, g, b):
    xf = x.astype(jnp.float32)
    mu = jnp.mean(xf, -1, keepdims=True)
    var = jnp.mean(jnp.square(xf - mu), -1, keepdims=True)
    return ((xf - mu) * lax.rsqrt(var + LN_EPS)).astype(x.dtype) * g + b


def modulate(x, shift, scale):
    return x * (1.0 + scale) + shift


def axial_rope(n):
    rows = n // GRID_W
    row = jnp.repeat(jnp.arange(rows), GRID_W).astype(jnp.float32)
    col = jnp.tile(jnp.arange(GRID_W), rows).astype(jnp.float32)
    inv = ROPE_THETA ** (-jnp.arange(0, ROPE_AXIS_DIM, 2, dtype=jnp.float32) / ROPE_AXIS_DIM)
    ar = row[:, None] * inv
    ac = col[:, None] * inv
    ang = jnp.concatenate([ar, ar, ac, ac], -1)
    return jnp.cos(ang), jnp.sin(ang)


def rot_half(x):
    x1, x2 = jnp.split(x, 2, axis=-1)
    return jnp.concatenate([-x2, x1], -1)


def apply_axial_rope(x, cos, sin):
    shape = (x.shape[1],) + (1,) * (x.ndim - 3) + (HEAD_DIM,)
    cos = cos.reshape(shape).astype(x.dtype)
    sin = sin.reshape(shape).astype(x.dtype)
    rot = jnp.concatenate([rot_half(x[..., :ROPE_AXIS_DIM]), rot_half(x[..., ROPE_AXIS_DIM:])], -1)
    return x * cos + rot * sin


def window_sink_attention(q, k, v, k_ctx, v_ctx, sink):
    Bn, S = q.shape[0], q.shape[1]
    nb = S // Q_BLOCK
    L = k_ctx.shape[1]
    scale = HEAD_DIM ** -0.5
    qb = q.reshape(Bn, nb, Q_BLOCK, A_KV_HEADS, A_GROUP, HEAD_DIM)

    def band(t):
        tp = jnp.pad(t, ((0, 0), (Q_BLOCK, Q_BLOCK), (0, 0), (0, 0)))
        tp = tp.reshape(Bn, nb + 2, Q_BLOCK, A_KV_HEADS, HEAD_DIM)
        return jnp.concatenate([tp[:, :-2], tp[:, 1:-1], tp[:, 2:]], axis=2)

    kw, vw = band(k), band(v)
    s_win = jnp.einsum('bnqhgd,bnkhd->bnhgqk', qb, kw).astype(jnp.float32) * scale
    s_ctx = jnp.einsum('bnqhgd,bchd->bnhgqc', qb, k_ctx).astype(jnp.float32) * scale
    qi = jnp.arange(nb)[:, None, None] * Q_BLOCK + jnp.arange(Q_BLOCK)[None, :, None]
    kj = (jnp.arange(nb)[:, None, None] - 1) * Q_BLOCK + jnp.arange(3 * Q_BLOCK)[None, None, :]
    valid = (jnp.abs(qi - kj) <= WINDOW) & (kj >= 0) & (kj < S)
    s_win = jnp.where(valid[None, :, None, None], s_win, NEG_INF)
    sink_col = jnp.broadcast_to(sink.astype(jnp.float32)[None, None, :, :, None, None], s_ctx.shape[:-1] + (1,))
    p = jax.nn.softmax(jnp.concatenate([sink_col, s_ctx, s_win], -1), axis=-1).astype(v.dtype)
    o = (jnp.einsum('bnhgqc,bchd->bnqhgd', p[..., 1:1 + L], v_ctx)
         + jnp.einsum('bnhgqk,bnkhd->bnqhgd', p[..., 1 + L:], vw))
    return o.reshape(Bn, S, A_WIDTH)


def ctx_sink_attention(q, k, v, sink):
    Bn, L = q.shape[0], q.shape[1]
    s = jnp.einsum('bqhgd,bkhd->bhgqk', q, k).astype(jnp.float32) * HEAD_DIM ** -0.5
    sink_col = jnp.broadcast_to(sink.astype(jnp.float32)[None, :, :, None, None], s.shape[:-1] + (1,))
    p = jax.nn.softmax(jnp.concatenate([sink_col, s], -1), axis=-1)[..., 1:].astype(v.dtype)
    return jnp.einsum('bhgqk,bkhd->bqhgd', p, v).reshape(Bn, L, A_WIDTH)


def diff_lambda(lam_vecs, lam_init):
    lv = lam_vecs.astype(jnp.float32)
    return jnp.exp(jnp.sum(lv[0] * lv[1])) - jnp.exp(jnp.sum(lv[2] * lv[3])) + lam_init


def diff_attend(q, k, v, lam):
    s = jnp.einsum('bqhmd,bkhmd->bhmqk', q, k).astype(jnp.float32) * HEAD_DIM ** -0.5
    p = jax.nn.softmax(s, axis=-1)
    a = (p[:, :, 0] - lam * p[:, :, 1]).astype(v.dtype)
    return jnp.einsum('bhqk,bkhe->bqhe', a, v)


def diff_attention_blocks(q, k, v, lam):
    Bn, S = q.shape[0], q.shape[1]
    nb = S // Q_BLOCK
    qb = jnp.moveaxis(q.reshape(Bn, nb, Q_BLOCK, B_HEADS, 2, HEAD_DIM), 1, 0)
    o = lax.map(lambda qq: diff_attend(qq, k, v, lam), qb)
    return jnp.moveaxis(o, 0, 1).reshape(Bn, S, B_HEADS, B_V_DIM)


def diff_head_norm(o, g, lam_init):
    of = o.astype(jnp.float32)
    of = of * lax.rsqrt(jnp.mean(of * of, -1, keepdims=True) + SUBLN_EPS)
    return of.astype(o.dtype) * g * (1.0 - lam_init)


def split_cols(h, sizes):
    offs = np.cumsum(sizes)[:-1]
    return jnp.split(h, [int(o) for o in offs], axis=-1)


def attn_mixer(u_ctx, u_lat, w_in, w_out, sink, lam_vecs, subln_g, lam_init, cos, sin, emit_ctx):
    Bn, L = u_ctx.shape[0], u_ctx.shape[1]
    S = u_lat.shape[1]
    N = L + S
    h = jnp.concatenate([u_ctx, u_lat], 1) @ w_in
    qa, ka, va, qb, kb, vb = split_cols(h, IN_SIZES)
    qa = qa.reshape(Bn, N, A_KV_HEADS, A_GROUP, HEAD_DIM)
    ka = ka.reshape(Bn, N, A_KV_HEADS, HEAD_DIM)
    va = va.reshape(Bn, N, A_KV_HEADS, HEAD_DIM)
    qb = qb.reshape(Bn, N, B_HEADS, 2, HEAD_DIM)
    kb = kb.reshape(Bn, N, B_HEADS, 2, HEAD_DIM)
    vb = vb.reshape(Bn, N, B_HEADS, B_V_DIM)
    qa_l = apply_axial_rope(qa[:, L:], cos, sin)
    ka_l = apply_axial_rope(ka[:, L:], cos, sin)
    qb_l = apply_axial_rope(qb[:, L:], cos, sin)
    kb_l = apply_axial_rope(kb[:, L:], cos, sin)
    ka_c, va_c, kb_c = ka[:, :L], va[:, :L], kb[:, :L]
    lam = diff_lambda(lam_vecs, lam_init)
    oa_l = window_sink_attention(qa_l, ka_l, va[:, L:], ka_c, va_c, sink)
    kb_all = jnp.concatenate([kb_c, kb_l], 1)
    ob_l = diff_head_norm(diff_attention_blocks(qb_l, kb_all, vb, lam), subln_g, lam_init)
    o_lat = jnp.concatenate([oa_l, ob_l.reshape(Bn, S, B_WIDTH)], -1) @ w_out
    if not emit_ctx:
        return None, o_lat
    oa_c = ctx_sink_attention(qa[:, :L], ka_c, va_c, sink)
    ob_c = diff_head_norm(diff_attend(qb[:, :L], kb_c, vb[:, :L], lam), subln_g, lam_init)
    o_ctx = jnp.concatenate([oa_c, ob_c.reshape(Bn, L, B_WIDTH)], -1) @ w_out
    return o_ctx, o_lat


def centred_shift_delta(x):
    xp = jnp.pad(x, ((0, 0), (1, 1), (0, 0)))
    return 0.5 * (xp[:, :-2] + xp[:, 2:]) - x


def wkv_scan(state0, seq, reverse, emit):
    def step(st, inp):
        r, w, k, v, a, b = inp
        sa = jnp.einsum('bhvk,bhk->bhv', st, a)
        st = st * w[:, :, None, :] + sa[..., None] * b[:, :, None, :] + v[..., None] * k[:, :, None, :]
        y = jnp.einsum('bhvk,bhk->bhv', st, r) if emit else None
        return st, y
    return lax.scan(step, state0, seq, reverse=reverse)


def rwkv7_bidir_mixer(u_ctx, u_lat, mu, w_rkv, w_out, dec0, dec1, dec2, icl0, icl1, icl2,
                      gate1, gate2, k_k, k_a, r_k, lnx, emit_ctx):
    Bn, L, D = u_ctx.shape
    u = jnp.concatenate([u_ctx, u_lat], 1)
    Nt = u.shape[1]
    dx = jnp.concatenate([centred_shift_delta(u_ctx), centred_shift_delta(u_lat)], 1)
    xr, xw, xk, xv, xa, xg = [u + dx * mu[m] for m in range(6)]
    r, k, v = xr @ w_rkv[0], xk @ w_rkv[1], xv @ w_rkv[2]
    g = jax.nn.sigmoid(xg @ gate1) @ gate2

    def heads(t):
        return t.astype(jnp.float32).reshape(Bn, Nt, RW_HEADS, HEAD_DIM)

    r_h, k_h, v_h = heads(r), heads(k), heads(v)
    kk = heads(k * k_k)
    kk = kk * lax.rsqrt(jnp.maximum(jnp.sum(kk * kk, -1, keepdims=True), 1e-24))
    k_a_h = k_a.astype(jnp.float32).reshape(RW_HEADS, HEAD_DIM)
    state0 = jnp.zeros((Bn, RW_HEADS, HEAD_DIM, HEAD_DIM), jnp.float32)
    y_lat_dirs, y_ctx_dirs, k_dirs = [], [], []
    for d in range(2):
        logw = -jax.nn.softplus(-(dec0[d] + jnp.tanh(xw @ dec1[d]) @ dec2[d])) - 0.5
        decay = jnp.exp(-jnp.exp(heads(logw)))
        a = heads(jax.nn.sigmoid(icl0[d] + (xa @ icl1[d]) @ icl2[d]))
        k_d = k_h * (1.0 + (a - 1.0) * k_a_h)
        k_dirs.append(k_d)
        seq = tuple(jnp.swapaxes(t, 0, 1) for t in (r_h, decay, k_d, v_h, -kk, kk * a))
        rev = d == 1
        s_ctx, y_c = wkv_scan(state0, tuple(t[:L] for t in seq), rev, emit_ctx)
        _, y_l = wkv_scan(s_ctx, tuple(t[L:] for t in seq), rev, True)
        y_lat_dirs.append(y_l)
        y_ctx_dirs.append(y_c)
    y_lat = y_lat_dirs[0] + y_lat_dirs[1]
    if emit_ctx:
        y = jnp.concatenate([y_ctx_dirs[0] + y_ctx_dirs[1], y_lat], 0)
        lo = 0
    else:
        y = y_lat
        lo = L
    y = jnp.swapaxes(y, 0, 1)
    ym = jnp.mean(y, -1, keepdims=True)
    yv = jnp.mean(jnp.square(y - ym), -1, keepdims=True)
    n_out = y.shape[1]
    yn = ((y - ym) * lax.rsqrt(yv + LNX_EPS)).reshape(Bn, n_out, D) * lnx[0] + lnx[1]
    k_sum = k_dirs[0] + k_dirs[1]
    bonus = jnp.sum(r_h[:, lo:] * k_sum[:, lo:] * r_k, -1, keepdims=True) * v_h[:, lo:]
    o = ((yn + bonus.reshape(Bn, n_out, D)).astype(u.dtype) * g[:, lo:]) @ w_out
    if emit_ctx:
        return o[:, :L], o[:, L:]
    return None, o


def moe_ffn(u, router, bias, w_in, w_out, ws_in, ws_out):
    Bn, N, D = u.shape
    T = Bn * N
    xt = u.reshape(T, D)
    scores = jax.nn.sigmoid((xt @ router).astype(jnp.float32))
    sel = scores + bias.astype(jnp.float32)
    per_group = N_EXPERTS // N_GROUPS
    grp_score = lax.top_k(sel.reshape(T, N_GROUPS, per_group), 2)[0].sum(-1)
    _, gidx = lax.top_k(grp_score, TOPK_GROUPS)
    gmask = jnp.any(gidx[:, :, None] == jnp.arange(N_GROUPS)[None, None, :], axis=1)
    sel = jnp.where(jnp.repeat(gmask, per_group, axis=1), sel, NEG_INF)
    _, eidx = lax.top_k(sel, TOP_K)
    gw = jnp.take_along_axis(scores, eidx, axis=1)
    gw = gw / jnp.sum(gw, -1, keepdims=True) * ROUTED_SCALE
    TK = T * TOP_K
    flat_e = eidx.reshape(-1)
    order = jnp.argsort(flat_e)
    sorted_e = flat_e[order]
    counts = jnp.bincount(flat_e, length=N_EXPERTS)
    padded = (counts + MOE_BLOCK - 1) // MOE_BLOCK * MOE_BLOCK
    start = jnp.cumsum(counts) - counts
    pstart = jnp.cumsum(padded) - padded
    dest = pstart[sorted_e] + jnp.arange(TK) - start[sorted_e]
    n_blocks = -(-(TK + N_EXPERTS * (MOE_BLOCK - 1)) // MOE_BLOCK)
    P = n_blocks * MOE_BLOCK
    row_tok = jnp.full((P,), T, jnp.int32).at[dest].set((order // TOP_K).astype(jnp.int32))
    row_w = jnp.zeros((P,), jnp.float32).at[dest].set(gw.reshape(-1)[order])
    block_exp = jnp.minimum(
        jnp.searchsorted(jnp.cumsum(padded), jnp.arange(n_blocks) * MOE_BLOCK, side='right'),
        N_EXPERTS - 1)
    x_pad = jnp.concatenate([xt, jnp.zeros((1, D), xt.dtype)], 0)

    def block(acc, blk):
        rows, e, rw = blk
        xb = x_pad[rows]
        gate, up = jnp.split(xb @ w_in[e], 2, axis=-1)
        out = (jax.nn.silu(gate) * up) @ w_out[e]
        return acc.at[rows].add(out.astype(jnp.float32) * rw[:, None]), None

    acc, _ = lax.scan(block, jnp.zeros((T + 1, D), jnp.float32),
                      (row_tok.reshape(n_blocks, MOE_BLOCK), block_exp, row_w.reshape(n_blocks, MOE_BLOCK)))
    sg, su = jnp.split(xt @ ws_in, 2, axis=-1)
    shared = (jax.nn.silu(sg) * su) @ ws_out
    return (acc[:T].astype(u.dtype) + shared).reshape(Bn, N, D)


def setup_inputs(seed: int = 0) -> dict:
    key = jax.random.key(seed)
    ks = iter(jax.random.split(key, 64))
    f32 = jnp.float32
    D = D_MODEL
    sd = D ** -0.5
    beta = DEEPNORM_BETA

    def nrm(shape, scale):
        return jax.random.normal(next(ks), shape, f32) * scale

    x = nrm((BATCH, SEQ, D), 1.0)
    c = nrm((BATCH, D), 1.0)
    ctx = nrm((BATCH, CTX_LEN, D), 1.0)
    c_ctx = nrm((D,), 1.0)
    ada_w = nrm((DEPTH, D, 6 * D), 0.5 * sd)
    ada_b = nrm((DEPTH, 6 * D), 0.02)
    post_ln_g = 1.0 + nrm((DEPTH, 2, D), 0.02)
    post_ln_b = nrm((DEPTH, 2, D), 0.02)
    att_w_in = jnp.concatenate([
        nrm((N_ATT, D, IN_SIZES[0]), sd),
        nrm((N_ATT, D, IN_SIZES[1]), sd),
        nrm((N_ATT, D, IN_SIZES[2]), sd * beta),
        nrm((N_ATT, D, IN_SIZES[3]), sd),
        nrm((N_ATT, D, IN_SIZES[4]), sd),
        nrm((N_ATT, D, IN_SIZES[5]), sd * beta)], axis=-1)
    att_w_out = nrm((N_ATT, A_WIDTH + B_WIDTH, D), (A_WIDTH + B_WIDTH) ** -0.5 * beta)
    att_sink = nrm((N_ATT, A_KV_HEADS, A_GROUP), 0.5)
    diff_lambda_vecs = nrm((N_ATT, 4, HEAD_DIM), 0.1)
    diff_subln_g = 1.0 + nrm((N_ATT, B_V_DIM), 0.02)
    rk_mu = jax.random.uniform(next(ks), (N_RWKV, 6, D), f32, 0.2, 0.8)
    rk_w_rkv = nrm((N_RWKV, 3, D, D), sd) * jnp.array([1.0, 1.0, beta], f32)[:, None, None]
    rk_w_out = nrm((N_RWKV, D, D), sd * beta)
    ramp = -6.5 + 5.0 * (jnp.arange(D, dtype=f32) / (D - 1)) ** 0.85
    rk_decay0 = ramp + nrm((N_RWKV, 2, D), 0.1)
    rk_decay1 = nrm((N_RWKV, 2, D, DECAY_LORA), sd)
    rk_decay2 = nrm((N_RWKV, 2, DECAY_LORA, D), 0.1 * DECAY_LORA ** -0.5)
    rk_iclr0 = nrm((N_RWKV, 2, D), 0.1)
    rk_iclr1 = nrm((N_RWKV, 2, D, ICLR_LORA), sd)
    rk_iclr2 = nrm((N_RWKV, 2, ICLR_LORA, D), 0.5 * ICLR_LORA ** -0.5)
    rk_gate1 = nrm((N_RWKV, D, GATE_LORA), sd)
    rk_gate2 = nrm((N_RWKV, GATE_LORA, D), GATE_LORA ** -0.5)
    rk_k_k = 0.85 + nrm((N_RWKV, D), 0.02)
    rk_k_a = 1.0 + nrm((N_RWKV, D), 0.02)
    rk_r_k = -0.04 + nrm((N_RWKV, RW_HEADS, HEAD_DIM), 0.02)
    rk_lnx = jnp.stack([1.0 + nrm((N_RWKV, D), 0.02), nrm((N_RWKV, D), 0.02)], axis=1)
    moe_router = nrm((DEPTH, D, N_EXPERTS), sd)
    moe_bias = nrm((DEPTH, N_EXPERTS), 0.01)
    moe_w_in = nrm((DEPTH, N_EXPERTS, D, 2 * EXPERT_FF), sd * beta)
    moe_w_out = nrm((DEPTH, N_EXPERTS, EXPERT_FF, D), EXPERT_FF ** -0.5 * beta)
    moe_ws_in = nrm((DEPTH, D, 2 * SHARED_FF), sd * beta)
    moe_ws_out = nrm((DEPTH, SHARED_FF, D), SHARED_FF ** -0.5 * beta)
    return {'x': x, 'c': c, 'ctx': ctx, 'c_ctx': c_ctx,
            'ada_w': ada_w, 'ada_b': ada_b, 'post_ln_g': post_ln_g, 'post_ln_b': post_ln_b,
            'att_w_in': att_w_in, 'att_w_out': att_w_out, 'att_sink': att_sink,
            'diff_lambda_vecs': diff_lambda_vecs, 'diff_subln_g': diff_subln_g,
            'rk_mu': rk_mu, 'rk_w_rkv': rk_w_rkv, 'rk_w_out': rk_w_out,
            'rk_decay0': rk_decay0, 'rk_decay1': rk_decay1, 'rk_decay2': rk_decay2,
            'rk_iclr0': rk_iclr0, 'rk_iclr1': rk_iclr1, 'rk_iclr2': rk_iclr2,
            'rk_gate1': rk_gate1, 'rk_gate2': rk_gate2, 'rk_k_k': rk_k_k, 'rk_k_a': rk_k_a,
            'rk_r_k': rk_r_k, 'rk_lnx': rk_lnx,
            'moe_router': moe_router, 'moe_bias': moe_bias, 'moe_w_in': moe_w_in,
            'moe_w_out': moe_w_out, 'moe_ws_in': moe_ws_in, 'moe_ws_out': moe_ws_out}


def reference(x, c, ctx, c_ctx, ada_w, ada_b, post_ln_g, post_ln_b,
              att_w_in, att_w_out, att_sink, diff_lambda_vecs, diff_subln_g,
              rk_mu, rk_w_rkv, rk_w_out, rk_decay0, rk_decay1, rk_decay2,
              rk_iclr0, rk_iclr1, rk_iclr2, rk_gate1, rk_gate2, rk_k_k, rk_k_a, rk_r_k, rk_lnx,
              moe_router, moe_bias, moe_w_in, moe_w_out, moe_ws_in, moe_ws_out):
    S = x.shape[1]
    L = ctx.shape[1]
    cos, sin = axial_rope(S)
    c_act = jax.nn.silu(c)
    cc_act = jax.nn.silu(c_ctx)
    h_lat, h_ctx = x, ctx
    for i in range(DEPTH):
        last = i == DEPTH - 1
        j = i // 2
        m_lat = jnp.split((c_act @ ada_w[i] + ada_b[i])[:, None, :], 6, axis=-1)
        m_ctx = jnp.split(cc_act @ ada_w[i] + ada_b[i], 6, axis=-1)
        u_lat = modulate(h_lat, m_lat[0], m_lat[1])
        u_ctx = modulate(h_ctx, m_ctx[0], m_ctx[1])
        if i % 2 == 0:
            lam_init = 0.8 - 0.6 * math.exp(-0.3 * i)
            o_ctx, o_lat = attn_mixer(u_ctx, u_lat, att_w_in[j], att_w_out[j], att_sink[j],
                                      diff_lambda_vecs[j], diff_subln_g[j], lam_init, cos, sin,
                                      not last)
        else:
            o_ctx, o_lat = rwkv7_bidir_mixer(u_ctx, u_lat, rk_mu[j], rk_w_rkv[j], rk_w_out[j],
                                             rk_decay0[j], rk_decay1[j], rk_decay2[j],
                                             rk_iclr0[j], rk_iclr1[j], rk_iclr2[j],
                                             rk_gate1[j], rk_gate2[j], rk_k_k[j], rk_k_a[j],
                                             rk_r_k[j], rk_lnx[j], not last)
        h_lat = layer_norm(DEEPNORM_ALPHA * h_lat + m_lat[2] * o_lat, post_ln_g[i, 0], post_ln_b[i, 0])
        u_lat = modulate(h_lat, m_lat[3], m_lat[4])
        if last:
            f_lat = moe_ffn(u_lat, moe_router[i], moe_bias[i], moe_w_in[i], moe_w_out[i],
                            moe_ws_in[i], moe_ws_out[i])
        else:
            h_ctx = layer_norm(DEEPNORM_ALPHA * h_ctx + m_ctx[2] * o_ctx, post_ln_g[i, 0], post_ln_b[i, 0])
            u_ctx = modulate(h_ctx, m_ctx[3], m_ctx[4])
            f = moe_ffn(jnp.concatenate([u_ctx, u_lat], 1), moe_router[i], moe_bias[i],
                        moe_w_in[i], moe_w_out[i], moe_ws_in[i], moe_ws_out[i])
            f_ctx, f_lat = f[:, :L], f[:, L:]
            h_ctx = layer_norm(DEEPNORM_ALPHA * h_ctx + m_ctx[5] * f_ctx, post_ln_g[i, 1], post_ln_b[i, 1])
        h_lat = layer_norm(DEEPNORM_ALPHA * h_lat + m_lat[5] * f_lat, post_ln_g[i, 1], post_ln_b[i, 1])
    return h_lat
```

```python
import math
import numpy as np
import concourse.bass as bass
import concourse.mybir as mybir
from concourse.bass_utils import run_bass_kernel_spmd
from contextlib import ExitStack, contextmanager

F32 = mybir.dt.float32
I32 = mybir.dt.int32
ALU = mybir.AluOpType
AF = mybir.ActivationFunctionType
AX = mybir.AxisListType

D = 1024
SEQ = 2048
LCTX = 256
NTOK = SEQ + LCTX
NT = NTOK // 128
NE = 256
CAP = 512
HALF_ROWS = (NE // 2) * CAP
ALPHA = 4 ** 0.25
LN_EPS = 1e-5
C0 = -math.exp(-0.5)


class T:
    def __init__(self, fw, t, name):
        self.fw = fw; self.t = t; self.name = name
        self.w = {}; self.r = {}
        self.dsem = None; self.dcnt = 0

    def __getitem__(self, k):
        return self.t[k]


class FW:
    ENG = ('pe', 'dve', 'act', 'pool', 'sp')

    def __init__(self, nc):
        self.nc = nc
        self.eng = {'pe': nc.tensor, 'dve': nc.vector, 'act': nc.scalar, 'pool': nc.gpsimd, 'sp': nc.sync}
        self.root = ExitStack()
        self.stacks = [self.root]
        self.scopeT = [[]]
        self.free_dsems = []
        self.sem = {}; self.cnt = {}; self.seen = {}
        self.nsem = 0
        for e in self.ENG:
            self._new_prog(e)
        self.bar_sem = self._alloc_sem("BAR"); self.bar_cnt = 0
        self.ninst = 0

    def _alloc_sem(self, name):
        self.nsem += 1
        return self.root.enter_context(self.nc.semaphore(name + "_%d" % self.nsem))

    def _new_prog(self, e):
        self.sem[e] = self._alloc_sem("S_" + e); self.cnt[e] = 0
        self.seen[e] = {}

    def _reg(self, b):
        self.scopeT[-1].append(b)
        return b

    def sb(self, name, shape, dtype=F32):
        self.uid = getattr(self, "uid", 0) + 1
        name = "%s_u%d" % (name, self.uid)
        t = self.stacks[-1].enter_context(self.nc.sbuf_tensor(name, list(shape), dtype))
        return self._reg(T(self, t, name))

    def ps(self, name, shape, dtype=F32):
        t = self.stacks[-1].enter_context(self.nc.psum_tensor(name, list(shape), dtype))
        return self._reg(T(self, t, name))

    def dram(self, name, shape, dtype=F32, kind="Internal"):
        t = self.nc.dram_tensor(name, list(shape), dtype, kind=kind)
        return self._reg(T(self, t.ap(), name))

    def _dsem(self, b):
        if b.dsem is None:
            if self.free_dsems:
                b.dsem, b.dcnt = self.free_dsems.pop()
            else:
                b.dsem = self._alloc_sem("D"); b.dcnt = 0
        return b.dsem

    def _waits(self, e, reads, writes):
        own = self.sem[e]
        waits = {}

        def merge(d, skip_own):
            for s, v in d.items():
                if skip_own and s is own:
                    continue
                k = id(s)
                if k not in waits or waits[k][1] < v:
                    waits[k] = (s, v)
        for b in reads:
            merge(b.w, False)
        for b in writes:
            merge(b.w, True)
            merge(b.r, True)
        seen = self.seen[e]
        for k, (s, v) in waits.items():
            if seen.get(k, 0) >= v:
                continue
            self.eng[e].wait_ge(s, v)
            seen[k] = v

    def op(self, e, fn, reads=(), writes=()):
        self._waits(e, reads, writes)
        inst = fn(self.eng[e])
        self.cnt[e] += 1; self.ninst += 1
        c = self.cnt[e]; s = self.sem[e]
        inst.then_inc(s, 1)
        for b in writes:
            b.w[s] = c; b.r = {}
        for b in reads:
            if b not in writes:
                b.r[s] = c
        return inst

    def dma(self, q, out_b, out_ap, in_b, in_ap, fn=None, extra_reads=()):
        self._waits(q, [in_b] + list(extra_reads), [out_b])
        s = self._dsem(out_b)
        if fn is None:
            inst = self.eng[q].dma_start(out=out_ap, in_=in_ap)
        else:
            inst = fn(self.eng[q])
        self.ninst += 1
        out_b.dcnt += 16
        inst.then_inc(s, 16)
        out_b.w[s] = out_b.dcnt
        out_b.r = {}
        in_b.r[s] = out_b.dcnt
        for b in extra_reads:
            b.r[s] = out_b.dcnt
        return inst

    def all_T(self):
        for lst in self.scopeT:
            for b in lst:
                yield b

    def barrier(self):
        sp = self.eng['sp']
        seen = self.seen['sp']
        for e in self.ENG:
            if e != 'sp' and self.cnt[e] > 0:
                sp.wait_ge(self.sem[e], self.cnt[e])
        for b in self.all_T():
            if b.dsem is not None and b.dcnt > 0 and seen.get(id(b.dsem), 0) < b.dcnt:
                sp.wait_ge(b.dsem, b.dcnt); seen[id(b.dsem)] = b.dcnt
        self.bar_cnt += 1
        sp.sem_inc(self.bar_sem, 1)
        for e in self.ENG:
            if e != 'sp':
                self.eng[e].wait_ge(self.bar_sem, self.bar_cnt)
        for b in self.all_T():
            b.w = {}; b.r = {}
        for e in self.ENG:
            if e != 'sp' and self.cnt[e] > 12000:
                self._new_prog(e)

    @contextmanager
    def scope(self):
        st = ExitStack(); self.stacks.append(st); self.scopeT.append([])
        yield
        self.barrier()
        for b in self.scopeT.pop():
            if b.dsem is not None:
                self.free_dsems.append((b.dsem, b.dcnt)); b.dsem = None
        self.stacks.pop().close()

    def finish(self):
        self.barrier()
        self.root.close()


def host_consts():
    c = {}
    c["ident"] = np.eye(128, dtype=np.float32)
    s = np.arange(128)[:, None]; t = np.arange(128)[None, :]
    c["tri_s"] = (s < t).astype(np.float32)
    c["ones"] = np.ones((128, 128), np.float32)
    c["base_e"] = np.broadcast_to((np.arange(NE, dtype=np.float32) * CAP + 1.0)[None, :], (128, NE)).copy()
    rows = SEQ // 64
    row = np.repeat(np.arange(rows), 64).astype(np.float32)
    col = np.tile(np.arange(64), rows).astype(np.float32)
    inv = (np.float32(10000.0) ** (-np.arange(0, 32, 2, dtype=np.float32) / np.float32(32))).astype(np.float32)
    ar = row[:, None] * inv; ac = col[:, None] * inv
    ang = np.concatenate([ar, ar, ac, ac], -1).astype(np.float32)
    cosT = np.cos(ang).T.astype(np.float32); sinT = np.sin(ang).T.astype(np.float32)
    c["cos2"] = np.concatenate([cosT, cosT], 0).copy(); c["sin2"] = np.concatenate([sinT, sinT], 0).copy()
    rm = np.zeros((64, 64), np.float32)
    for d in range(64):
        if (d % 32) < 16:
            rm[d + 16, d] = -1.0
        else:
            rm[d - 16, d] = 1.0
    rm2 = np.zeros((128, 128), np.float32); rm2[:64, :64] = rm; rm2[64:, 64:] = rm
    c["rm2"] = rm2
    kk = np.arange(128)[:, None]; qq = np.arange(128)[None, :]
    c["mask_prev"] = (qq <= kk).astype(np.float32)
    c["mask_next"] = (kk <= qq).astype(np.float32)
    c["triI_f"] = (C0 * (s <= t)).astype(np.float32); c["triE_f"] = (C0 * (s < t)).astype(np.float32)
    c["triI_b"] = (C0 * (s >= t)).astype(np.float32); c["triE_b"] = (C0 * (s > t)).astype(np.float32)
    c["msi_f"] = np.stack([(s < t), (s <= t)], 1).astype(np.float32)
    c["msi_b"] = np.stack([(s > t), (s >= t)], 1).astype(np.float32)
    c["mn_f"] = (t < s).astype(np.float32)
    c["mn_b"] = (t > s).astype(np.float32)
    c["blk64"] = ((s // 64) == (t // 64)).astype(np.float32)
    sel2 = np.zeros((128, 2), np.float32); sel2[:64, 0] = 1.0; sel2[64:, 1] = 1.0
    c["sel2"] = sel2
    return c


CONST_SHAPES = {k: v.shape for k, v in host_consts().items()}

W_SHAPES = {
    'ada_w': (2, 1024, 6144), 'ada_b': (2, 6144), 'post_ln_g': (2, 2, 1024), 'post_ln_b': (2, 2, 1024),
    'att_w_in': (1, 1024, 2304), 'att_w_out': (1, 1024, 1024), 'att_sink': (1, 2, 4),
    'diff_lambda_vecs': (1, 4, 64), 'diff_subln_g': (1, 128),
    'rk_mu': (1, 6, 1024), 'rk_w_rkv': (1, 3, 1024, 1024), 'rk_w_out': (1, 1024, 1024),
    'rk_decay0': (1, 2, 1024), 'rk_decay1': (1, 2, 1024, 64), 'rk_decay2': (1, 2, 64, 1024),
    'rk_iclr0': (1, 2, 1024), 'rk_iclr1': (1, 2, 1024, 64), 'rk_iclr2': (1, 2, 64, 1024),
    'rk_gate1': (1, 1024, 160), 'rk_gate2': (1, 160, 1024), 'rk_k_k': (1, 1024), 'rk_k_a': (1, 1024),
    'rk_r_k': (1, 16, 64), 'rk_lnx': (1, 2, 1024),
    'moe_router': (2, 1024, 256), 'moe_bias': (2, 256), 'moe_w_in': (2, 256, 1024, 512),
    'moe_w_out': (2, 256, 256, 1024), 'moe_ws_in': (2, 1024, 512), 'moe_ws_out': (2, 256, 1024),
}


class K:
    pass


def build(stop_after=None, debug=False, ne_dbg=None):
    nc = bass.Bass("TRN2", target_bir_lowering=False)
    fw = FW(nc)
    k = K(); k.fw = fw; k.nc = nc; k.debug = debug; k.stop_after = stop_after; k.ne_dbg = ne_dbg
    import os as _os
    k.scan_stop = int(_os.environ['SCAN_STOP']) if 'SCAN_STOP' in _os.environ else None
    k.breg = nc.gpsimd.to_reg(HALF_ROWS - 1)

    def din(name, shape, dt=F32):
        return fw._reg(T(fw, nc.dram_tensor(name, list(shape), dt, kind="ExternalInput").ap(), name))
    k.sinkb = din("sinkb", [128, 8])
    k.mucol = din("mucol", [128, 6, 8]); k.icl0col = din("icl0col", [128, 2, 8])
    k.kkcol = din("kkcol", [128, 8]); k.kacol = din("kacol", [128, 8]); k.rkcol = din("rkcol", [128, 8])
    k.x = din("x", [SEQ, D]); k.ctx = din("ctx", [LCTX, D]); k.cc = din("cc", [128, 8, 2])
    class Lazy(dict):
        def __init__(self, shapes, prefix):
            super().__init__(); self.shapes = shapes; self.prefix = prefix
        def __missing__(self, n):
            v = din(self.prefix + n, self.shapes[n]); self[n] = v; return v
    wsh = dict(W_SHAPES)
    if ne_dbg:
        wsh['moe_w_in'] = (1, ne_dbg, 1024, 512); wsh['moe_w_out'] = (1, ne_dbg, 256, 1024)
    k.W = Lazy(wsh, ""); k.C = Lazy(CONST_SHAPES, "c_")
    nc._used = lambda: ["x", "ctx", "cc", "sinkb", "mucol", "icl0col", "kkcol", "kacol", "rkcol"] + (["H2in"] if k.h2in else []) + list(k.W.keys()) + ["c_" + n for n in k.C.keys()]
    k.out = fw._reg(T(fw, nc.dram_tensor("out", [SEQ, D], F32, kind="ExternalOutput").ap(), "out"))
    skind = "ExternalOutput" if debug else "Internal"
    k.H1 = fw.dram("H1", [NTOK, D], kind=skind)
    k.H2 = fw.dram("H2", [NTOK, D], kind=skind)
    k.AO = fw.dram("AO", [NTOK, D], kind=skind)
    k.XS = [fw.dram("XS%d" % i, [HALF_ROWS, D]) for i in range(2)]
    k.YS = [fw.dram("YS%d" % i, [HALF_ROWS, D]) for i in range(2)]
    k.SH = fw.dram("SH", [NTOK, D])
    k.PS = [fw.ps("psb%d" % i, [128, 512]) for i in range(8)]

    k.h2in = bool(stop_after and stop_after.startswith("r_"))
    if k.h2in:
        k.H2 = din("H2in", [NTOK, D])
    with fw.scope():
        k.ident = fw.sb("ident", [128, 128]); fw.dma('sp', k.ident, k.ident[:, :], k.C["ident"], k.C["ident"][:, :])
        k.ones = fw.sb("ones", [128, 128]); fw.dma('sp', k.ones, k.ones[:, :], k.C["ones"], k.C["ones"][:, :])
        k.epsln = fw.sb("epsln", [128, 1]); fw.op('pool', lambda e: e.memset(k.epsln[:, :], LN_EPS), [], [k.epsln])
        k.cact = fw.sb("cact", [128, 8, 2])
        fw.dma('sp', k.cact, k.cact[:, :, :], k.cc, k.cc[:, :, :])
        fw.op('act', lambda e: e.activation(k.cact[:, :, :], k.cact[:, :, :], AF.Silu), [k.cact], [k.cact])
        k.cbc = []
        for kk in range(2):
            t = fw.sb("cbc%d" % kk, [128, 8, 128])
            fw.op('dve', lambda e: e.tensor_copy(t[:, :, :], k.cact[:, :, kk:kk + 1].to_broadcast([128, 8, 128])), [k.cact], [t])
            k.cbc.append(t)
        if not k.h2in:
            zero_fill(k)
        print("nsem", fw.nsem, "ninst", fw.ninst, flush=True)
        if stop_after == "zero":
            fw.finish(); return nc
        if k.h2in:
            rwkv_phase(k)
            if stop_after != "r_all":
                fw.finish(); return nc
            outproj_ln_phase(k, 1, k.W['rk_w_out'], [k.H2, k.H2], k.H1, range(2, NT))
            fw.finish(); return nc
        attn_phase(k)
        if stop_after in ("a0", "a1", "b0", "b1", "b2", "w0", "w1", "w2"):
            fw.finish(); return nc
        print("nsem", fw.nsem, "ninst", fw.ninst, flush=True)
        if stop_after == "attn":
            fw.finish(); return nc
        outproj_ln_phase(k, 0, k.W['att_w_out'], [k.ctx, k.x], k.H1, range(NT))
        if stop_after == "mix0":
            fw.finish(); return nc
        moe_phase(k, 0, k.H1, k.H2, list(range(NT)), None)
        if stop_after == "moe0":
            fw.finish(); return nc
        rwkv_phase(k)
        if stop_after == "rwkv":
            fw.finish(); return nc
        outproj_ln_phase(k, 1, k.W['rk_w_out'], [k.H2, k.H2], k.H1, range(2, NT))
        if stop_after == "mix1":
            fw.finish(); return nc
        moe_phase(k, 1, k.H1, None, list(range(2, NT)), k.out)
    fw.finish()
    return nc


def src_rows(k, srcs, t):
    if srcs[0] is srcs[1]:
        return srcs[0], srcs[0][t * 128:(t + 1) * 128, :]
    if t < 2:
        return srcs[0], srcs[0][t * 128:(t + 1) * 128, :]
    return srcs[1], srcs[1][(t - 2) * 128:(t - 1) * 128, :]


def zero_fill(k):
    fw = k.fw
    with fw.scope():
        z = fw.sb("zf", [128, 4, D])
        fw.op('pool', lambda e: e.memset(z[:, :, :], 0.0), [], [z])
        for hf in range(2):
            for i in range(HALF_ROWS // 512):
                q = 'sp' if i % 2 == 0 else 'act'
                fw.dma(q, k.XS[hf], k.XS[hf][i * 512:(i + 1) * 512, :].rearrange("(a p) n -> p a n", p=128), z, z[:, :, :])
                if k.ne_dbg:
                    fw.dma(q, k.YS[hf], k.YS[hf][i * 512:(i + 1) * 512, :].rearrange("(a p) n -> p a n", p=128), z, z[:, :, :])


def ada_bc(k, layer, j, kk, dst, plus_one=False):
    fw = k.fw
    aw = k.W['ada_w']; ab = k.W['ada_b']
    with fw.scope():
        bb = fw.sb("ada_bb", [128, D])
        fw.dma('act', bb, bb[:, :], ab, ab[layer:layer + 1, j * D:(j + 1) * D].partition_broadcast(128))
        for half in range(2):
            wt = fw.sb("ada_wt%d" % half, [128, 8, 512])
            c0 = j * D + half * 512
            fw.dma('sp', wt, wt[:, :, :], aw, aw[layer, :, c0:c0 + 512].rearrange("(c p) n -> p c n", p=128))
            ps = k.PS[half]
            for c in range(8):
                fw.op('pe', lambda e: e.matmul(ps[:, :], k.cbc[kk][:, c, :], wt[:, c, :], start=(c == 0), stop=(c == 7)),
                      [k.cbc[kk], wt], [ps])
            fw.op('dve', lambda e: e.tensor_tensor(dst[:, half * 512:(half + 1) * 512], ps[:, :], bb[:, half * 512:(half + 1) * 512], ALU.add),
                  [ps, bb], [dst])
        if plus_one:
            fw.op('dve', lambda e: e.tensor_scalar(dst[:, :], dst[:, :], 1.0, None, ALU.add), [dst], [dst])


def load_bc_row(k, dst, src, src_ap, q='act'):
    k.fw.dma(q, dst, dst[:, :], src, src_ap.partition_broadcast(128))


def transpose_tile(k, src, dstT, dst_ap_fn, psa, psb, eng2='act'):
    fw = k.fw
    for half, ps in ((0, psa), (1, psb)):
        for c in range(4):
            cc = half * 4 + c
            fw.op('pe', lambda e: e.transpose(ps[:, c * 128:(c + 1) * 128], src[:, cc * 128:(cc + 1) * 128], k.ident[:, :]),
                  [src, k.ident], [ps])
        if half == 0:
            fw.op('dve', lambda e: e.tensor_copy(dst_ap_fn(0, 4), ps[:, :].rearrange("p (c n) -> p c n", c=4)), [ps], [dstT])
        else:
            fw.op(eng2, (lambda e: e.activation(dst_ap_fn(4, 8), ps[:, :].rearrange("p (c n) -> p c n", c=4), AF.Copy)) if eng2 == 'act'
                  else (lambda e: e.tensor_copy(dst_ap_fn(4, 8), ps[:, :].rearrange("p (c n) -> p c n", c=4))), [ps], [dstT])


def layer_norm_tile(k, z, g_bc, b_bc, tmp, st):
    fw = k.fw
    fw.op('dve', lambda e: e.tensor_reduce(st[:, 0:1], z[:, :], AX.X, ALU.add), [z], [st])
    fw.op('dve', lambda e: e.tensor_scalar(st[:, 1:2], st[:, 0:1], -1.0 / D, None, ALU.mult), [st], [st])
    fw.op('dve', lambda e: e.tensor_scalar(z[:, :], z[:, :], st[:, 1:2], None, ALU.add), [z, st], [z])
    fw.op('act', lambda e: e.activation(tmp[:, :], z[:, :], AF.Square, accum_out=st[:, 2:3]), [z], [tmp, st])
    fw.op('act', lambda e: e.activation(st[:, 3:4], st[:, 2:3], AF.Sqrt, bias=k.epsln[:, 0:1], scale=1.0 / D), [st, k.epsln], [st])
    fw.op('dve', lambda e: e.reciprocal(st[:, 3:4], st[:, 3:4]), [st], [st])
    fw.op('dve', lambda e: e.scalar_tensor_tensor(z[:, :], z[:, :], st[:, 3:4], g_bc[:, :], ALU.mult, ALU.mult), [z, st, g_bc], [z])
    fw.op('pool', lambda e: e.tensor_tensor(z[:, :], z[:, :], b_bc[:, :], ALU.add), [z, b_bc], [z])


def attn_phase(k):
    fw = k.fw
    W = k.W['att_w_in']
    with fw.scope():
        uT = fw.sb("uT", [128, 8, NTOK])
        with fw.scope():
            if k.stop_after == "a0":
                return
            sc = [fw.sb("a_sc%d" % i, [128, D]) for i in range(2)]
            sh = [fw.sb("a_sh%d" % i, [128, D]) for i in range(2)]
            for kk in range(2):
                ada_bc(k, 0, 0, kk, sh[kk]); ada_bc(k, 0, 1, kk, sc[kk], plus_one=True)
            hb = [fw.sb("a_h%d" % i, [128, D]) for i in range(2)]
            for t in range(NT):
                h = hb[t % 2]
                sT, sap = src_rows(k, [k.ctx, k.x], t)
                fw.dma('sp', h, h[:, :], sT, sap)
                kk = 0 if t >= 2 else 1
                fw.op('dve', lambda e: e.tensor_tensor(h[:, :], h[:, :], sc[kk][:, :], ALU.mult), [h, sc[kk]], [h])
                fw.op('pool', lambda e: e.tensor_tensor(h[:, :], h[:, :], sh[kk][:, :], ALU.add), [h, sh[kk]], [h])
                transpose_tile(k, h, uT, lambda a, b: uT[:, a:b, t * 128:(t + 1) * 128], k.PS[(t % 2) * 2], k.PS[(t % 2) * 2 + 1])
        if k.stop_after == "a1":
            for c in range(8):
                fw.dma('sp', k.AO, k.AO[c * 128:(c + 1) * 128, 0:NTOK // 4].rearrange("p (a n) -> p a n", a=1)[:, 0, :], uT, uT[:, c, 0:NTOK // 4])
            return
        cos2 = fw.sb("cos2", [128, SEQ]); sin2 = fw.sb("sin2", [128, SEQ]); rm2 = fw.sb("rm2", [128, 128])
        fw.dma('sp', cos2, cos2[:, :], k.C["cos2"], k.C["cos2"][:, :])
        fw.dma('act', sin2, sin2[:, :], k.C["sin2"], k.C["sin2"][:, :])
        fw.dma('sp', rm2, rm2[:, :], k.C["rm2"], k.C["rm2"][:, :])
        TB = [(0, 256)] + [(256 + i * 512, 512) for i in range(4)]

        def proj_fm(dst, wt, M=128):
            for bi, (t0, n) in enumerate(TB):
                ps = k.PS[4 + bi % 2]
                for c in range(8):
                    fw.op('pe', lambda e: e.matmul(ps[0:M, 0:n], wt[:, c, 0:M], uT[:, c, t0:t0 + n], start=(c == 0), stop=(c == 7)),
                          [wt, uT], [ps])
                if bi % 2 == 0:
                    fw.op('dve', lambda e: e.tensor_copy(dst[0:M, t0:t0 + n], ps[0:M, 0:n]), [ps], [dst])
                else:
                    fw.op('act', lambda e: e.activation(dst[0:M, t0:t0 + n], ps[0:M, 0:n], AF.Copy), [ps], [dst])

        def proj_tok(dst, wt, M, ncol):
            for g in range(0, NT, 3):
                ps = k.PS[6 + (g // 3) % 2]
                for i in range(3):
                    t = g + i
                    for c in range(8):
                        fw.op('pe', lambda e: e.matmul(ps[:, i * 128:i * 128 + M], uT[:, c, t * 128:(t + 1) * 128], wt[:, c, 0:M],
                                                       start=(c == 0), stop=(c == 7)), [uT, wt], [ps])
                fw.op('dve', lambda e: e.tensor_copy(dst[:, g:g + 3, 0:M], ps[:, 0:384].rearrange("p (a n) -> p a n", a=3)[:, :, 0:M]), [ps], [dst])

        def rope(dst, tmp):
            for bi in range(4):
                t0 = 256 + bi * 512
                ps = k.PS[4 + bi % 2]
                fw.op('pe', lambda e: e.matmul(ps[:, :], rm2[:, :], dst[:, t0:t0 + 512], start=True, stop=True), [rm2, dst], [ps])
                fw.op('dve', lambda e: e.tensor_tensor(tmp[:, :], ps[:, :], sin2[:, bi * 512:(bi + 1) * 512], ALU.mult), [ps, sin2], [tmp])
                fw.op('pool', lambda e: e.tensor_tensor(dst[:, t0:t0 + 512], dst[:, t0:t0 + 512], cos2[:, bi * 512:(bi + 1) * 512], ALU.mult), [dst, cos2], [dst])
                fw.op('dve', lambda e: e.tensor_tensor(dst[:, t0:t0 + 512], dst[:, t0:t0 + 512], tmp[:, :], ALU.add), [dst, tmp], [dst])

        def load_w(wt, col0, M, off=0, q='sp'):
            fw.dma(q, wt, wt[:, :, off:off + M], W, W[0, :, col0:col0 + M].rearrange("(c p) n -> p c n", p=128))

        with fw.scope():
          if k.stop_after not in ("w0", "w1", "w2"):
                lam_init = 0.8 - 0.6 * math.exp(0.0)
                lvf = fw.sb("lv", [128, 256]); lsm = fw.sb("lsm", [128, 4]); lam = fw.sb("lam", [128, 2])
                fw.dma('act', lvf, lvf[:, :], k.W['diff_lambda_vecs'], k.W['diff_lambda_vecs'][0].rearrange("(o b) c -> o (b c)", o=1).partition_broadcast(128))
                fw.op('dve', lambda e: e.tensor_tensor(lvf[:, 0:64], lvf[:, 0:64], lvf[:, 64:128], ALU.mult), [lvf], [lvf])
                fw.op('dve', lambda e: e.tensor_tensor(lvf[:, 128:192], lvf[:, 128:192], lvf[:, 192:256], ALU.mult), [lvf], [lvf])
                fw.op('dve', lambda e: e.tensor_reduce(lsm[:, :], lvf[:, :].rearrange("p (b c) -> p b c", b=4), AX.X, ALU.add), [lvf], [lsm])
                fw.op('act', lambda e: e.activation(lsm[:, :], lsm[:, :], AF.Exp), [lsm], [lsm])
                fw.op('dve', lambda e: e.tensor_tensor(lam[:, 0:1], lsm[:, 0:1], lsm[:, 2:3], ALU.subtract), [lsm], [lam])
                fw.op('dve', lambda e: e.tensor_scalar(lam[:, 1:2], lam[:, 0:1], lam_init, -1.0, ALU.add, ALU.mult), [lam], [lam])
                gsc = fw.sb("gsc", [128, 128])
                load_bc_row(k, gsc, k.W['diff_subln_g'], k.W['diff_subln_g'][0:1, :])
                fw.op('dve', lambda e: e.tensor_scalar(gsc[:, :], gsc[:, :], 1.0 - lam_init, None, ALU.mult), [gsc], [gsc])
                epsb = fw.sb("epsb", [128, 1]); fw.op('pool', lambda e: e.memset(epsb[:, :], 1e-5), [], [epsb])
                qT = fw.sb("b_qT", [128, NTOK]); kT = fw.sb("b_kT", [128, NTOK]); vt = fw.sb("b_v", [128, NT, 132])
                tmp = fw.sb("b_tmp", [128, 512])
                wts = [fw.sb("b_w%d" % i, [128, 8, 128]) for i in range(3)]
                Eb = [fw.sb("b_E%d" % i, [128, 512]) for i in range(2)]
                osb = [fw.sb("b_o%d" % i, [128, 128]) for i in range(2)]
                t0b = fw.sb("b_t0", [128, 128]); sq = fw.sb("b_sq", [128, 128]); st = fw.sb("b_st", [128, 8])
                fw.op('pool', lambda e: e.memset(vt[:, :, 128:129], 1.0), [], [vt])
                for h in range(4):
                    load_w(wts[0], 768 + h * 128, 128); load_w(wts[1], 1280 + h * 128, 128, q='act'); load_w(wts[2], 1792 + h * 128, 128)
                    proj_fm(qT, wts[0]); proj_fm(kT, wts[1]); proj_tok(vt, wts[2], 128, 132)
                    rope(qT, tmp); rope(kT, tmp)
                    if k.stop_after == "b0":
                        fw.dma('sp', k.AO, k.AO[0:128, :], qT, qT[:, 0:1024]); fw.dma('sp', k.AO, k.AO[128:256, :], kT, kT[:, 256:1280])
                        fw.dma('sp', k.AO, k.AO[256:384, 0:129 * 7].rearrange("p (a n) -> p a n", a=7), vt, vt[:, 0:7, 0:129])
                        break
                    for qb in range(NT):
                        if k.stop_after == "b1" and (h > 0 or qb > 3):
                            break
                        keys = range(NT) if qb >= 2 else range(2)
                        kgroups = [list(keys)[i:i + 4] for i in range(0, len(keys), 4)]
                        po = [k.PS[2], k.PS[3]]
                        it = 0
                        for m in range(2):
                            pr = slice(m * 64, (m + 1) * 64)
                            for gi, kg in enumerate(kgroups):
                                ps = k.PS[it % 2]; E = Eb[it % 2]; it += 1
                                n = len(kg)
                                for i, kt in enumerate(kg):
                                    fw.op('pe', lambda e: e.matmul(ps[:, i * 128:(i + 1) * 128], kT[pr, kt * 128:(kt + 1) * 128],
                                                                   qT[pr, qb * 128:(qb + 1) * 128], start=True, stop=True), [kT, qT], [ps])
                                fw.op('act', lambda e: e.activation(E[:, 0:n * 128], ps[:, 0:n * 128], AF.Exp, scale=0.125), [ps], [E])
                                for i, kt in enumerate(kg):
                                    fw.op('pe', lambda e: e.matmul(po[m][:, 0:129], E[:, i * 128:(i + 1) * 128], vt[:, kt, 0:129],
                                                                   start=(gi == 0 and i == 0), stop=(gi == len(kgroups) - 1 and i == n - 1)),
                                          [E, vt], [po[m]])
                        o = osb[qb % 2]
                        fw.op('dve', lambda e: e.reciprocal(st[:, 0:1], po[0][:, 128:129]), [po[0]], [st])
                        fw.op('dve', lambda e: e.reciprocal(st[:, 1:2], po[1][:, 128:129]), [po[1]], [st])
                        fw.op('dve', lambda e: e.tensor_tensor(st[:, 1:2], st[:, 1:2], lam[:, 1:2], ALU.mult), [st, lam], [st])
                        fw.op('dve', lambda e: e.tensor_scalar(t0b[:, :], po[0][:, 0:128], st[:, 0:1], None, ALU.mult), [po[0], st], [t0b])
                        fw.op('dve', lambda e: e.scalar_tensor_tensor(t0b[:, :], po[1][:, 0:128], st[:, 1:2], t0b[:, :], ALU.mult, ALU.add), [po[1], st, t0b], [t0b])
                        fw.op('act', lambda e: e.activation(sq[:, :], t0b[:, :], AF.Square, accum_out=st[:, 2:3]), [t0b], [sq, st])
                        fw.op('act', lambda e: e.activation(st[:, 3:4], st[:, 2:3], AF.Sqrt, bias=epsb[:, 0:1], scale=1.0 / 128), [st, epsb], [st])
                        fw.op('dve', lambda e: e.reciprocal(st[:, 3:4], st[:, 3:4]), [st], [st])
                        fw.op('dve', lambda e: e.scalar_tensor_tensor(o[:, :], t0b[:, :], st[:, 3:4], gsc[:, :], ALU.mult, ALU.mult), [t0b, st, gsc], [o])
                        fw.dma('pool', k.AO, k.AO[qb * 128:(qb + 1) * 128, 512 + h * 128:512 + (h + 1) * 128], o, o[:, :])
        if k.stop_after in ("b0", "b1", "b2"):
            return
        with fw.scope():
            snk = fw.sb("snk", [128, 8])
            fw.dma('sp', snk, snk[:, :], k.sinkb, k.sinkb[:, :])
            fw.op('act', lambda e: e.activation(snk[:, :], snk[:, :], AF.Exp), [snk], [snk])
            mp = fw.sb("mprev", [128, 128]); mn = fw.sb("mnext", [128, 128])
            fw.dma('sp', mp, mp[:, :], k.C["mask_prev"], k.C["mask_prev"][:, :])
            fw.dma('sp', mn, mn[:, :], k.C["mask_next"], k.C["mask_next"][:, :])
            qTs = [fw.sb("a_qT%d" % i, [128, NTOK]) for i in range(2)]
            kd = fw.sb("a_kd", [128, NTOK]); vt = fw.sb("a_v", [128, NT, 68])
            tmp = fw.sb("a_tmp", [128, 512])
            wq = [fw.sb("a_wq%d" % i, [128, 8, 128]) for i in range(2)]
            wk = fw.sb("a_wk", [128, 8, 128]); wv = fw.sb("a_wv", [128, 8, 64])
            Eall = fw.sb("a_E", [128, 5, 512])
            osb = [fw.sb("a_o%d" % i, [128, 256]) for i in range(2)]
            st = fw.sb("a_st", [128, 8])
            fw.op('pool', lambda e: e.memset(vt[:, :, 64:65], 1.0), [], [vt])
            for g in range(2):
                load_w(wq[0], g * 256, 128); load_w(wq[1], g * 256 + 128, 128, q='act')
                load_w(wk, 512 + g * 64, 64, off=0); load_w(wk, 512 + g * 64, 64, off=64, q='act')
                load_w(wv, 640 + g * 64, 64)
                proj_fm(qTs[0], wq[0]); proj_fm(qTs[1], wq[1]); proj_fm(kd, wk); proj_tok(vt, wv, 64, 68)
                rope(qTs[0], tmp); rope(qTs[1], tmp); rope(kd, tmp)
                if k.stop_after == "w0":
                    break
                for qb in range(NT):
                    if k.stop_after == "w1" and (g > 0 or qb > 4):
                        break
                    if qb < 2:
                        kts = [(0, None), (1, None)]
                    else:
                        kts = [(0, None), (1, None)]
                        if qb - 1 >= 2:
                            kts.append((qb - 1, mp))
                        kts.append((qb, None))
                        if qb + 1 < NT:
                            kts.append((qb + 1, mn))
                    for i, (kt, msk) in enumerate(kts):
                        for par in range(2):
                            ps = k.PS[(i % 2) * 2 + par]
                            pr = slice(par * 64, par * 64 + 64)
                            for jj in range(2):
                                r = jj * 2 + par
                                fw.op('pe', lambda e: e.matmul(ps[:, jj * 128:(jj + 1) * 128], kd[pr, kt * 128:(kt + 1) * 128],
                                                               qTs[r // 2][pr, qb * 128:(qb + 1) * 128], start=True, stop=True), [kd, qTs[r // 2]], [ps])
                            fw.op('act', lambda e: e.activation(Eall[:, i, par * 256:(par + 1) * 256], ps[:, 0:256], AF.Exp, scale=0.125), [ps], [Eall])
                        if msk is not None:
                            fw.op('dve', lambda e: e.tensor_tensor(Eall[:, i, :].rearrange("p (r q) -> p r q", r=4), Eall[:, i, :].rearrange("p (r q) -> p r q", r=4),
                                                                   msk[:, :].unsqueeze(1).to_broadcast([128, 4, 128]), ALU.mult), [Eall, msk], [Eall])
                    o = osb[qb % 2]
                    for r in range(4):
                        po = k.PS[4 + r % 2]
                        jr = (r % 2) * 2 + r // 2
                        for i, (kt, msk) in enumerate(kts):
                            fw.op('pe', lambda e: e.matmul(po[:, 0:65], Eall[:, i, jr * 128:(jr + 1) * 128], vt[:, kt, 0:65],
                                                           start=(i == 0), stop=(i == len(kts) - 1)), [Eall, vt], [po])
                        fw.op('dve', lambda e: e.tensor_tensor(st[:, r:r + 1], po[:, 64:65], snk[:, g * 4 + r:g * 4 + r + 1], ALU.add), [po, snk], [st])
                        fw.op('dve', lambda e: e.reciprocal(st[:, r:r + 1], st[:, r:r + 1]), [st], [st])
                        fw.op('dve', lambda e: e.tensor_scalar(o[:, r * 64:(r + 1) * 64], po[:, 0:64], st[:, r:r + 1], None, ALU.mult), [po, st], [o])
                    fw.dma('pool', k.AO, k.AO[qb * 128:(qb + 1) * 128, g * 256:(g + 1) * 256], o, o[:, :])


def outproj_ln_phase(k, layer, wout, hsrc, hdst, tiles):
    fw = k.fw
    with fw.scope():
        g_bc = fw.sb("o_g", [128, D]); b_bc = fw.sb("o_b", [128, D])
        load_bc_row(k, g_bc, k.W['post_ln_g'], k.W['post_ln_g'][layer, 0:1, :])
        load_bc_row(k, b_bc, k.W['post_ln_b'], k.W['post_ln_b'][layer, 0:1, :])
        kks = sorted(set(0 if t >= 2 else 1 for t in tiles))
        gate = {}
        for kk in kks:
            gate[kk] = fw.sb("o_gate%d" % kk, [128, D]); ada_bc(k, layer, 2, kk, gate[kk])
        wt = fw.sb("o_w", [128, 8, D])
        fw.dma('sp', wt, wt[:, 0:4, :], wout, wout[0, 0:512, :].rearrange("(c p) n -> p c n", p=128))
        fw.dma('act', wt, wt[:, 4:8, :], wout, wout[0, 512:1024, :].rearrange("(c p) n -> p c n", p=128))
        ao = [fw.sb("o_ao%d" % i, [128, D]) for i in range(2)]
        hb = [fw.sb("o_h%d" % i, [128, D]) for i in range(2)]
        aoT = [fw.sb("o_aoT%d" % i, [128, 8, 128]) for i in range(2)]
        tmp = fw.sb("o_tmp", [128, D]); st = fw.sb("o_st", [128, 4])
        for it, t in enumerate(tiles):
            a = ao[it % 2]; h = hb[it % 2]; aT = aoT[it % 2]
            fw.dma('sp', a, a[:, :], k.AO, k.AO[t * 128:(t + 1) * 128, :])
            sT, sap = src_rows(k, hsrc, t)
            fw.dma('act', h, h[:, :], sT, sap)
            transpose_tile(k, a, aT, lambda c0, c1: aT[:, c0:c1, :], k.PS[0], k.PS[1])
            kk = 0 if t >= 2 else 1
            for half in range(2):
                ps = k.PS[2 + half]
                for c in range(8):
                    fw.op('pe', lambda e: e.matmul(ps[:, :], aT[:, c, :], wt[:, c, half * 512:(half + 1) * 512], start=(c == 0), stop=(c == 7)), [aT, wt], [ps])
                fw.op('dve', lambda e: e.tensor_tensor(tmp[:, half * 512:(half + 1) * 512], ps[:, :], gate[kk][:, half * 512:(half + 1) * 512], ALU.mult), [ps, gate[kk]], [tmp])
            fw.op('dve', lambda e: e.scalar_tensor_tensor(h[:, :], h[:, :], ALPHA, tmp[:, :], ALU.mult, ALU.add), [h, tmp], [h])
            layer_norm_tile(k, h, g_bc, b_bc, tmp, st)
            fw.dma('pool', hdst, hdst[t * 128:(t + 1) * 128, :], h, h[:, :])


def moe_phase(k, layer, hin, hout, tiles, final_out):
    fw = k.fw
    ntl = len(tiles)
    with fw.scope():
        IDX = fw.sb("m_idx", [128, NT, 16], I32); GK = fw.sb("m_gk", [128, NT, 8])
        with fw.scope():
            sc = {}; sh = {}
            for kk in sorted(set(0 if t >= 2 else 1 for t in tiles)):
                sc[kk] = fw.sb("m_sc%d" % kk, [128, D]); sh[kk] = fw.sb("m_sh%d" % kk, [128, D])
                ada_bc(k, layer, 3, kk, sh[kk]); ada_bc(k, layer, 4, kk, sc[kk], plus_one=True)
            wr = fw.sb("m_wr", [128, 8, NE])
            fw.dma('sp', wr, wr[:, :, :], k.W['moe_router'], k.W['moe_router'][layer].rearrange("(c p) n -> p c n", p=128))
            bias = fw.sb("m_bias", [128, NE]); load_bc_row(k, bias, k.W['moe_bias'], k.W['moe_bias'][layer:layer + 1, :])
            base_e = fw.sb("m_base", [128, NE]); fw.dma('sp', base_e, base_e[:, :], k.C["base_e"], k.C["base_e"][:, :])
            tri = fw.sb("m_tri", [128, 128]); fw.dma('sp', tri, tri[:, :], k.C["tri_s"], k.C["tri_s"][:, :])
            wsi = fw.sb("m_wsi", [128, 8, 512]); wso = fw.sb("m_wso", [128, 2, D])
            fw.dma('act', wsi, wsi[:, :, :], k.W['moe_ws_in'], k.W['moe_ws_in'][layer].rearrange("(c p) n -> p c n", p=128))
            fw.dma('act', wso, wso[:, :, :], k.W['moe_ws_out'], k.W['moe_ws_out'][layer].rearrange("(c p) n -> p c n", p=128))
            cnt = fw.sb("m_cnt", [128, NE]); fw.op('pool', lambda e: e.memset(cnt[:, :], 0.0), [], [cnt])
            ub = [fw.sb("m_u%d" % i, [128, D]) for i in range(3)]
            uTb = [fw.sb("m_uT%d" % i, [128, 8, 128]) for i in range(2)]
            scs = fw.sb("m_scs", [128, NE]); sel = fw.sb("m_sel", [128, NE]); selm = fw.sb("m_selm", [128, NE])
            M = fw.sb("m_M", [128, NE]); G = fw.sb("m_G", [128, NE]); enc = fw.sb("m_enc", [128, NE])
            g8 = fw.sb("m_g8", [128, 8, 8]); gs = fw.sb("m_gs", [128, 8]); v8 = fw.sb("m_v8", [128, 8]); gm = fw.sb("m_gm", [128, 8]); pen = fw.sb("m_pen", [128, 8])
            e8 = fw.sb("m_e8", [128, 8]); oh = fw.sb("m_oh", [128, 8, NE]); st = fw.sb("m_st", [128, 4])
            hs = fw.sb("m_hs", [128, 4, 128]); hT = fw.sb("m_hT", [128, 2, 128]); shb = [fw.sb("m_shb%d" % i, [128, D]) for i in range(2)]
            for it, t in enumerate(tiles):
                u = ub[it % 3]; uT = uTb[it % 2]
                kk = 0 if t >= 2 else 1
                fw.dma('sp', u, u[:, :], hin, hin[t * 128:(t + 1) * 128, :])
                fw.op('dve', lambda e: e.tensor_tensor(u[:, :], u[:, :], sc[kk][:, :], ALU.mult), [u, sc[kk]], [u])
                fw.op('pool', lambda e: e.tensor_tensor(u[:, :], u[:, :], sh[kk][:, :], ALU.add), [u, sh[kk]], [u])
                transpose_tile(k, u, uT, lambda c0, c1: uT[:, c0:c1, :], k.PS[0], k.PS[1])
                ps = k.PS[2]
                for c in range(8):
                    fw.op('pe', lambda e: e.matmul(ps[:, 0:NE], uT[:, c, :], wr[:, c, :], start=(c == 0), stop=(c == 7)), [uT, wr], [ps])
                fw.op('act', lambda e: e.activation(scs[:, :], ps[:, 0:NE], AF.Sigmoid), [ps], [scs])
                fw.op('dve', lambda e: e.tensor_tensor(sel[:, :], scs[:, :], bias[:, :], ALU.add), [scs, bias], [sel])
                for gI in range(8):
                    fw.op('dve', lambda e: e.max(g8[:, gI, :], sel[:, gI * 32:(gI + 1) * 32]), [sel], [g8])
                fw.op('dve', lambda e: e.tensor_tensor(gs[:, :], g8[:, :, 0], g8[:, :, 1], ALU.add), [g8], [gs])
                fw.op('dve', lambda e: e.max(v8[:, :], gs[:, :]), [gs], [v8])
                fw.op('dve', lambda e: e.tensor_scalar(gm[:, :], gs[:, :], v8[:, 3:4], None, ALU.is_ge), [gs, v8], [gm])
                fw.op('dve', lambda e: e.tensor_scalar(pen[:, :], gm[:, :], -1.0, 1e30, ALU.add, ALU.mult), [gm], [pen])
                fw.op('dve', lambda e: e.tensor_tensor(selm[:, :].rearrange("p (g j) -> p g j", g=8), sel[:, :].rearrange("p (g j) -> p g j", g=8),
                                                       gm[:, :].unsqueeze(2).to_broadcast([128, 8, 32]), ALU.mult), [sel, gm], [selm])
                fw.op('dve', lambda e: e.tensor_tensor(selm[:, :].rearrange("p (g j) -> p g j", g=8), selm[:, :].rearrange("p (g j) -> p g j", g=8),
                                                       pen[:, :].unsqueeze(2).to_broadcast([128, 8, 32]), ALU.add), [selm, pen], [selm])
                fw.op('dve', lambda e: e.max(v8[:, :], selm[:, :]), [selm], [v8])
                fw.op('dve', lambda e: e.tensor_scalar(M[:, :], selm[:, :], v8[:, 7:8], None, ALU.is_ge), [selm, v8], [M])
                fw.op('dve', lambda e: e.tensor_tensor(G[:, :], M[:, :], scs[:, :], ALU.mult), [M, scs], [G])
                fw.op('dve', lambda e: e.tensor_reduce(st[:, 0:1], G[:, :], AX.X, ALU.add), [G], [st])
                fw.op('dve', lambda e: e.reciprocal(st[:, 0:1], st[:, 0:1]), [st], [st])
                fw.op('dve', lambda e: e.tensor_scalar(G[:, :], G[:, :], st[:, 0:1], 2.5, ALU.mult, ALU.mult), [G, st], [G])
                pp = k.PS[3]
                fw.op('pe', lambda e: e.matmul(pp[:, 0:NE], tri[:, :], M[:, :], start=True, stop=True), [tri, M], [pp])
                fw.op('pe', lambda e: e.matmul(pp[:, NE:2 * NE], k.ones[:, :], M[:, :], start=True, stop=True), [k.ones, M], [pp])
                fw.op('dve', lambda e: e.tensor_tensor(enc[:, :], pp[:, 0:NE], cnt[:, :], ALU.add), [pp, cnt], [enc])
                fw.op('dve', lambda e: e.tensor_tensor(cnt[:, :], cnt[:, :], pp[:, NE:2 * NE], ALU.add), [pp, cnt], [cnt])
                fw.op('dve', lambda e: e.tensor_scalar(selm[:, :], enc[:, :], float(CAP) - 0.5, None, ALU.is_lt), [enc], [selm])
                fw.op('dve', lambda e: e.tensor_tensor(G[:, :], G[:, :], selm[:, :], ALU.mult), [G, selm], [G])
                fw.op('dve', lambda e: e.tensor_tensor(enc[:, :], enc[:, :], base_e[:, :], ALU.add), [enc, base_e], [enc])
                fw.op('dve', lambda e: e.tensor_tensor(enc[:, :], enc[:, :], M[:, :], ALU.mult), [enc, M], [enc])
                fw.op('dve', lambda e: e.tensor_tensor(enc[:, :], enc[:, :], selm[:, :], ALU.mult), [enc, selm], [enc])
                fw.op('dve', lambda e: e.max(e8[:, :], enc[:, :]), [enc], [e8])
                fw.op('dve', lambda e: e.tensor_scalar(IDX[:, t, 0:8], e8[:, :], -1.0, None, ALU.add), [e8], [IDX])
                fw.op('dve', lambda e: e.tensor_scalar(IDX[:, t, 8:16], e8[:, :], -1.0 - HALF_ROWS, None, ALU.add), [e8], [IDX])
                fw.op('dve', lambda e: e.tensor_tensor(oh[:, :, :], enc[:, :].unsqueeze(1).to_broadcast([128, 8, NE]),
                                                       e8[:, :].unsqueeze(2).to_broadcast([128, 8, NE]), ALU.is_equal), [enc, e8], [oh])
                fw.op('pool', lambda e: e.tensor_tensor(oh[:, :, :], oh[:, :, :], G[:, :].unsqueeze(1).to_broadcast([128, 8, NE]), ALU.mult), [oh, G], [oh])
                fw.op('dve', lambda e: e.tensor_reduce(GK[:, t, :], oh[:, :, :], AX.X, ALU.add), [oh], [GK])
                for j in range(16):
                    xs = k.XS[j // 8]
                    fw.dma('pool', xs, None, u, None, fn=lambda e: e.indirect_dma_start(
                        out=xs[:, :], out_offset=bass.IndirectOffsetOnAxis(ap=IDX[:, t, j:j + 1], axis=0), in_=u[:, :], in_offset=None,
                        bounds_check=k.breg, oob_is_err=False), extra_reads=[IDX])
                for f in range(4):
                    ps2 = k.PS[4 + f % 2]
                    for c in range(8):
                        fw.op('pe', lambda e: e.matmul(ps2[:, 0:128], wsi[:, c, f * 128:(f + 1) * 128], uT[:, c, :], start=(c == 0), stop=(c == 7)), [wsi, uT], [ps2])
                    if f < 2:
                        fw.op('act', lambda e: e.activation(hs[:, f, :], ps2[:, 0:128], AF.Silu), [ps2], [hs])
                    else:
                        fw.op('dve', lambda e: e.tensor_tensor(hT[:, f - 2, :], ps2[:, 0:128], hs[:, f - 2, :], ALU.mult), [ps2, hs], [hT])
                so = shb[it % 2]
                for half in range(2):
                    ps3 = k.PS[6 + half]
                    for f in range(2):
                        fw.op('pe', lambda e: e.matmul(ps3[:, :], hT[:, f, :], wso[:, f, half * 512:(half + 1) * 512], start=(f == 0), stop=(f == 1)), [hT, wso], [ps3])
                    fw.op('act', lambda e: e.activation(so[:, half * 512:(half + 1) * 512], ps3[:, :], AF.Copy), [ps3], [so])
                fw.dma('act', k.SH, k.SH[t * 128:(t + 1) * 128, :], so, so[:, :])
        with fw.scope():
            NB = 3
            wi = [fw.sb("e_wi%d" % i, [128, 8, 512]) for i in range(NB)]
            wo = [fw.sb("e_wo%d" % i, [128, 2, D]) for i in range(NB)]
            xg = [fw.sb("e_x%d" % i, [128, D]) for i in range(2)]
            xT = [fw.sb("e_xT%d" % i, [128, 8, 128]) for i in range(2)]
            hs = [fw.sb("e_hs%d" % i, [128, 2, 128]) for i in range(2)]
            hT = [fw.sb("e_hT%d" % i, [128, 2, 128]) for i in range(2)]
            yb = [fw.sb("e_y%d" % i, [128, D]) for i in range(2)]
            win = k.W['moe_w_in']; wout = k.W['moe_w_out']
            lw = 0 if k.ne_dbg else layer
            bi = 0
            for ex in range(k.ne_dbg or NE):
                w1 = wi[ex % NB]; w2 = wo[ex % NB]
                fw.dma('sp', w1, w1[:, 0:4, :], win, win[lw, ex, 0:512, :].rearrange("(c p) n -> p c n", p=128))
                fw.dma('act', w1, w1[:, 4:8, :], win, win[lw, ex, 512:1024, :].rearrange("(c p) n -> p c n", p=128))
                fw.dma('sp', w2, w2[:, :, :], wout, wout[lw, ex].rearrange("(c p) n -> p c n", p=128))
                for blk in range(CAP // 128):
                    x = xg[bi % 2]; xt = xT[bi % 2]; h1 = hs[bi % 2]; h2 = hT[bi % 2]; y = yb[bi % 2]; bi += 1
                    hf = ex // (NE // 2); row0 = (ex % (NE // 2)) * CAP + blk * 128
                    fw.dma('act', x, x[:, :], k.XS[hf], k.XS[hf][row0:row0 + 128, :])
                    transpose_tile(k, x, xt, lambda c0, c1: xt[:, c0:c1, :], k.PS[0], k.PS[1], eng2='dve')
                    for f in range(4):
                        ps2 = k.PS[2 + f]
                        for c in range(8):
                            fw.op('pe', lambda e: e.matmul(ps2[:, 0:128], w1[:, c, f * 128:(f + 1) * 128], xt[:, c, :], start=(c == 0), stop=(c == 7)), [w1, xt], [ps2])
                        if f < 2:
                            fw.op('act', lambda e: e.activation(h1[:, f, :], ps2[:, 0:128], AF.Silu), [ps2], [h1])
                        else:
                            fw.op('dve', lambda e: e.tensor_tensor(h2[:, f - 2, :], ps2[:, 0:128], h1[:, f - 2, :], ALU.mult), [ps2, h1], [h2])
                    for half in range(2):
                        ps3 = k.PS[6 + half]
                        for f in range(2):
                            fw.op('pe', lambda e: e.matmul(ps3[:, :], h2[:, f, :], w2[:, f, half * 512:(half + 1) * 512], start=(f == 0), stop=(f == 1)), [h2, w2], [ps3])
                        if half == 0:
                            fw.op('act', lambda e: e.activation(y[:, 0:512], ps3[:, :], AF.Copy), [ps3], [y])
                        else:
                            fw.op('dve', lambda e: e.tensor_copy(y[:, 512:1024], ps3[:, :]), [ps3], [y])
                    fw.dma('pool', k.YS[hf], k.YS[hf][row0:row0 + 128, :], y, y[:, :])
        with fw.scope():
            g_bc = fw.sb("c_g", [128, D]); b_bc = fw.sb("c_b", [128, D])
            load_bc_row(k, g_bc, k.W['post_ln_g'], k.W['post_ln_g'][layer, 1:2, :])
            load_bc_row(k, b_bc, k.W['post_ln_b'], k.W['post_ln_b'][layer, 1:2, :])
            gate = {}
            for kk in sorted(set(0 if t >= 2 else 1 for t in tiles)):
                gate[kk] = fw.sb("c_gate%d" % kk, [128, D]); ada_bc(k, layer, 5, kk, gate[kk])
            R = [fw.sb("c_R%d" % i, [128, D]) for i in range(4)]
            for r in R:
                fw.op('pool', lambda e: e.memset(r[:, :], 0.0), [], [r])
            acc = [fw.sb("c_acc%d" % i, [128, D]) for i in range(2)]
            hb = [fw.sb("c_h%d" % i, [128, D]) for i in range(2)]
            tmp = fw.sb("c_tmp", [128, D]); st = fw.sb("c_st", [128, 4])
            ri = 0
            for it, t in enumerate(tiles):
                a = acc[it % 2]; h = hb[it % 2]
                kk = 0 if t >= 2 else 1
                fw.dma('sp', a, a[:, :], k.SH, k.SH[t * 128:(t + 1) * 128, :])
                fw.dma('act', h, h[:, :], hin, hin[t * 128:(t + 1) * 128, :])
                for j in range(8):
                    r = R[ri % 4]; ri += 1
                    for hf in range(2):
                        ys = k.YS[hf]
                        fw.dma('pool', r, None, ys, None, fn=lambda e: e.indirect_dma_start(
                            out=r[:, :], out_offset=None, in_=ys[:, :], in_offset=bass.IndirectOffsetOnAxis(ap=IDX[:, t, hf * 8 + j:hf * 8 + j + 1], axis=0),
                            bounds_check=k.breg, oob_is_err=False), extra_reads=[IDX])
                    eng = 'dve'
                    fw.op(eng, lambda e: e.scalar_tensor_tensor(a[:, :], r[:, :], GK[:, t, j:j + 1], a[:, :], ALU.mult, ALU.add), [r, GK, a], [a])
                fw.op('dve', lambda e: e.tensor_tensor(a[:, :], a[:, :], gate[kk][:, :], ALU.mult), [a, gate[kk]], [a])
                fw.op('dve', lambda e: e.scalar_tensor_tensor(h[:, :], h[:, :], ALPHA, a[:, :], ALU.mult, ALU.add), [h, a], [h])
                layer_norm_tile(k, h, g_bc, b_bc, tmp, st)
                if final_out is not None:
                    fw.dma('sp', final_out, final_out[(t - 2) * 128:(t - 1) * 128, :], h, h[:, :])
                else:
                    fw.dma('sp', hout, hout[t * 128:(t + 1) * 128, :], h, h[:, :])


def rwkv_phase(k):
    fw = k.fw; W = k.W
    NB = 9

    def dr(name, shape):
        return fw.dram("rk_" + name, shape)
    UD = dr("U", [NTOK, D]); UT = dr("UT", [D, NTOK]); NBT = dr("NBT", [D, NTOK])
    RT = dr("RT", [D, NTOK]); KT = dr("KT", [D, NTOK]); AT = dr("AT", [D, NTOK])
    AD = [dr("AD%d" % d, [D, NTOK]) for d in range(2)]
    BT = [dr("BT%d" % d, [D, NTOK]) for d in range(2)]
    KD = [dr("KD%d" % d, [D, NTOK]) for d in range(2)]
    VT = dr("VT", [NTOK, D]); SG = [dr("SG%d" % d, [NTOK, D]) for d in range(2)]
    GT = dr("GT", [NTOK, D]); BON = dr("BON", [NTOK, 16])

    def fmv(X):
        return X[:, :].rearrange("(c p) n -> p c n", p=128)

    with fw.scope():
        sc = [fw.sb("r_sc%d" % i, [128, D]) for i in range(2)]; sh = [fw.sb("r_sh%d" % i, [128, D]) for i in range(2)]
        for kk in range(2):
            ada_bc(k, 1, 0, kk, sh[kk]); ada_bc(k, 1, 1, kk, sc[kk], plus_one=True)
        hb = [fw.sb("r_h%d" % i, [128, D]) for i in range(3)]
        for t in range(NT):
            h = hb[t % 3]; kk = 0 if t >= 2 else 1
            fw.dma('sp', h, h[:, :], k.H2, k.H2[t * 128:(t + 1) * 128, :])
            fw.op('dve', lambda e: e.tensor_tensor(h[:, :], h[:, :], sc[kk][:, :], ALU.mult), [h, sc[kk]], [h])
            fw.op('pool', lambda e: e.tensor_tensor(h[:, :], h[:, :], sh[kk][:, :], ALU.add), [h, sh[kk]], [h])
            fw.dma('act', UD, UD[t * 128:(t + 1) * 128, :], h, h[:, :])
    with fw.scope():
        ub = [fw.sb("r_u%d" % i, [128, D]) for i in range(2)]; upb = [fw.sb("r_up%d" % i, [128, D]) for i in range(2)]
        unb = [fw.sb("r_un%d" % i, [128, D]) for i in range(2)]
        uTt = [fw.sb("r_uT%d" % i, [128, 8, 128]) for i in range(2)]; nTt = [fw.sb("r_nT%d" % i, [128, 8, 128]) for i in range(2)]
        for t in range(NT):
            u = ub[t % 2]; up = upb[t % 2]; un = unb[t % 2]; uT = uTt[t % 2]; nT = nTt[t % 2]
            r0 = t * 128
            fw.dma('sp', u, u[:, :], UD, UD[r0:r0 + 128, :])
            fw.op('pool', lambda e: e.memset(up[:, :], 0.0), [], [up])
            fw.op('pool', lambda e: e.memset(un[:, :], 0.0), [], [un])
            fw.dma('sp', up, up[1:128, :], UD, UD[r0:r0 + 127, :])
            if t not in (0, 2):
                fw.dma('sp', up, up[0:1, :], UD, UD[r0 - 1:r0, :])
            fw.dma('act', un, un[0:127, :], UD, UD[r0 + 1:r0 + 128, :])
            if t not in (1, NT - 1):
                fw.dma('act', un, un[127:128, :], UD, UD[r0 + 128:r0 + 129, :])
            fw.op('dve', lambda e: e.tensor_tensor(up[:, :], up[:, :], un[:, :], ALU.add), [up, un], [up])
            transpose_tile(k, u, uT, lambda c0, c1: uT[:, c0:c1, :], k.PS[0], k.PS[1])
            transpose_tile(k, up, nT, lambda c0, c1: nT[:, c0:c1, :], k.PS[2], k.PS[3])
            fw.dma('pool', UT, fmv(UT)[:, :, r0:r0 + 128], uT, uT[:, :, :])
            fw.dma('pool', NBT, fmv(NBT)[:, :, r0:r0 + 128], nT, nT[:, :, :])
    if k.stop_after == 'r_1b':
        return
    with fw.scope():
        mu = fw.sb("r_mu", [128, 6, 8]); om = fw.sb("r_om", [128, 6, 8]); hm = fw.sb("r_hm", [128, 6, 8])
        fw.dma('sp', mu, mu[:, :, :], k.mucol, k.mucol[:, :, :])
        fw.op('dve', lambda e: e.tensor_scalar(om[:, :, :], mu[:, :, :], -1.0, 1.0, ALU.mult, ALU.add), [mu], [om])
        fw.op('dve', lambda e: e.tensor_scalar(hm[:, :, :], mu[:, :, :], 0.5, None, ALU.mult), [mu], [hm])
        ublk = [fw.sb("r_ub%d" % i, [128, 8, 256]) for i in range(2)]; nblk = [fw.sb("r_nb%d" % i, [128, 8, 256]) for i in range(2)]
        xm = [fw.sb("r_xm%d" % i, [128, 8, 256]) for i in range(2)]
        stage = [fw.sb("r_stg%d" % i, [128, 8, 256]) for i in range(2)]
        cnt = [0]

        def get_xm(b, m):
            i = cnt[0] % 2; cnt[0] += 1
            ub_, nb_, x_ = ublk[i], nblk[i], xm[i]
            fw.dma('sp', ub_, ub_[:, :, :], UT, fmv(UT)[:, :, b * 256:(b + 1) * 256])
            fw.dma('sp', nb_, nb_[:, :, :], NBT, fmv(NBT)[:, :, b * 256:(b + 1) * 256])
            for c in range(8):
                fw.op('act', lambda e: e.activation(x_[:, c, :], ub_[:, c, :], AF.Copy, scale=om[:, m, c:c + 1]), [ub_, om], [x_])
                fw.op('dve', lambda e: e.scalar_tensor_tensor(x_[:, c, :], nb_[:, c, :], hm[:, m, c:c + 1], x_[:, c, :], ALU.mult, ALU.add), [nb_, hm, x_], [x_])
            return x_

        def tokview(st):
            return st[:, :, :].rearrange("p a n -> p (a n)").rearrange("p (a n) -> p a n", a=2)

        with fw.scope():
            wt = fw.sb("r_w", [128, 8, D])
            for job, (mi, dst) in enumerate(((0, RT), (2, KT), (3, VT))):
                fw.dma('sp', wt, wt[:, 0:4, :], W['rk_w_rkv'], W['rk_w_rkv'][0, job, 0:512, :].rearrange("(c p) n -> p c n", p=128))
                fw.dma('act', wt, wt[:, 4:8, :], W['rk_w_rkv'], W['rk_w_rkv'][0, job, 512:1024, :].rearrange("(c p) n -> p c n", p=128))
                for b in range(NB):
                    x_ = get_xm(b, mi); st = stage[b % 2]
                    if job < 2:
                        for oc in range(8):
                            ps = k.PS[oc % 4]
                            for c in range(8):
                                fw.op('pe', lambda e: e.matmul(ps[:, 0:256], wt[:, c, oc * 128:(oc + 1) * 128], x_[:, c, :], start=(c == 0), stop=(c == 7)), [wt, x_], [ps])
                            if oc % 2 == 0:
                                fw.op('dve', lambda e: e.tensor_copy(st[:, oc, :], ps[:, 0:256]), [ps], [st])
                            else:
                                fw.op('act', lambda e: e.activation(st[:, oc, :], ps[:, 0:256], AF.Copy), [ps], [st])
                        fw.dma('pool', dst, fmv(dst)[:, :, b * 256:(b + 1) * 256], st, st[:, :, :])
                    else:
                        tv = tokview(st)
                        for tt in range(2):
                            for half in range(2):
                                ps = k.PS[4 + (tt * 2 + half) % 4]
                                for c in range(8):
                                    fw.op('pe', lambda e: e.matmul(ps[:, :], x_[:, c, tt * 128:(tt + 1) * 128], wt[:, c, half * 512:(half + 1) * 512], start=(c == 0), stop=(c == 7)), [x_, wt], [ps])
                                if half == 0:
                                    fw.op('dve', lambda e: e.tensor_copy(tv[:, tt, 0:512], ps[:, :]), [ps], [st])
                                else:
                                    fw.op('act', lambda e: e.activation(tv[:, tt, 512:1024], ps[:, :], AF.Copy), [ps], [st])
                        fw.dma('pool', dst, dst[b * 256:(b + 1) * 256, :].rearrange("(a p) n -> p a n", p=128), st, tv)
        with fw.scope():
            d0bc = fw.sb("r_d0", [128, D]); w1 = fw.sb("r_w1", [128, 8, 64]); w2 = fw.sb("r_w2", [64, D]); t1 = fw.sb("r_t1", [64, 256])
            i0 = fw.sb("r_i0", [128, 2, 8]); fw.dma('sp', i0, i0[:, :, :], k.icl0col, k.icl0col[:, :, :])
            for d in range(2):
                load_bc_row(k, d0bc, W['rk_decay0'], W['rk_decay0'][0, d:d + 1, :])
                fw.dma('sp', w1, w1[:, :, :], W['rk_decay1'], W['rk_decay1'][0, d].rearrange("(c p) n -> p c n", p=128))
                fw.dma('sp', w2, w2[:, :], W['rk_decay2'], W['rk_decay2'][0, d])
                for b in range(NB):
                    x_ = get_xm(b, 1); st = stage[b % 2]; tv = tokview(st)
                    ps = k.PS[0]
                    for c in range(8):
                        fw.op('pe', lambda e: e.matmul(ps[0:64, 0:256], w1[:, c, :], x_[:, c, :], start=(c == 0), stop=(c == 7)), [w1, x_], [ps])
                    fw.op('act', lambda e: e.activation(t1[:, :], ps[0:64, 0:256], AF.Tanh), [ps], [t1])
                    for tt in range(2):
                        for half in range(2):
                            ps2 = k.PS[4 + (tt * 2 + half) % 4]
                            fw.op('pe', lambda e: e.matmul(ps2[:, :], t1[0:64, tt * 128:(tt + 1) * 128], w2[0:64, half * 512:(half + 1) * 512], start=True, stop=True), [t1, w2], [ps2])
                            fw.op('dve', lambda e: e.tensor_tensor(tv[:, tt, half * 512:(half + 1) * 512], ps2[:, :], d0bc[:, half * 512:(half + 1) * 512], ALU.add), [ps2, d0bc], [st])
                    fw.op('act', lambda e: e.activation(st[:, :, :], st[:, :, :], AF.Sigmoid), [st], [st])
                    fw.dma('pool', SG[d], SG[d][b * 256:(b + 1) * 256, :].rearrange("(a p) n -> p a n", p=128), st, tv)
            for d in range(2):
                fw.dma('sp', w1, w1[:, :, :], W['rk_iclr1'], W['rk_iclr1'][0, d].rearrange("(c p) n -> p c n", p=128))
                fw.dma('sp', w2, w2[:, :], W['rk_iclr2'], W['rk_iclr2'][0, d])
                for b in range(NB):
                    x_ = get_xm(b, 4); st = stage[b % 2]
                    ps = k.PS[0]
                    for c in range(8):
                        fw.op('pe', lambda e: e.matmul(ps[0:64, 0:256], w1[:, c, :], x_[:, c, :], start=(c == 0), stop=(c == 7)), [w1, x_], [ps])
                    fw.op('act', lambda e: e.activation(t1[:, :], ps[0:64, 0:256], AF.Copy), [ps], [t1])
                    for oc in range(8):
                        ps2 = k.PS[4 + oc % 4]
                        fw.op('pe', lambda e: e.matmul(ps2[:, 0:256], w2[0:64, oc * 128:(oc + 1) * 128], t1[0:64, :], start=True, stop=True), [t1, w2], [ps2])
                        fw.op('act', lambda e: e.activation(st[:, oc, :], ps2[:, 0:256], AF.Sigmoid, bias=i0[:, d, oc:oc + 1]), [ps2, i0], [st])
                    fw.dma('pool', AD[d], fmv(AD[d])[:, :, b * 256:(b + 1) * 256], st, st[:, :, :])
        with fw.scope():
            g1 = fw.sb("r_g1", [128, 8, 160]); g2a = fw.sb("r_g2a", [128, D]); g2b = fw.sb("r_g2b", [32, D])
            sa = fw.sb("r_sa", [128, 256]); sbb = fw.sb("r_sb", [32, 256])
            fw.dma('sp', g1, g1[:, :, :], W['rk_gate1'], W['rk_gate1'][0].rearrange("(c p) n -> p c n", p=128))
            fw.dma('sp', g2a, g2a[:, :], W['rk_gate2'], W['rk_gate2'][0, 0:128, :]); fw.dma('sp', g2b, g2b[:, :], W['rk_gate2'], W['rk_gate2'][0, 128:160, :])
            for b in range(NB):
                x_ = get_xm(b, 5); st = stage[b % 2]; tv = tokview(st)
                ps = k.PS[0]; psb = k.PS[1]
                for c in range(8):
                    fw.op('pe', lambda e: e.matmul(ps[:, 0:256], g1[:, c, 0:128], x_[:, c, :], start=(c == 0), stop=(c == 7)), [g1, x_], [ps])
                for c in range(8):
                    fw.op('pe', lambda e: e.matmul(psb[0:32, 0:256], g1[:, c, 128:160], x_[:, c, :], start=(c == 0), stop=(c == 7)), [g1, x_], [psb])
                fw.op('act', lambda e: e.activation(sa[:, :], ps[:, 0:256], AF.Sigmoid), [ps], [sa])
                fw.op('act', lambda e: e.activation(sbb[:, :], psb[0:32, 0:256], AF.Sigmoid), [psb], [sbb])
                for tt in range(2):
                    for half in range(2):
                        ps2 = k.PS[4 + (tt * 2 + half) % 4]
                        fw.op('pe', lambda e: e.matmul(ps2[:, :], sa[:, tt * 128:(tt + 1) * 128], g2a[:, half * 512:(half + 1) * 512], start=True, stop=False), [sa, g2a], [ps2])
                        fw.op('pe', lambda e: e.matmul(ps2[:, :], sbb[0:32, tt * 128:(tt + 1) * 128], g2b[0:32, half * 512:(half + 1) * 512], start=False, stop=True), [sbb, g2b], [ps2])
                        if half == 0:
                            fw.op('dve', lambda e: e.tensor_copy(tv[:, tt, 0:512], ps2[:, :]), [ps2], [st])
                        else:
                            fw.op('act', lambda e: e.activation(tv[:, tt, 512:1024], ps2[:, :], AF.Copy), [ps2], [st])
                fw.dma('pool', GT, GT[b * 256:(b + 1) * 256, :].rearrange("(a p) n -> p a n", p=128), st, tv)
    if k.stop_after == 'r_1c':
        return
    with fw.scope():
        kkc = fw.sb("r_kkc", [128, 8]); kac = fw.sb("r_kac", [128, 8]); rkc = fw.sb("r_rkc", [128, 8])
        fw.dma('sp', kkc, kkc[:, :], k.kkcol, k.kkcol[:, :]); fw.dma('sp', kac, kac[:, :], k.kacol, k.kacol[:, :]); fw.dma('sp', rkc, rkc[:, :], k.rkcol, k.rkcol[:, :])
        blk = fw.sb("r_blk", [128, 128]); fw.dma('sp', blk, blk[:, :], k.C["blk64"], k.C["blk64"][:, :])
        sel2 = fw.sb("r_sel2", [128, 2]); fw.dma('sp', sel2, sel2[:, :], k.C["sel2"], k.C["sel2"][:, :])
        tiny = fw.sb("r_tiny", [128, 1]); fw.op('pool', lambda e: e.memset(tiny[:, :], 0.0), [], [tiny])
        kt = fw.sb("r2_k", [128, 8, 256]); rt = fw.sb("r2_r", [128, 8, 256]); a0 = fw.sb("r2_a0", [128, 8, 256]); a1 = fw.sb("r2_a1", [128, 8, 256])
        kkt = fw.sb("r2_kk", [128, 8, 256]); tm = fw.sb("r2_tm", [128, 8, 256]); ks = fw.sb("r2_ks", [128, 8, 256]); bon = fw.sb("r2_bon", [128, 2, 16])
        for b in range(NB):
            cs = slice(b * 256, (b + 1) * 256)
            fw.dma('sp', kt, kt[:, :, :], KT, fmv(KT)[:, :, cs]); fw.dma('act', rt, rt[:, :, :], RT, fmv(RT)[:, :, cs])
            fw.dma('sp', a0, a0[:, :, :], AD[0], fmv(AD[0])[:, :, cs]); fw.dma('act', a1, a1[:, :, :], AD[1], fmv(AD[1])[:, :, cs])
            for c in range(8):
                fw.op('act', lambda e: e.activation(kkt[:, c, :], kt[:, c, :], AF.Copy, scale=kkc[:, c:c + 1]), [kt, kkc], [kkt])
            fw.op('pool', lambda e: e.tensor_tensor(tm[:, :, :], kkt[:, :, :], kkt[:, :, :], ALU.mult), [kkt], [tm])
            for c in range(8):
                ps = k.PS[c % 4]
                fw.op('pe', lambda e: e.matmul(ps[:, 0:256], blk[:, :], tm[:, c, :], start=True, stop=True), [blk, tm], [ps])
                fw.op('dve', lambda e: e.tensor_scalar(ks[:, c, :], ps[:, 0:256], 1e-24, None, ALU.max), [ps], [ks])
            fw.op('act', lambda e: e.activation(ks[:, :, :], ks[:, :, :], AF.Sqrt, bias=tiny[:, 0:1], scale=1.0), [ks, tiny], [ks])
            fw.op('dve', lambda e: e.reciprocal(ks[:, :, :], ks[:, :, :]), [ks], [ks])
            fw.op('dve', lambda e: e.tensor_tensor(kkt[:, :, :], kkt[:, :, :], ks[:, :, :], ALU.mult), [kkt, ks], [kkt])
            fw.op('act', lambda e: e.activation(tm[:, :, :], kkt[:, :, :], AF.Copy, scale=-1.0), [kkt], [tm])
            fw.dma('pool', AT, fmv(AT)[:, :, cs], tm, tm[:, :, :])
            fw.op('pool', lambda e: e.memset(ks[:, :, :], 0.0), [], [ks])
            for d, ad in enumerate((a0, a1)):
                fw.op('dve', lambda e: e.tensor_tensor(tm[:, :, :], kkt[:, :, :], ad[:, :, :], ALU.mult), [kkt, ad], [tm])
                fw.dma('pool', BT[d], fmv(BT[d])[:, :, cs], tm, tm[:, :, :])
                for c in range(8):
                    fw.op('dve', lambda e: e.tensor_scalar(ad[:, c, :], ad[:, c, :], -1.0, kac[:, c:c + 1], ALU.add, ALU.mult), [ad, kac], [ad])
                fw.op('dve', lambda e: e.tensor_scalar(ad[:, :, :], ad[:, :, :], 1.0, None, ALU.add), [ad], [ad])
                fw.op('dve', lambda e: e.tensor_tensor(ad[:, :, :], ad[:, :, :], kt[:, :, :], ALU.mult), [ad, kt], [ad])
                fw.dma('pool', KD[d], fmv(KD[d])[:, :, cs], ad, ad[:, :, :])
                fw.op('dve', lambda e: e.tensor_tensor(ks[:, :, :], ks[:, :, :], ad[:, :, :], ALU.add), [ks, ad], [ks])
            fw.op('dve', lambda e: e.tensor_tensor(ks[:, :, :], ks[:, :, :], rt[:, :, :], ALU.mult), [ks, rt], [ks])
            for c in range(8):
                fw.op('act', lambda e: e.activation(ks[:, c, :], ks[:, c, :], AF.Copy, scale=rkc[:, c:c + 1]), [ks, rkc], [ks])
            pb = k.PS[4 + b % 2]
            for tt in range(2):
                for c in range(8):
                    fw.op('pe', lambda e: e.matmul(pb[:, tt * 16 + 2 * c:tt * 16 + 2 * c + 2], ks[:, c, tt * 128:(tt + 1) * 128], sel2[:, :], start=True, stop=True), [ks, sel2], [pb])
            fw.op('dve', lambda e: e.tensor_copy(bon[:, :, :], pb[:, 0:32].rearrange("p (a n) -> p a n", a=2)), [pb], [bon])
            fw.dma('pool', BON, BON[b * 256:(b + 1) * 256, :].rearrange("(a p) n -> p a n", p=128), bon, bon[:, :, :])
    if k.stop_after == 'r_2':
        return
    with fw.scope():
        ident = k.ident
        triI = [fw.sb("s_triI%d" % d, [128, 128]) for d in range(2)]; triE = [fw.sb("s_triE%d" % d, [128, 128]) for d in range(2)]
        msi2 = [fw.sb("s_msi%d" % d, [128, 4, 128]) for d in range(2)]; mn = [fw.sb("s_mn%d" % d, [128, 128]) for d in range(2)]
        for d, sfx in enumerate(("f", "b")):
            fw.dma('sp', triI[d], triI[d][:, :], k.C["triI_" + sfx], k.C["triI_" + sfx][:, :])
            fw.dma('sp', triE[d], triE[d][:, :], k.C["triE_" + sfx], k.C["triE_" + sfx][:, :])
            fw.dma('sp', msi2[d], msi2[d][:, 0:2, :], k.C["msi_" + sfx], k.C["msi_" + sfx][:, :, :])
            fw.dma('sp', msi2[d], msi2[d][:, 2:4, :], k.C["msi_" + sfx], k.C["msi_" + sfx][:, :, :])
            fw.dma('sp', mn[d], mn[d][:, :], k.C["mn_" + sfx], k.C["mn_" + sfx][:, :])
        lng = fw.sb("s_lng", [128, D]); lnb = fw.sb("s_lnb", [128, D])
        load_bc_row(k, lng, W['rk_lnx'], W['rk_lnx'][0, 0:1, :]); load_bc_row(k, lnb, W['rk_lnx'], W['rk_lnx'][0, 1:2, :])
        epsx = fw.sb("s_eps", [128, 1]); fw.op('pool', lambda e: e.memset(epsx[:, :], 64e-5), [], [epsx])
        U4 = range(4)
        F = [[fw.sb("s_F%d_%d" % (u, i), [64, 4, 128]) for i in range(2)] for u in U4]
        Vb = [[fw.sb("s_V%d_%d" % (u, i), [128, 64]) for i in range(2)] for u in U4]
        Sb = [[fw.sb("s_S%d_%d" % (u, i), [128, 64]) for i in range(2)] for u in U4]
        PI4 = fw.sb("s_PI", [64, 4, 128]); PE4 = fw.sb("s_PE", [64, 4, 128]); PV4 = fw.sb("s_PV", [64, 4, 128])
        AR = [fw.sb("s_AR%d" % u, [64, 2, 128]) for u in U4]; BK = [fw.sb("s_BK%d" % u, [64, 2, 128]) for u in U4]
        TK = [fw.sb("s_TK%d" % u, [128, 4, 128]) for u in U4]; NM = [fw.sb("s_NM%d" % u, [128, 128]) for u in U4]
        SQ = [[fw.sb("s_SQ%d_%d" % (u, i), [128, 2, 128]) for i in range(2)] for u in U4]
        Wsb = [fw.sb("s_W%d" % u, [128, 64]) for u in U4]; W1s = [fw.sb("s_W1%d" % u, [128, 64]) for u in U4]
        BKt = [fw.sb("s_BKt%d" % u, [128, 2, 64]) for u in U4]
        Hst = [fw.sb("s_H%d" % u, [64, 64]) for u in U4]
        Yacc = [fw.sb("s_Y%d" % hh, [128, 16, 64]) for hh in range(2)]
        ytmp = fw.sb("s_ytmp", [128, 64]); y1s = fw.sb("s_y1s", [128, 64])
        fin = fw.sb("s_fin", [128, 16, 64]); fsq = fw.sb("s_fsq", [128, 16, 64]); fst = fw.sb("s_fst", [128, 16, 4])
        vfin = fw.sb("s_vfin", [128, 16, 64]); gfin = fw.sb("s_gfin", [128, 16, 64]); bfin = fw.sb("s_bfin", [128, 16, 16])
        order = [list(range(NT)), [1, 0] + list(range(NT - 1, 1, -1))]
        for pair in range(8):
            if k.stop_after == "r_scan1" and pair > 0:
                break
            for u in U4:
                fw.op('pool', lambda e: e.memset(Hst[u][:, :], 0.0), [], [Hst[u]])
            ywritten = [set(), set()]
            for it in range(NT if k.scan_stop is None else 1):
                units = [(u, pair * 2 + u // 2, u % 2, order[u % 2][it]) for u in U4]
                i2 = it % 2
                for (u, h, d, c) in units:
                    f = F[u][i2]; rs = slice(h * 64, (h + 1) * 64); cs = slice(c * 128, (c + 1) * 128)
                    fw.dma('sp', f, f[:, 0, :], RT, RT[rs, cs]); fw.dma('sp', f, f[:, 1, :], KD[d], KD[d][rs, cs])
                    fw.dma('sp', f, f[:, 2, :], AT, AT[rs, cs]); fw.dma('sp', f, f[:, 3, :], BT[d], BT[d][rs, cs])
                    fw.dma('act', Vb[u][i2], Vb[u][i2][:, :], VT, VT[cs, rs]); fw.dma('act', Sb[u][i2], Sb[u][i2][:, :], SG[d], SG[d][cs, rs])
                if k.scan_stop is not None and k.scan_stop < 1:
                    break
                for (u, h, d, c) in units:
                    fw.op('pe', lambda e: e.matmul(k.PS[3][0:64, u * 128:(u + 1) * 128], Sb[u][i2][:, :], triI[d][:, :], start=True, stop=True), [Sb[u][i2], triI[d]], [k.PS[3]])
                    fw.op('pe', lambda e: e.matmul(k.PS[4][0:64, u * 128:(u + 1) * 128], Sb[u][i2][:, :], triE[d][:, :], start=True, stop=True), [Sb[u][i2], triE[d]], [k.PS[4]])
                fw.op('act', lambda e: e.activation(PI4[:, :, :], k.PS[3][0:64, :].rearrange("p (a n) -> p a n", a=4), AF.Exp), [k.PS[3]], [PI4])
                fw.op('act', lambda e: e.activation(PV4[:, :, :], k.PS[3][0:64, :].rearrange("p (a n) -> p a n", a=4), AF.Exp, scale=-1.0), [k.PS[3]], [PV4])
                fw.op('act', lambda e: e.activation(PE4[:, :, :], k.PS[4][0:64, :].rearrange("p (a n) -> p a n", a=4), AF.Exp), [k.PS[4]], [PE4])
                if k.scan_stop is not None and k.scan_stop < 2:
                    break
                for (u, h, d, c) in units:
                    f = F[u][i2]
                    fw.op('dve', lambda e: e.tensor_tensor(AR[u][:, 0, :], f[:, 2, :], PE4[:, u, :], ALU.mult), [f, PE4], [AR[u]])
                    fw.op('pool', lambda e: e.tensor_tensor(AR[u][:, 1, :], f[:, 0, :], PI4[:, u, :], ALU.mult), [f, PI4], [AR[u]])
                    fw.op('dve', lambda e: e.tensor_tensor(BK[u][:, 0, :], f[:, 3, :], PV4[:, u, :], ALU.mult), [f, PV4], [BK[u]])
                    fw.op('pool', lambda e: e.tensor_tensor(BK[u][:, 1, :], f[:, 1, :], PV4[:, u, :], ALU.mult), [f, PV4], [BK[u]])
                if k.scan_stop is not None and k.scan_stop < 3:
                    break
                for (u, h, d, c) in units:
                    ar2 = AR[u][:, :, :].rearrange("p a n -> p (a n)")
                    fw.op('pe', lambda e: e.matmul(k.PS[1][:, u * 128:(u + 1) * 128], AR[u][:, 0, :], BK[u][:, 0, :], start=True, stop=True), [AR[u], BK[u]], [k.PS[1]])
                    fw.op('pe', lambda e: e.matmul(k.PS[0][:, 0:256], BK[u][:, 0, :], ar2, start=True, stop=True), [AR[u], BK[u]], [k.PS[0]])
                    fw.op('pe', lambda e: e.matmul(k.PS[0][:, 256:512], BK[u][:, 1, :], ar2, start=True, stop=True), [AR[u], BK[u]], [k.PS[0]])
                    fw.op('dve', lambda e: e.tensor_tensor(TK[u][:, :, :], k.PS[0][:, :].rearrange("p (a n) -> p a n", a=4), msi2[d][:, :, :], ALU.mult), [k.PS[0], msi2[d]], [TK[u]])
                if k.scan_stop is not None and k.scan_stop < 4:
                    break
                for (u, h, d, c) in units:
                    fw.op('dve', lambda e: e.tensor_tensor(NM[u][:, :], k.PS[1][:, u * 128:(u + 1) * 128], mn[d][:, :], ALU.mult), [k.PS[1], mn[d]], [NM[u]])
                if k.scan_stop is not None and k.scan_stop < 5:
                    break
                for (u, h, d, c) in units:
                    fw.op('pe', lambda e: e.matmul(k.PS[2][:, u * 64:(u + 1) * 64], AR[u][:, 0, :], Hst[u][:, :], start=True, stop=True), [AR[u], Hst[u]], [k.PS[2]])
                    fw.op('pe', lambda e: e.matmul(k.PS[5][:, u * 64:(u + 1) * 64], TK[u][:, 2, :], Vb[u][i2][:, :], start=True, stop=True), [TK[u], Vb[u][i2]], [k.PS[5]])
                for (u, h, d, c) in units:
                    fw.op('act', lambda e: e.activation(W1s[u][:, :], k.PS[2][:, u * 64:(u + 1) * 64], AF.Copy), [k.PS[2]], [W1s[u]])
                    fw.op('dve', lambda e: e.tensor_tensor(Wsb[u][:, :], k.PS[5][:, u * 64:(u + 1) * 64], W1s[u][:, :], ALU.add), [k.PS[5], W1s[u]], [Wsb[u]])
                if k.scan_stop is not None and k.scan_stop < 6:
                    break
                Ncur = {u: (NM[u][:, :], TK[u][:, 0, :], [NM[u], TK[u]]) for u in U4}
                for lvl in range(7):
                    for (u, h, d, c) in units:
                        N_, NT_, deps = Ncur[u]
                        fw.op('pe', lambda e: e.matmul(k.PS[5][:, u * 64:(u + 1) * 64], NT_, Wsb[u][:, :], start=True, stop=True), deps + [Wsb[u]], [k.PS[5]])
                        if lvl < 6:
                            fw.op('pe', lambda e: e.matmul(k.PS[6 + u // 2][:, (u % 2) * 256:(u % 2) * 256 + 128], NT_, N_, start=True, stop=True), deps, [k.PS[6 + u // 2]])
                            fw.op('pe', lambda e: e.matmul(k.PS[6 + u // 2][:, (u % 2) * 256 + 128:(u % 2) * 256 + 256], N_, NT_, start=True, stop=True), deps, [k.PS[6 + u // 2]])
                    for (u, h, d, c) in units:
                        fw.op('dve', lambda e: e.tensor_tensor(Wsb[u][:, :], Wsb[u][:, :], k.PS[5][:, u * 64:(u + 1) * 64], ALU.add), [k.PS[5], Wsb[u]], [Wsb[u]])
                        if lvl < 6:
                            sq = SQ[u][lvl % 2]
                            fw.op('act', lambda e: e.activation(sq[:, :, :], k.PS[6 + u // 2][:, (u % 2) * 256:(u % 2) * 256 + 256].rearrange("p (a n) -> p a n", a=2), AF.Copy), [k.PS[6 + u // 2]], [sq])
                            Ncur[u] = (sq[:, 0, :], sq[:, 1, :], [sq])
                if k.scan_stop is not None and k.scan_stop < 7:
                    break
                for (u, h, d, c) in units:
                    if c < 2:
                        continue
                    hh = u // 2
                    fw.op('pe', lambda e: e.matmul(k.PS[2][:, 256 + u * 64:256 + (u + 1) * 64], AR[u][:, 1, :], Hst[u][:, :], start=True, stop=True), [AR[u], Hst[u]], [k.PS[2]])
                    fw.op('pe', lambda e: e.matmul(k.PS[5][:, 256 + u * 64:256 + (u + 1) * 64], TK[u][:, 1, :], Wsb[u][:, :], start=True, stop=False), [TK[u], Wsb[u]], [k.PS[5]])
                    fw.op('pe', lambda e: e.matmul(k.PS[5][:, 256 + u * 64:256 + (u + 1) * 64], TK[u][:, 3, :], Vb[u][i2][:, :], start=False, stop=True), [TK[u], Vb[u][i2]], [k.PS[5]])
                    fw.op('act', lambda e: e.activation(y1s[:, :], k.PS[2][:, 256 + u * 64:256 + (u + 1) * 64], AF.Copy), [k.PS[2]], [y1s])
                    if c in ywritten[hh]:
                        fw.op('dve', lambda e: e.tensor_tensor(ytmp[:, :], k.PS[5][:, 256 + u * 64:256 + (u + 1) * 64], y1s[:, :], ALU.add), [k.PS[5], y1s], [ytmp])
                        fw.op('dve', lambda e: e.tensor_tensor(Yacc[hh][:, c - 2, :], Yacc[hh][:, c - 2, :], ytmp[:, :], ALU.add), [Yacc[hh], ytmp], [Yacc[hh]])
                    else:
                        fw.op('dve', lambda e: e.tensor_tensor(Yacc[hh][:, c - 2, :], k.PS[5][:, 256 + u * 64:256 + (u + 1) * 64], y1s[:, :], ALU.add), [k.PS[5], y1s], [Yacc[hh]])
                        ywritten[hh].add(c)
                if k.scan_stop is not None and k.scan_stop < 8:
                    break
                for (u, h, d, c) in units:
                    fw.op('pe', lambda e: e.matmul(k.PS[1][:, u * 128:u * 128 + 64], BK[u][:, 0, :], ident[0:64, 0:64], start=True, stop=True), [BK[u], ident], [k.PS[1]])
                    fw.op('pe', lambda e: e.matmul(k.PS[1][:, u * 128 + 64:u * 128 + 128], BK[u][:, 1, :], ident[0:64, 0:64], start=True, stop=True), [BK[u], ident], [k.PS[1]])
                for (u, h, d, c) in units:
                    fw.op('act', lambda e: e.activation(BKt[u][:, :, :], k.PS[1][:, u * 128:(u + 1) * 128].rearrange("p (a n) -> p a n", a=2), AF.Copy), [k.PS[1]], [BKt[u]])
                for (u, h, d, c) in units:
                    fw.op('pe', lambda e: e.matmul(k.PS[4][0:64, u * 64:(u + 1) * 64], BKt[u][:, 0, :], Wsb[u][:, :], start=True, stop=False), [BKt[u], Wsb[u]], [k.PS[4]])
                    fw.op('pe', lambda e: e.matmul(k.PS[4][0:64, u * 64:(u + 1) * 64], BKt[u][:, 1, :], Vb[u][i2][:, :], start=False, stop=True), [BKt[u], Vb[u][i2]], [k.PS[4]])
                for (u, h, d, c) in units:
                    pc = PI4[:, u, 127:128] if d == 0 else PI4[:, u, 0:1]
                    fw.op('dve', lambda e: e.tensor_tensor(Hst[u][:, :], Hst[u][:, :], k.PS[4][0:64, u * 64:(u + 1) * 64], ALU.add), [k.PS[4], Hst[u]], [Hst[u]])
                    fw.op('dve', lambda e: e.tensor_scalar(Hst[u][:, :], Hst[u][:, :], pc, None, ALU.mult), [Hst[u], PI4], [Hst[u]])
            if k.scan_stop is not None:
                break
            for hh in range(2):
                h = pair * 2 + hh; rs = slice(h * 64, (h + 1) * 64)
                Y = Yacc[hh]
                fw.dma('sp', vfin, vfin[:, :, :], VT, VT[256:NTOK, rs].rearrange("(a p) n -> p a n", p=128))
                fw.dma('act', gfin, gfin[:, :, :], GT, GT[256:NTOK, rs].rearrange("(a p) n -> p a n", p=128))
                fw.dma('sp', bfin, bfin[:, :, :], BON, BON[256:NTOK, :].rearrange("(a p) n -> p a n", p=128))
                fw.op('dve', lambda e: e.tensor_reduce(fst[:, :, 0], Y[:, :, :], AX.X, ALU.add), [Y], [fst])
                fw.op('dve', lambda e: e.tensor_scalar(fst[:, :, 0], fst[:, :, 0], -1.0 / 64, None, ALU.mult), [fst], [fst])
                fw.op('dve', lambda e: e.tensor_tensor(fin[:, :, :], Y[:, :, :], fst[:, :, 0:1].to_broadcast([128, 16, 64]), ALU.add), [Y, fst], [fin])
                fw.op('pool', lambda e: e.tensor_tensor(fsq[:, :, :], fin[:, :, :], fin[:, :, :], ALU.mult), [fin], [fsq])
                fw.op('dve', lambda e: e.tensor_reduce(fst[:, :, 1], fsq[:, :, :], AX.X, ALU.add), [fsq], [fst])
                fw.op('act', lambda e: e.activation(fst[:, :, 2], fst[:, :, 1], AF.Sqrt, bias=epsx[:, 0:1], scale=1.0 / 64), [fst, epsx], [fst])
                fw.op('dve', lambda e: e.reciprocal(fst[:, :, 2], fst[:, :, 2]), [fst], [fst])
                fw.op('dve', lambda e: e.tensor_tensor(fin[:, :, :], fin[:, :, :], fst[:, :, 2:3].to_broadcast([128, 16, 64]), ALU.mult), [fin, fst], [fin])
                fw.op('dve', lambda e: e.tensor_tensor(fin[:, :, :], fin[:, :, :], lng[:, rs].unsqueeze(1).to_broadcast([128, 16, 64]), ALU.mult), [fin, lng], [fin])
                fw.op('dve', lambda e: e.tensor_tensor(fin[:, :, :], fin[:, :, :], lnb[:, rs].unsqueeze(1).to_broadcast([128, 16, 64]), ALU.add), [fin, lnb], [fin])
                fw.op('dve', lambda e: e.tensor_tensor(vfin[:, :, :], vfin[:, :, :], bfin[:, :, h:h + 1].to_broadcast([128, 16, 64]), ALU.mult), [vfin, bfin], [vfin])
                fw.op('dve', lambda e: e.tensor_tensor(fin[:, :, :], fin[:, :, :], vfin[:, :, :], ALU.add), [fin, vfin], [fin])
                fw.op('dve', lambda e: e.tensor_tensor(fin[:, :, :], fin[:, :, :], gfin[:, :, :], ALU.mult), [fin, gfin], [fin])
                fw.dma('pool', k.AO, k.AO[256:NTOK, rs].rearrange("(a p) n -> p a n", p=128), fin, fin[:, :, :])


_NC_CACHE = {}


def make_in_maps(inputs, used=None):
    consts = host_consts()
    maps = []
    shared = {n: np.ascontiguousarray(np.asarray(inputs[n], dtype=np.float32)) for n in W_SHAPES}
    def col(v):
        return np.ascontiguousarray(np.asarray(v, np.float32).reshape(8, 128).T)
    shared["mucol"] = np.ascontiguousarray(np.stack([col(shared["rk_mu"][0, m]) for m in range(6)], 1))
    shared["icl0col"] = np.ascontiguousarray(np.stack([col(shared["rk_iclr0"][0, d]) for d in range(2)], 1))
    shared["kkcol"] = col(shared["rk_k_k"][0]); shared["kacol"] = col(shared["rk_k_a"][0]); shared["rkcol"] = col(shared["rk_r_k"][0].reshape(-1))
    shared["sinkb"] = np.ascontiguousarray(np.broadcast_to(shared["att_sink"].reshape(1, 8), (128, 8)))
    for n, v in consts.items():
        shared["c_" + n] = np.ascontiguousarray(v.astype(np.float32))
    x = np.asarray(inputs['x'], np.float32); c = np.asarray(inputs['c'], np.float32)
    ctx = np.asarray(inputs['ctx'], np.float32); c_ctx = np.asarray(inputs['c_ctx'], np.float32)
    for b in range(8):
        m = dict(shared)
        m["x"] = np.ascontiguousarray(x[b]); m["ctx"] = np.ascontiguousarray(ctx[b])
        cc = np.stack([c[b], c_ctx], -1).reshape(8, 128, 2).transpose(1, 0, 2)
        m["cc"] = np.ascontiguousarray(cc)
        if used is not None:
            m = {n: m[n] for n in used}
        maps.append(m)
    return maps


def kernel(**inputs):
    if "nc" not in _NC_CACHE:
        _NC_CACHE["nc"] = build()
    nc = _NC_CACHE["nc"]
    res = run_bass_kernel_spmd(nc, make_in_maps(inputs, nc._used()), core_ids=list(range(8)))
    return np.stack([np.asarray(r["out"]) for r in res.results], 0).astype(np.float32)
```

```python
import math
import numpy as np
import concourse.bass as bass
import concourse.mybir as mybir
from concourse.bass_utils import run_bass_kernel_spmd
from contextlib import ExitStack, contextmanager

F32 = mybir.dt.float32
I32 = mybir.dt.int32
BF16 = mybir.dt.bfloat16
ALU = mybir.AluOpType
AF = mybir.ActivationFunctionType
AX = mybir.AxisListType

D = 1024
SEQ = 2048
LCTX = 256
NTOK = SEQ + LCTX
NT = NTOK // 128
NE = 256
CAP = 384
HALF_ROWS = (NE // 2) * CAP
ALPHA = 4 ** 0.25
LN_EPS = 1e-5
C0 = -math.exp(-0.5)


class T:
    def __init__(self, fw, t, name):
        self.fw = fw; self.t = t; self.name = name
        self.w = {}; self.r = {}
        self.dsem = None; self.dcnt = 0

    def __getitem__(self, k):
        return self.t[k]


class FW:
    ENG = ('pe', 'dve', 'act', 'pool', 'sp')

    def __init__(self, nc):
        self.nc = nc
        self.eng = {'pe': nc.tensor, 'dve': nc.vector, 'act': nc.scalar, 'pool': nc.gpsimd, 'sp': nc.sync}
        self.root = ExitStack()
        self.stacks = [self.root]
        self.scopeT = [[]]
        self.free_dsems = []
        self.sem = {}; self.cnt = {}; self.seen = {}
        self.nsem = 0
        for e in self.ENG:
            self._new_prog(e)
        self.bar_sem = self._alloc_sem("BAR"); self.bar_cnt = 0
        self.ninst = 0

    def _alloc_sem(self, name):
        self.nsem += 1
        return self.root.enter_context(self.nc.semaphore(name + "_%d" % self.nsem))

    def _new_prog(self, e):
        self.sem[e] = self._alloc_sem("S_" + e); self.cnt[e] = 0
        self.seen[e] = {}

    def _reg(self, b):
        self.scopeT[-1].append(b)
        return b

    def sb(self, name, shape, dtype=F32):
        self.uid = getattr(self, "uid", 0) + 1
        name = "%s_u%d" % (name, self.uid)
        t = self.stacks[-1].enter_context(self.nc.sbuf_tensor(name, list(shape), dtype))
        return self._reg(T(self, t, name))

    def ps(self, name, shape, dtype=F32):
        t = self.stacks[-1].enter_context(self.nc.psum_tensor(name, list(shape), dtype))
        return self._reg(T(self, t, name))

    def dram(self, name, shape, dtype=F32, kind="Internal"):
        t = self.nc.dram_tensor(name, list(shape), dtype, kind=kind)
        return self._reg(T(self, t.ap(), name))

    def _dsem(self, b):
        if b.dsem is None:
            if self.free_dsems:
                b.dsem, b.dcnt = self.free_dsems.pop()
            else:
                b.dsem = self._alloc_sem("D"); b.dcnt = 0
        return b.dsem

    def _waits(self, e, reads, writes):
        own = self.sem[e]
        waits = {}

        def merge(d, skip_own):
            for s, v in d.items():
                if skip_own and s is own:
                    continue
                k = id(s)
                if k not in waits or waits[k][1] < v:
                    waits[k] = (s, v)
        for b in reads:
            merge(b.w, False)
        for b in writes:
            merge(b.w, True)
            merge(b.r, True)
        seen = self.seen[e]
        for k, (s, v) in waits.items():
            if seen.get(k, 0) >= v:
                continue
            self.eng[e].wait_ge(s, v)
            seen[k] = v

    def op(self, e, fn, reads=(), writes=()):
        self._waits(e, reads, writes)
        inst = fn(self.eng[e])
        self.cnt[e] += 1; self.ninst += 1
        c = self.cnt[e]; s = self.sem[e]
        inst.then_inc(s, 1)
        for b in writes:
            b.w[s] = c; b.r = {}
        for b in reads:
            if b not in writes:
                b.r[s] = c
        return inst

    def dma(self, q, out_b, out_ap, in_b, in_ap, fn=None, extra_reads=()):
        self._waits(q, [in_b] + list(extra_reads), [out_b])
        s = self._dsem(out_b)
        if fn is None:
            inst = self.eng[q].dma_start(out=out_ap, in_=in_ap)
        else:
            inst = fn(self.eng[q])
        self.ninst += 1
        out_b.dcnt += 16
        inst.then_inc(s, 16)
        out_b.w[s] = out_b.dcnt
        out_b.r = {}
        in_b.r[s] = out_b.dcnt
        for b in extra_reads:
            b.r[s] = out_b.dcnt
        return inst

    def all_T(self):
        for lst in self.scopeT:
            for b in lst:
                yield b

    def barrier(self):
        sp = self.eng['sp']
        seen = self.seen['sp']
        for e in self.ENG:
            if e != 'sp' and self.cnt[e] > 0:
                sp.wait_ge(self.sem[e], self.cnt[e])
        for b in self.all_T():
            if b.dsem is not None and b.dcnt > 0 and seen.get(id(b.dsem), 0) < b.dcnt:
                sp.wait_ge(b.dsem, b.dcnt); seen[id(b.dsem)] = b.dcnt
        self.bar_cnt += 1
        sp.sem_inc(self.bar_sem, 1)
        for e in self.ENG:
            if e != 'sp':
                self.eng[e].wait_ge(self.bar_sem, self.bar_cnt)
        for b in self.all_T():
            b.w = {}; b.r = {}
        for e in self.ENG:
            if e != 'sp' and self.cnt[e] > 12000:
                self._new_prog(e)

    @contextmanager
    def scope(self):
        st = ExitStack(); self.stacks.append(st); self.scopeT.append([])
        yield
        self.barrier()
        for b in self.scopeT.pop():
            if b.dsem is not None:
                self.free_dsems.append((b.dsem, b.dcnt)); b.dsem = None
        self.stacks.pop().close()

    def finish(self):
        self.barrier()
        self.root.close()


def host_consts():
    c = {}
    c["ident"] = np.eye(128, dtype=np.float32)
    s = np.arange(128)[:, None]; t = np.arange(128)[None, :]
    c["tri_s"] = (s < t).astype(np.float32)
    c["ones"] = np.ones((128, 128), np.float32)
    c["base_e"] = np.broadcast_to((np.arange(NE, dtype=np.float32) * CAP + 1.0)[None, :], (128, NE)).copy()
    rows = SEQ // 64
    row = np.repeat(np.arange(rows), 64).astype(np.float32)
    col = np.tile(np.arange(64), rows).astype(np.float32)
    inv = (np.float32(10000.0) ** (-np.arange(0, 32, 2, dtype=np.float32) / np.float32(32))).astype(np.float32)
    ar = row[:, None] * inv; ac = col[:, None] * inv
    ang = np.concatenate([ar, ar, ac, ac], -1).astype(np.float32)
    cosT = np.cos(ang).T.astype(np.float32); sinT = np.sin(ang).T.astype(np.float32)
    c["cos2"] = np.concatenate([cosT, cosT], 0).copy(); c["sin2"] = np.concatenate([sinT, sinT], 0).copy()
    rm = np.zeros((64, 64), np.float32)
    for d in range(64):
        if (d % 32) < 16:
            rm[d + 16, d] = -1.0
        else:
            rm[d - 16, d] = 1.0
    rm2 = np.zeros((128, 128), np.float32); rm2[:64, :64] = rm; rm2[64:, 64:] = rm
    c["rm2"] = rm2
    kk = np.arange(128)[:, None]; qq = np.arange(128)[None, :]
    c["mask_prev"] = (qq <= kk).astype(np.float32)
    c["mask_next"] = (kk <= qq).astype(np.float32)
    c["triI_f"] = (C0 * (s <= t)).astype(np.float32); c["triE_f"] = (C0 * (s < t)).astype(np.float32)
    c["triI_b"] = (C0 * (s >= t)).astype(np.float32); c["triE_b"] = (C0 * (s > t)).astype(np.float32)
    c["msi_f"] = np.stack([(s < t), (s <= t)], 1).astype(np.float32)
    c["msi_b"] = np.stack([(s > t), (s >= t)], 1).astype(np.float32)
    c["mn_f"] = (t < s).astype(np.float32)
    c["mn_b"] = (t > s).astype(np.float32)
    c["blk64"] = ((s // 64) == (t // 64)).astype(np.float32)
    sel2 = np.zeros((128, 2), np.float32); sel2[:64, 0] = 1.0; sel2[64:, 1] = 1.0
    c["sel2"] = sel2
    return c


CONST_SHAPES = {k: v.shape for k, v in host_consts().items()}

W_SHAPES = {
    'ada_w': (2, 1024, 6144), 'ada_b': (2, 6144), 'post_ln_g': (2, 2, 1024), 'post_ln_b': (2, 2, 1024),
    'att_w_in': (1, 1024, 2304), 'att_w_out': (1, 1024, 1024), 'att_sink': (1, 2, 4),
    'diff_lambda_vecs': (1, 4, 64), 'diff_subln_g': (1, 128),
    'rk_mu': (1, 6, 1024), 'rk_w_rkv': (1, 3, 1024, 1024), 'rk_w_out': (1, 1024, 1024),
    'rk_decay0': (1, 2, 1024), 'rk_decay1': (1, 2, 1024, 64), 'rk_decay2': (1, 2, 64, 1024),
    'rk_iclr0': (1, 2, 1024), 'rk_iclr1': (1, 2, 1024, 64), 'rk_iclr2': (1, 2, 64, 1024),
    'rk_gate1': (1, 1024, 160), 'rk_gate2': (1, 160, 1024), 'rk_k_k': (1, 1024), 'rk_k_a': (1, 1024),
    'rk_r_k': (1, 16, 64), 'rk_lnx': (1, 2, 1024),
    'moe_router': (2, 1024, 256), 'moe_bias': (2, 256), 'moe_w_in': (2, 256, 1024, 512),
    'moe_w_out': (2, 256, 256, 1024), 'moe_ws_in': (2, 1024, 512), 'moe_ws_out': (2, 256, 1024),
}


class K:
    pass


def build(stop_after=None, debug=False, ne_dbg=None):
    nc = bass.Bass("TRN2", target_bir_lowering=False)
    try:
        nc.allow_low_precision("bf16 expert matmuls with fp32 accumulation")
    except Exception:
        pass
    fw = FW(nc)
    k = K(); k.fw = fw; k.nc = nc; k.debug = debug; k.stop_after = stop_after; k.ne_dbg = ne_dbg
    import os as _os
    k.scan_stop = int(_os.environ['SCAN_STOP']) if 'SCAN_STOP' in _os.environ else None
    k.breg = nc.gpsimd.to_reg(HALF_ROWS - 1)

    def din(name, shape, dt=F32):
        return fw._reg(T(fw, nc.dram_tensor(name, list(shape), dt, kind="ExternalInput").ap(), name))
    k.sinkb = din("sinkb", [128, 8])
    k.mucol = din("mucol", [128, 6, 8]); k.icl0col = din("icl0col", [128, 2, 8])
    k.kkcol = din("kkcol", [128, 8]); k.kacol = din("kacol", [128, 8]); k.rkcol = din("rkcol", [128, 8])
    k.x = din("x", [SEQ, D]); k.ctx = din("ctx", [LCTX, D]); k.cc = din("cc", [128, 8, 2])
    class Lazy(dict):
        def __init__(self, shapes, prefix):
            super().__init__(); self.shapes = shapes; self.prefix = prefix
        def __missing__(self, n):
            v = din(self.prefix + n, self.shapes[n]); self[n] = v; return v
    wsh = dict(W_SHAPES)
    if ne_dbg:
        wsh['moe_w_in'] = (1, ne_dbg, 1024, 512); wsh['moe_w_out'] = (1, ne_dbg, 256, 1024)
    k.W = Lazy(wsh, ""); k.C = Lazy(CONST_SHAPES, "c_")
    nc._used = lambda: ["x", "ctx", "cc", "sinkb", "mucol", "icl0col", "kkcol", "kacol", "rkcol"] + (["H2in"] if k.h2in else []) + list(k.W.keys()) + ["c_" + n for n in k.C.keys()]
    k.out = fw._reg(T(fw, nc.dram_tensor("out", [SEQ, D], F32, kind="ExternalOutput").ap(), "out"))
    skind = "ExternalOutput" if debug else "Internal"
    k.H1 = fw.dram("H1", [NTOK, D], kind=skind)
    k.H2 = fw.dram("H2", [NTOK, D], kind=skind)
    k.AO = fw.dram("AO", [NTOK, D], kind=skind)
    k.XS = [fw.dram("XS%d" % i, [HALF_ROWS, D]) for i in range(2)]
    k.YS = [fw.dram("YS%d" % i, [HALF_ROWS, D]) for i in range(2)]
    k.SH = fw.dram("SH", [NTOK, D])
    k.PS = [fw.ps("psb%d" % i, [128, 512]) for i in range(8)]

    k.h2in = bool(stop_after and stop_after.startswith("r_"))
    if k.h2in:
        k.H2 = din("H2in", [NTOK, D])
    with fw.scope():
        k.ident = fw.sb("ident", [128, 128]); fw.dma('sp', k.ident, k.ident[:, :], k.C["ident"], k.C["ident"][:, :])
        k.ones = fw.sb("ones", [128, 128]); fw.dma('sp', k.ones, k.ones[:, :], k.C["ones"], k.C["ones"][:, :])
        k.epsln = fw.sb("epsln", [128, 1]); fw.op('pool', lambda e: e.memset(k.epsln[:, :], LN_EPS), [], [k.epsln])
        k.cact = fw.sb("cact", [128, 8, 2])
        fw.dma('sp', k.cact, k.cact[:, :, :], k.cc, k.cc[:, :, :])
        fw.op('act', lambda e: e.activation(k.cact[:, :, :], k.cact[:, :, :], AF.Silu), [k.cact], [k.cact])
        k.cbc = []
        for kk in range(2):
            t = fw.sb("cbc%d" % kk, [128, 8, 128])
            fw.op('dve', lambda e: e.tensor_copy(t[:, :, :], k.cact[:, :, kk:kk + 1].to_broadcast([128, 8, 128])), [k.cact], [t])
            k.cbc.append(t)
        if not k.h2in:
            zero_fill(k)
        print("nsem", fw.nsem, "ninst", fw.ninst, flush=True)
        if stop_after == "zero":
            fw.finish(); return nc
        if k.h2in:
            rwkv_phase(k)
            if stop_after != "r_all":
                fw.finish(); return nc
            outproj_ln_phase(k, 1, k.W['rk_w_out'], [k.H2, k.H2], k.H1, range(2, NT))
            fw.finish(); return nc
        attn_phase(k)
        if stop_after in ("a0", "a1", "b0", "b1", "b2", "w0", "w1", "w2"):
            fw.finish(); return nc
        print("nsem", fw.nsem, "ninst", fw.ninst, flush=True)
        if stop_after == "attn":
            fw.finish(); return nc
        outproj_ln_phase(k, 0, k.W['att_w_out'], [k.ctx, k.x], k.H1, range(NT))
        if stop_after == "mix0":
            fw.finish(); return nc
        moe_phase(k, 0, k.H1, k.H2, list(range(NT)), None)
        if stop_after == "moe0":
            fw.finish(); return nc
        rwkv_phase(k)
        if stop_after == "rwkv":
            fw.finish(); return nc
        outproj_ln_phase(k, 1, k.W['rk_w_out'], [k.H2, k.H2], k.H1, range(2, NT))
        if stop_after == "mix1":
            fw.finish(); return nc
        moe_phase(k, 1, k.H1, None, list(range(2, NT)), k.out)
    fw.finish()
    return nc


def src_rows(k, srcs, t):
    if srcs[0] is srcs[1]:
        return srcs[0], srcs[0][t * 128:(t + 1) * 128, :]
    if t < 2:
        return srcs[0], srcs[0][t * 128:(t + 1) * 128, :]
    return srcs[1], srcs[1][(t - 2) * 128:(t - 1) * 128, :]


def zero_fill(k):
    fw = k.fw
    with fw.scope():
        z = fw.sb("zf", [128, 4, D])
        fw.op('pool', lambda e: e.memset(z[:, :, :], 0.0), [], [z])
        for hf in range(2):
            for i in range(HALF_ROWS // 512):
                q = 'sp' if i % 2 == 0 else 'act'
                fw.dma(q, k.XS[hf], k.XS[hf][i * 512:(i + 1) * 512, :].rearrange("(a p) n -> p a n", p=128), z, z[:, :, :])
                if k.ne_dbg:
                    fw.dma(q, k.YS[hf], k.YS[hf][i * 512:(i + 1) * 512, :].rearrange("(a p) n -> p a n", p=128), z, z[:, :, :])


def ada_bc(k, layer, j, kk, dst, plus_one=False):
    fw = k.fw
    aw = k.W['ada_w']; ab = k.W['ada_b']
    with fw.scope():
        bb = fw.sb("ada_bb", [128, D])
        fw.dma('act', bb, bb[:, :], ab, ab[layer:layer + 1, j * D:(j + 1) * D].partition_broadcast(128))
        for half in range(2):
            wt = fw.sb("ada_wt%d" % half, [128, 8, 512])
            c0 = j * D + half * 512
            fw.dma('sp', wt, wt[:, :, :], aw, aw[layer, :, c0:c0 + 512].rearrange("(c p) n -> p c n", p=128))
            ps = k.PS[half]
            for c in range(8):
                fw.op('pe', lambda e: e.matmul(ps[:, :], k.cbc[kk][:, c, :], wt[:, c, :], start=(c == 0), stop=(c == 7)),
                      [k.cbc[kk], wt], [ps])
            fw.op('dve', lambda e: e.tensor_tensor(dst[:, half * 512:(half + 1) * 512], ps[:, :], bb[:, half * 512:(half + 1) * 512], ALU.add),
                  [ps, bb], [dst])
        if plus_one:
            fw.op('dve', lambda e: e.tensor_scalar(dst[:, :], dst[:, :], 1.0, None, ALU.add), [dst], [dst])


def load_bc_row(k, dst, src, src_ap, q='act'):
    k.fw.dma(q, dst, dst[:, :], src, src_ap.partition_broadcast(128))


def transpose_tile(k, src, dstT, dst_ap_fn, psa, psb, eng2='act'):
    fw = k.fw
    for half, ps in ((0, psa), (1, psb)):
        for c in range(4):
            cc = half * 4 + c
            fw.op('pe', lambda e: e.transpose(ps[:, c * 128:(c + 1) * 128], src[:, cc * 128:(cc + 1) * 128], k.ident[:, :]),
                  [src, k.ident], [ps])
        if half == 0:
            fw.op('dve', lambda e: e.tensor_copy(dst_ap_fn(0, 4), ps[:, :].rearrange("p (c n) -> p c n", c=4)), [ps], [dstT])
        else:
            fw.op(eng2, (lambda e: e.activation(dst_ap_fn(4, 8), ps[:, :].rearrange("p (c n) -> p c n", c=4), AF.Copy)) if eng2 == 'act'
                  else (lambda e: e.tensor_copy(dst_ap_fn(4, 8), ps[:, :].rearrange("p (c n) -> p c n", c=4))), [ps], [dstT])


def layer_norm_tile(k, z, g_bc, b_bc, tmp, st):
    fw = k.fw
    fw.op('dve', lambda e: e.tensor_reduce(st[:, 0:1], z[:, :], AX.X, ALU.add), [z], [st])
    fw.op('dve', lambda e: e.tensor_scalar(st[:, 1:2], st[:, 0:1], -1.0 / D, None, ALU.mult), [st], [st])
    fw.op('dve', lambda e: e.tensor_scalar(z[:, :], z[:, :], st[:, 1:2], None, ALU.add), [z, st], [z])
    fw.op('act', lambda e: e.activation(tmp[:, :], z[:, :], AF.Square, accum_out=st[:, 2:3]), [z], [tmp, st])
    fw.op('act', lambda e: e.activation(st[:, 3:4], st[:, 2:3], AF.Sqrt, bias=k.epsln[:, 0:1], scale=1.0 / D), [st, k.epsln], [st])
    fw.op('dve', lambda e: e.reciprocal(st[:, 3:4], st[:, 3:4]), [st], [st])
    fw.op('dve', lambda e: e.scalar_tensor_tensor(z[:, :], z[:, :], st[:, 3:4], g_bc[:, :], ALU.mult, ALU.mult), [z, st, g_bc], [z])
    fw.op('pool', lambda e: e.tensor_tensor(z[:, :], z[:, :], b_bc[:, :], ALU.add), [z, b_bc], [z])


def attn_phase(k):
    fw = k.fw
    W = k.W['att_w_in']
    with fw.scope():
        uT = fw.sb("uT", [128, 8, NTOK])
        with fw.scope():
            if k.stop_after == "a0":
                return
            sc = [fw.sb("a_sc%d" % i, [128, D]) for i in range(2)]
            sh = [fw.sb("a_sh%d" % i, [128, D]) for i in range(2)]
            for kk in range(2):
                ada_bc(k, 0, 0, kk, sh[kk]); ada_bc(k, 0, 1, kk, sc[kk], plus_one=True)
            hb = [fw.sb("a_h%d" % i, [128, D]) for i in range(2)]
            for t in range(NT):
                h = hb[t % 2]
                sT, sap = src_rows(k, [k.ctx, k.x], t)
                fw.dma('sp', h, h[:, :], sT, sap)
                kk = 0 if t >= 2 else 1
                fw.op('dve', lambda e: e.tensor_tensor(h[:, :], h[:, :], sc[kk][:, :], ALU.mult), [h, sc[kk]], [h])
                fw.op('pool', lambda e: e.tensor_tensor(h[:, :], h[:, :], sh[kk][:, :], ALU.add), [h, sh[kk]], [h])
                transpose_tile(k, h, uT, lambda a, b: uT[:, a:b, t * 128:(t + 1) * 128], k.PS[(t % 2) * 2], k.PS[(t % 2) * 2 + 1])
        if k.stop_after == "a1":
            for c in range(8):
                fw.dma('sp', k.AO, k.AO[c * 128:(c + 1) * 128, 0:NTOK // 4].rearrange("p (a n) -> p a n", a=1)[:, 0, :], uT, uT[:, c, 0:NTOK // 4])
            return
        cos2 = fw.sb("cos2", [128, SEQ]); sin2 = fw.sb("sin2", [128, SEQ]); rm2 = fw.sb("rm2", [128, 128])
        fw.dma('sp', cos2, cos2[:, :], k.C["cos2"], k.C["cos2"][:, :])
        fw.dma('act', sin2, sin2[:, :], k.C["sin2"], k.C["sin2"][:, :])
        fw.dma('sp', rm2, rm2[:, :], k.C["rm2"], k.C["rm2"][:, :])
        TB = [(0, 256)] + [(256 + i * 512, 512) for i in range(4)]

        def proj_fm(dst, wt, M=128):
            for bi, (t0, n) in enumerate(TB):
                ps = k.PS[4 + bi % 2]
                for c in range(8):
                    fw.op('pe', lambda e: e.matmul(ps[0:M, 0:n], wt[:, c, 0:M], uT[:, c, t0:t0 + n], start=(c == 0), stop=(c == 7)),
                          [wt, uT], [ps])
                if bi % 2 == 0:
                    fw.op('dve', lambda e: e.tensor_copy(dst[0:M, t0:t0 + n], ps[0:M, 0:n]), [ps], [dst])
                else:
                    fw.op('act', lambda e: e.activation(dst[0:M, t0:t0 + n], ps[0:M, 0:n], AF.Copy), [ps], [dst])

        def proj_tok(dst, wt, M, ncol):
            for g in range(0, NT, 3):
                ps = k.PS[6 + (g // 3) % 2]
                for i in range(3):
                    t = g + i
                    for c in range(8):
                        fw.op('pe', lambda e: e.matmul(ps[:, i * 128:i * 128 + M], uT[:, c, t * 128:(t + 1) * 128], wt[:, c, 0:M],
                                                       start=(c == 0), stop=(c == 7)), [uT, wt], [ps])
                fw.op('dve', lambda e: e.tensor_copy(dst[:, g:g + 3, 0:M], ps[:, 0:384].rearrange("p (a n) -> p a n", a=3)[:, :, 0:M]), [ps], [dst])

        def rope(dst, tmp):
            for bi in range(4):
                t0 = 256 + bi * 512
                ps = k.PS[4 + bi % 2]
                fw.op('pe', lambda e: e.matmul(ps[:, :], rm2[:, :], dst[:, t0:t0 + 512], start=True, stop=True), [rm2, dst], [ps])
                fw.op('dve', lambda e: e.tensor_tensor(tmp[:, :], ps[:, :], sin2[:, bi * 512:(bi + 1) * 512], ALU.mult), [ps, sin2], [tmp])
                fw.op('pool', lambda e: e.tensor_tensor(dst[:, t0:t0 + 512], dst[:, t0:t0 + 512], cos2[:, bi * 512:(bi + 1) * 512], ALU.mult), [dst, cos2], [dst])
                fw.op('dve', lambda e: e.tensor_tensor(dst[:, t0:t0 + 512], dst[:, t0:t0 + 512], tmp[:, :], ALU.add), [dst, tmp], [dst])

        def load_w(wt, col0, M, off=0, q='sp'):
            fw.dma(q, wt, wt[:, :, off:off + M], W, W[0, :, col0:col0 + M].rearrange("(c p) n -> p c n", p=128))

        with fw.scope():
          if k.stop_after not in ("w0", "w1", "w2"):
                lam_init = 0.8 - 0.6 * math.exp(0.0)
                lvf = fw.sb("lv", [128, 256]); lsm = fw.sb("lsm", [128, 4]); lam = fw.sb("lam", [128, 2])
                fw.dma('act', lvf, lvf[:, :], k.W['diff_lambda_vecs'], k.W['diff_lambda_vecs'][0].rearrange("(o b) c -> o (b c)", o=1).partition_broadcast(128))
                fw.op('dve', lambda e: e.tensor_tensor(lvf[:, 0:64], lvf[:, 0:64], lvf[:, 64:128], ALU.mult), [lvf], [lvf])
                fw.op('dve', lambda e: e.tensor_tensor(lvf[:, 128:192], lvf[:, 128:192], lvf[:, 192:256], ALU.mult), [lvf], [lvf])
                fw.op('dve', lambda e: e.tensor_reduce(lsm[:, :], lvf[:, :].rearrange("p (b c) -> p b c", b=4), AX.X, ALU.add), [lvf], [lsm])
                fw.op('act', lambda e: e.activation(lsm[:, :], lsm[:, :], AF.Exp), [lsm], [lsm])
                fw.op('dve', lambda e: e.tensor_tensor(lam[:, 0:1], lsm[:, 0:1], lsm[:, 2:3], ALU.subtract), [lsm], [lam])
                fw.op('dve', lambda e: e.tensor_scalar(lam[:, 1:2], lam[:, 0:1], lam_init, -1.0, ALU.add, ALU.mult), [lam], [lam])
                gsc = fw.sb("gsc", [128, 128])
                load_bc_row(k, gsc, k.W['diff_subln_g'], k.W['diff_subln_g'][0:1, :])
                fw.op('dve', lambda e: e.tensor_scalar(gsc[:, :], gsc[:, :], 1.0 - lam_init, None, ALU.mult), [gsc], [gsc])
                epsb = fw.sb("epsb", [128, 1]); fw.op('pool', lambda e: e.memset(epsb[:, :], 1e-5), [], [epsb])
                qT = fw.sb("b_qT", [128, NTOK]); kT = fw.sb("b_kT", [128, NTOK]); vt = fw.sb("b_v", [128, NT, 132])
                tmp = fw.sb("b_tmp", [128, 512])
                wts = [fw.sb("b_w%d" % i, [128, 8, 128]) for i in range(3)]
                Eb = [fw.sb("b_E%d" % i, [128, 512]) for i in range(2)]
                osb = [fw.sb("b_o%d" % i, [128, 128]) for i in range(2)]
                t0b = fw.sb("b_t0", [128, 128]); sq = fw.sb("b_sq", [128, 128]); st = fw.sb("b_st", [128, 8])
                fw.op('pool', lambda e: e.memset(vt[:, :, 128:129], 1.0), [], [vt])
                for h in range(4):
                    load_w(wts[0], 768 + h * 128, 128); load_w(wts[1], 1280 + h * 128, 128, q='act'); load_w(wts[2], 1792 + h * 128, 128)
                    proj_fm(qT, wts[0]); proj_fm(kT, wts[1]); proj_tok(vt, wts[2], 128, 132)
                    rope(qT, tmp); rope(kT, tmp)
                    if k.stop_after == "b0":
                        fw.dma('sp', k.AO, k.AO[0:128, :], qT, qT[:, 0:1024]); fw.dma('sp', k.AO, k.AO[128:256, :], kT, kT[:, 256:1280])
                        fw.dma('sp', k.AO, k.AO[256:384, 0:129 * 7].rearrange("p (a n) -> p a n", a=7), vt, vt[:, 0:7, 0:129])
                        break
                    for qb in range(NT):
                        if k.stop_after == "b1" and (h > 0 or qb > 3):
                            break
                        keys = range(NT) if qb >= 2 else range(2)
                        kgroups = [list(keys)[i:i + 4] for i in range(0, len(keys), 4)]
                        po = [k.PS[2], k.PS[3]]
                        it = 0
                        for m in range(2):
                            pr = slice(m * 64, (m + 1) * 64)
                            for gi, kg in enumerate(kgroups):
                                ps = k.PS[it % 2]; E = Eb[it % 2]; it += 1
                                n = len(kg)
                                for i, kt in enumerate(kg):
                                    fw.op('pe', lambda e: e.matmul(ps[:, i * 128:(i + 1) * 128], kT[pr, kt * 128:(kt + 1) * 128],
                                                                   qT[pr, qb * 128:(qb + 1) * 128], start=True, stop=True), [kT, qT], [ps])
                                fw.op('act', lambda e: e.activation(E[:, 0:n * 128], ps[:, 0:n * 128], AF.Exp, scale=0.125), [ps], [E])
                                for i, kt in enumerate(kg):
                                    fw.op('pe', lambda e: e.matmul(po[m][:, 0:129], E[:, i * 128:(i + 1) * 128], vt[:, kt, 0:129],
                                                                   start=(gi == 0 and i == 0), stop=(gi == len(kgroups) - 1 and i == n - 1)),
                                          [E, vt], [po[m]])
                        o = osb[qb % 2]
                        fw.op('dve', lambda e: e.reciprocal(st[:, 0:1], po[0][:, 128:129]), [po[0]], [st])
                        fw.op('dve', lambda e: e.reciprocal(st[:, 1:2], po[1][:, 128:129]), [po[1]], [st])
                        fw.op('dve', lambda e: e.tensor_tensor(st[:, 1:2], st[:, 1:2], lam[:, 1:2], ALU.mult), [st, lam], [st])
                        fw.op('dve', lambda e: e.tensor_scalar(t0b[:, :], po[0][:, 0:128], st[:, 0:1], None, ALU.mult), [po[0], st], [t0b])
                        fw.op('dve', lambda e: e.scalar_tensor_tensor(t0b[:, :], po[1][:, 0:128], st[:, 1:2], t0b[:, :], ALU.mult, ALU.add), [po[1], st, t0b], [t0b])
                        fw.op('act', lambda e: e.activation(sq[:, :], t0b[:, :], AF.Square, accum_out=st[:, 2:3]), [t0b], [sq, st])
                        fw.op('act', lambda e: e.activation(st[:, 3:4], st[:, 2:3], AF.Sqrt, bias=epsb[:, 0:1], scale=1.0 / 128), [st, epsb], [st])
                        fw.op('dve', lambda e: e.reciprocal(st[:, 3:4], st[:, 3:4]), [st], [st])
                        fw.op('dve', lambda e: e.scalar_tensor_tensor(o[:, :], t0b[:, :], st[:, 3:4], gsc[:, :], ALU.mult, ALU.mult), [t0b, st, gsc], [o])
                        fw.dma('pool', k.AO, k.AO[qb * 128:(qb + 1) * 128, 512 + h * 128:512 + (h + 1) * 128], o, o[:, :])
        if k.stop_after in ("b0", "b1", "b2"):
            return
        with fw.scope():
            snk = fw.sb("snk", [128, 8])
            fw.dma('sp', snk, snk[:, :], k.sinkb, k.sinkb[:, :])
            fw.op('act', lambda e: e.activation(snk[:, :], snk[:, :], AF.Exp), [snk], [snk])
            mp = fw.sb("mprev", [128, 128]); mn = fw.sb("mnext", [128, 128])
            fw.dma('sp', mp, mp[:, :], k.C["mask_prev"], k.C["mask_prev"][:, :])
            fw.dma('sp', mn, mn[:, :], k.C["mask_next"], k.C["mask_next"][:, :])
            qTs = [fw.sb("a_qT%d" % i, [128, NTOK]) for i in range(2)]
            kd = fw.sb("a_kd", [128, NTOK]); vt = fw.sb("a_v", [128, NT, 68])
            tmp = fw.sb("a_tmp", [128, 512])
            wq = [fw.sb("a_wq%d" % i, [128, 8, 128]) for i in range(2)]
            wk = fw.sb("a_wk", [128, 8, 128]); wv = fw.sb("a_wv", [128, 8, 64])
            Eall = fw.sb("a_E", [128, 5, 512])
            osb = [fw.sb("a_o%d" % i, [128, 256]) for i in range(2)]
            st = fw.sb("a_st", [128, 8])
            fw.op('pool', lambda e: e.memset(vt[:, :, 64:65], 1.0), [], [vt])
            for g in range(2):
                load_w(wq[0], g * 256, 128); load_w(wq[1], g * 256 + 128, 128, q='act')
                load_w(wk, 512 + g * 64, 64, off=0); load_w(wk, 512 + g * 64, 64, off=64, q='act')
                load_w(wv, 640 + g * 64, 64)
                proj_fm(qTs[0], wq[0]); proj_fm(qTs[1], wq[1]); proj_fm(kd, wk); proj_tok(vt, wv, 64, 68)
                rope(qTs[0], tmp); rope(qTs[1], tmp); rope(kd, tmp)
                if k.stop_after == "w0":
                    break
                for qb in range(NT):
                    if k.stop_after == "w1" and (g > 0 or qb > 4):
                        break
                    if qb < 2:
                        kts = [(0, None), (1, None)]
                    else:
                        kts = [(0, None), (1, None)]
                        if qb - 1 >= 2:
                            kts.append((qb - 1, mp))
                        kts.append((qb, None))
                        if qb + 1 < NT:
                            kts.append((qb + 1, mn))
                    for i, (kt, msk) in enumerate(kts):
                        for par in range(2):
                            ps = k.PS[(i % 2) * 2 + par]
                            pr = slice(par * 64, par * 64 + 64)
                            for jj in range(2):
                                r = jj * 2 + par
                                fw.op('pe', lambda e: e.matmul(ps[:, jj * 128:(jj + 1) * 128], kd[pr, kt * 128:(kt + 1) * 128],
                                                               qTs[r // 2][pr, qb * 128:(qb + 1) * 128], start=True, stop=True), [kd, qTs[r // 2]], [ps])
                            fw.op('act', lambda e: e.activation(Eall[:, i, par * 256:(par + 1) * 256], ps[:, 0:256], AF.Exp, scale=0.125), [ps], [Eall])
                        if msk is not None:
                            fw.op('dve', lambda e: e.tensor_tensor(Eall[:, i, :].rearrange("p (r q) -> p r q", r=4), Eall[:, i, :].rearrange("p (r q) -> p r q", r=4),
                                                                   msk[:, :].unsqueeze(1).to_broadcast([128, 4, 128]), ALU.mult), [Eall, msk], [Eall])
                    o = osb[qb % 2]
                    for r in range(4):
                        po = k.PS[4 + r % 2]
                        jr = (r % 2) * 2 + r // 2
                        for i, (kt, msk) in enumerate(kts):
                            fw.op('pe', lambda e: e.matmul(po[:, 0:65], Eall[:, i, jr * 128:(jr + 1) * 128], vt[:, kt, 0:65],
                                                           start=(i == 0), stop=(i == len(kts) - 1)), [Eall, vt], [po])
                        fw.op('dve', lambda e: e.tensor_tensor(st[:, r:r + 1], po[:, 64:65], snk[:, g * 4 + r:g * 4 + r + 1], ALU.add), [po, snk], [st])
                        fw.op('dve', lambda e: e.reciprocal(st[:, r:r + 1], st[:, r:r + 1]), [st], [st])
                        fw.op('dve', lambda e: e.tensor_scalar(o[:, r * 64:(r + 1) * 64], po[:, 0:64], st[:, r:r + 1], None, ALU.mult), [po, st], [o])
                    fw.dma('pool', k.AO, k.AO[qb * 128:(qb + 1) * 128, g * 256:(g + 1) * 256], o, o[:, :])


def outproj_ln_phase(k, layer, wout, hsrc, hdst, tiles):
    fw = k.fw
    with fw.scope():
        g_bc = fw.sb("o_g", [128, D]); b_bc = fw.sb("o_b", [128, D])
        load_bc_row(k, g_bc, k.W['post_ln_g'], k.W['post_ln_g'][layer, 0:1, :])
        load_bc_row(k, b_bc, k.W['post_ln_b'], k.W['post_ln_b'][layer, 0:1, :])
        kks = sorted(set(0 if t >= 2 else 1 for t in tiles))
        gate = {}
        for kk in kks:
            gate[kk] = fw.sb("o_gate%d" % kk, [128, D]); ada_bc(k, layer, 2, kk, gate[kk])
        wt = fw.sb("o_w", [128, 8, D])
        fw.dma('sp', wt, wt[:, 0:4, :], wout, wout[0, 0:512, :].rearrange("(c p) n -> p c n", p=128))
        fw.dma('act', wt, wt[:, 4:8, :], wout, wout[0, 512:1024, :].rearrange("(c p) n -> p c n", p=128))
        ao = [fw.sb("o_ao%d" % i, [128, D]) for i in range(2)]
        hb = [fw.sb("o_h%d" % i, [128, D]) for i in range(2)]
        aoT = [fw.sb("o_aoT%d" % i, [128, 8, 128]) for i in range(2)]
        tmp = fw.sb("o_tmp", [128, D]); st = fw.sb("o_st", [128, 4])
        for it, t in enumerate(tiles):
            a = ao[it % 2]; h = hb[it % 2]; aT = aoT[it % 2]
            fw.dma('sp', a, a[:, :], k.AO, k.AO[t * 128:(t + 1) * 128, :])
            sT, sap = src_rows(k, hsrc, t)
            fw.dma('act', h, h[:, :], sT, sap)
            transpose_tile(k, a, aT, lambda c0, c1: aT[:, c0:c1, :], k.PS[0], k.PS[1])
            kk = 0 if t >= 2 else 1
            for half in range(2):
                ps = k.PS[2 + half]
                for c in range(8):
                    fw.op('pe', lambda e: e.matmul(ps[:, :], aT[:, c, :], wt[:, c, half * 512:(half + 1) * 512], start=(c == 0), stop=(c == 7)), [aT, wt], [ps])
                fw.op('dve', lambda e: e.tensor_tensor(tmp[:, half * 512:(half + 1) * 512], ps[:, :], gate[kk][:, half * 512:(half + 1) * 512], ALU.mult), [ps, gate[kk]], [tmp])
            fw.op('dve', lambda e: e.scalar_tensor_tensor(h[:, :], h[:, :], ALPHA, tmp[:, :], ALU.mult, ALU.add), [h, tmp], [h])
            layer_norm_tile(k, h, g_bc, b_bc, tmp, st)
            fw.dma('pool', hdst, hdst[t * 128:(t + 1) * 128, :], h, h[:, :])


def moe_phase(k, layer, hin, hout, tiles, final_out):
    fw = k.fw
    ntl = len(tiles)
    with fw.scope():
        IDX = fw.sb("m_idx", [128, NT, 16], I32); GK = fw.sb("m_gk", [128, NT, 8])
        with fw.scope():
            sc = {}; sh = {}
            for kk in sorted(set(0 if t >= 2 else 1 for t in tiles)):
                sc[kk] = fw.sb("m_sc%d" % kk, [128, D]); sh[kk] = fw.sb("m_sh%d" % kk, [128, D])
                ada_bc(k, layer, 3, kk, sh[kk]); ada_bc(k, layer, 4, kk, sc[kk], plus_one=True)
            wr = fw.sb("m_wr", [128, 8, NE])
            fw.dma('sp', wr, wr[:, :, :], k.W['moe_router'], k.W['moe_router'][layer].rearrange("(c p) n -> p c n", p=128))
            bias = fw.sb("m_bias", [128, NE]); load_bc_row(k, bias, k.W['moe_bias'], k.W['moe_bias'][layer:layer + 1, :])
            base_e = fw.sb("m_base", [128, NE]); fw.dma('sp', base_e, base_e[:, :], k.C["base_e"], k.C["base_e"][:, :])
            tri = fw.sb("m_tri", [128, 128]); fw.dma('sp', tri, tri[:, :], k.C["tri_s"], k.C["tri_s"][:, :])
            wsi = fw.sb("m_wsi", [128, 8, 512]); wso = fw.sb("m_wso", [128, 2, D])
            fw.dma('act', wsi, wsi[:, :, :], k.W['moe_ws_in'], k.W['moe_ws_in'][layer].rearrange("(c p) n -> p c n", p=128))
            fw.dma('act', wso, wso[:, :, :], k.W['moe_ws_out'], k.W['moe_ws_out'][layer].rearrange("(c p) n -> p c n", p=128))
            cnt = fw.sb("m_cnt", [128, NE]); fw.op('pool', lambda e: e.memset(cnt[:, :], 0.0), [], [cnt])
            ub = [fw.sb("m_u%d" % i, [128, D]) for i in range(3)]
            uTb = [fw.sb("m_uT%d" % i, [128, 8, 128]) for i in range(2)]
            scs = fw.sb("m_scs", [128, NE]); sel = fw.sb("m_sel", [128, NE]); selm = fw.sb("m_selm", [128, NE])
            M = fw.sb("m_M", [128, NE]); G = fw.sb("m_G", [128, NE]); enc = fw.sb("m_enc", [128, NE])
            g8 = fw.sb("m_g8", [128, 8, 8]); gs = fw.sb("m_gs", [128, 8]); v8 = fw.sb("m_v8", [128, 8]); gm = fw.sb("m_gm", [128, 8]); pen = fw.sb("m_pen", [128, 8])
            e8 = fw.sb("m_e8", [128, 8]); oh = fw.sb("m_oh", [128, 8, NE]); st = fw.sb("m_st", [128, 4])
            hs = fw.sb("m_hs", [128, 4, 128]); hT = fw.sb("m_hT", [128, 2, 128]); shb = [fw.sb("m_shb%d" % i, [128, D]) for i in range(2)]
            for it, t in enumerate(tiles):
                u = ub[it % 3]; uT = uTb[it % 2]
                kk = 0 if t >= 2 else 1
                fw.dma('sp', u, u[:, :], hin, hin[t * 128:(t + 1) * 128, :])
                fw.op('dve', lambda e: e.tensor_tensor(u[:, :], u[:, :], sc[kk][:, :], ALU.mult), [u, sc[kk]], [u])
                fw.op('pool', lambda e: e.tensor_tensor(u[:, :], u[:, :], sh[kk][:, :], ALU.add), [u, sh[kk]], [u])
                transpose_tile(k, u, uT, lambda c0, c1: uT[:, c0:c1, :], k.PS[0], k.PS[1])
                ps = k.PS[2]
                for c in range(8):
                    fw.op('pe', lambda e: e.matmul(ps[:, 0:NE], uT[:, c, :], wr[:, c, :], start=(c == 0), stop=(c == 7)), [uT, wr], [ps])
                fw.op('act', lambda e: e.activation(scs[:, :], ps[:, 0:NE], AF.Sigmoid), [ps], [scs])
                fw.op('dve', lambda e: e.tensor_tensor(sel[:, :], scs[:, :], bias[:, :], ALU.add), [scs, bias], [sel])
                for gI in range(8):
                    fw.op('dve', lambda e: e.max(g8[:, gI, :], sel[:, gI * 32:(gI + 1) * 32]), [sel], [g8])
                fw.op('dve', lambda e: e.tensor_tensor(gs[:, :], g8[:, :, 0], g8[:, :, 1], ALU.add), [g8], [gs])
                fw.op('dve', lambda e: e.max(v8[:, :], gs[:, :]), [gs], [v8])
                fw.op('dve', lambda e: e.tensor_scalar(gm[:, :], gs[:, :], v8[:, 3:4], None, ALU.is_ge), [gs, v8], [gm])
                fw.op('dve', lambda e: e.tensor_scalar(pen[:, :], gm[:, :], -1.0, 1e30, ALU.add, ALU.mult), [gm], [pen])
                fw.op('dve', lambda e: e.tensor_tensor(selm[:, :].rearrange("p (g j) -> p g j", g=8), sel[:, :].rearrange("p (g j) -> p g j", g=8),
                                                       gm[:, :].unsqueeze(2).to_broadcast([128, 8, 32]), ALU.mult), [sel, gm], [selm])
                fw.op('dve', lambda e: e.tensor_tensor(selm[:, :].rearrange("p (g j) -> p g j", g=8), selm[:, :].rearrange("p (g j) -> p g j", g=8),
                                                       pen[:, :].unsqueeze(2).to_broadcast([128, 8, 32]), ALU.add), [selm, pen], [selm])
                fw.op('dve', lambda e: e.max(v8[:, :], selm[:, :]), [selm], [v8])
                fw.op('dve', lambda e: e.tensor_scalar(M[:, :], selm[:, :], v8[:, 7:8], None, ALU.is_ge), [selm, v8], [M])
                fw.op('dve', lambda e: e.tensor_tensor(G[:, :], M[:, :], scs[:, :], ALU.mult), [M, scs], [G])
                fw.op('dve', lambda e: e.tensor_reduce(st[:, 0:1], G[:, :], AX.X, ALU.add), [G], [st])
                fw.op('dve', lambda e: e.reciprocal(st[:, 0:1], st[:, 0:1]), [st], [st])
                fw.op('dve', lambda e: e.tensor_scalar(G[:, :], G[:, :], st[:, 0:1], 2.5, ALU.mult, ALU.mult), [G, st], [G])
                pp = k.PS[3]
                fw.op('pe', lambda e: e.matmul(pp[:, 0:NE], tri[:, :], M[:, :], start=True, stop=True), [tri, M], [pp])
                fw.op('pe', lambda e: e.matmul(pp[:, NE:2 * NE], k.ones[:, :], M[:, :], start=True, stop=True), [k.ones, M], [pp])
                fw.op('dve', lambda e: e.tensor_tensor(enc[:, :], pp[:, 0:NE], cnt[:, :], ALU.add), [pp, cnt], [enc])
                fw.op('dve', lambda e: e.tensor_tensor(cnt[:, :], cnt[:, :], pp[:, NE:2 * NE], ALU.add), [pp, cnt], [cnt])
                fw.op('dve', lambda e: e.tensor_scalar(selm[:, :], enc[:, :], float(CAP) - 0.5, None, ALU.is_lt), [enc], [selm])
                fw.op('dve', lambda e: e.tensor_tensor(G[:, :], G[:, :], selm[:, :], ALU.mult), [G, selm], [G])
                fw.op('dve', lambda e: e.tensor_tensor(enc[:, :], enc[:, :], base_e[:, :], ALU.add), [enc, base_e], [enc])
                fw.op('dve', lambda e: e.tensor_tensor(enc[:, :], enc[:, :], M[:, :], ALU.mult), [enc, M], [enc])
                fw.op('dve', lambda e: e.tensor_tensor(enc[:, :], enc[:, :], selm[:, :], ALU.mult), [enc, selm], [enc])
                fw.op('dve', lambda e: e.max(e8[:, :], enc[:, :]), [enc], [e8])
                fw.op('dve', lambda e: e.tensor_scalar(IDX[:, t, 0:8], e8[:, :], -1.0, None, ALU.add), [e8], [IDX])
                fw.op('dve', lambda e: e.tensor_scalar(IDX[:, t, 8:16], e8[:, :], -1.0 - HALF_ROWS, None, ALU.add), [e8], [IDX])
                fw.op('dve', lambda e: e.tensor_tensor(oh[:, :, :], enc[:, :].unsqueeze(1).to_broadcast([128, 8, NE]),
                                                       e8[:, :].unsqueeze(2).to_broadcast([128, 8, NE]), ALU.is_equal), [enc, e8], [oh])
                fw.op('pool', lambda e: e.tensor_tensor(oh[:, :, :], oh[:, :, :], G[:, :].unsqueeze(1).to_broadcast([128, 8, NE]), ALU.mult), [oh, G], [oh])
                fw.op('dve', lambda e: e.tensor_reduce(GK[:, t, :], oh[:, :, :], AX.X, ALU.add), [oh], [GK])
                for j in range(16):
                    xs = k.XS[j // 8]
                    fw.dma('pool', xs, None, u, None, fn=lambda e: e.indirect_dma_start(
                        out=xs[:, :], out_offset=bass.IndirectOffsetOnAxis(ap=IDX[:, t, j:j + 1], axis=0), in_=u[:, :], in_offset=None,
                        bounds_check=k.breg, oob_is_err=False), extra_reads=[IDX])
                for f in range(4):
                    ps2 = k.PS[4 + f % 2]
                    for c in range(8):
                        fw.op('pe', lambda e: e.matmul(ps2[:, 0:128], wsi[:, c, f * 128:(f + 1) * 128], uT[:, c, :], start=(c == 0), stop=(c == 7)), [wsi, uT], [ps2])
                    if f < 2:
                        fw.op('act', lambda e: e.activation(hs[:, f, :], ps2[:, 0:128], AF.Silu), [ps2], [hs])
                    else:
                        fw.op('dve', lambda e: e.tensor_tensor(hT[:, f - 2, :], ps2[:, 0:128], hs[:, f - 2, :], ALU.mult), [ps2, hs], [hT])
                so = shb[it % 2]
                for half in range(2):
                    ps3 = k.PS[6 + half]
                    for f in range(2):
                        fw.op('pe', lambda e: e.matmul(ps3[:, :], hT[:, f, :], wso[:, f, half * 512:(half + 1) * 512], start=(f == 0), stop=(f == 1)), [hT, wso], [ps3])
                    fw.op('act', lambda e: e.activation(so[:, half * 512:(half + 1) * 512], ps3[:, :], AF.Copy), [ps3], [so])
                fw.dma('act', k.SH, k.SH[t * 128:(t + 1) * 128, :], so, so[:, :])
        with fw.scope():
            NB = 3
            wi = [fw.sb("e_wi%d" % i, [128, 8, 512]) for i in range(NB)]
            wo = [fw.sb("e_wo%d" % i, [128, 2, D]) for i in range(NB)]
            xg = [fw.sb("e_x%d" % i, [128, D]) for i in range(2)]
            xT = [fw.sb("e_xT%d" % i, [128, 8, 128], BF16) for i in range(2)]
            hs = [fw.sb("e_hs%d" % i, [128, 2, 128]) for i in range(2)]
            hT = [fw.sb("e_hT%d" % i, [128, 2, 128], BF16) for i in range(2)]
            yb = [fw.sb("e_y%d" % i, [128, D]) for i in range(2)]
            wib = [fw.sb("e_wib%d" % i, [128, 8, 512], BF16) for i in range(2)]
            wob = [fw.sb("e_wob%d" % i, [128, 2, D], BF16) for i in range(2)]
            win = k.W['moe_w_in']; wout = k.W['moe_w_out']
            lw = 0 if k.ne_dbg else layer
            bi = 0
            for ex in range(k.ne_dbg or NE):
                w1f = wi[ex % NB]; w2f = wo[ex % NB]; w1 = wib[ex % 2]; w2 = wob[ex % 2]
                fw.dma('sp', w1f, w1f[:, 0:4, :], win, win[lw, ex, 0:512, :].rearrange("(c p) n -> p c n", p=128))
                fw.dma('act', w1f, w1f[:, 4:8, :], win, win[lw, ex, 512:1024, :].rearrange("(c p) n -> p c n", p=128))
                fw.dma('sp', w2f, w2f[:, :, :], wout, wout[lw, ex].rearrange("(c p) n -> p c n", p=128))
                fw.op('pool', lambda e: e.tensor_copy(w1[:, :, :], w1f[:, :, :]), [w1f], [w1])
                fw.op('act', lambda e: e.activation(w2[:, :, :], w2f[:, :, :], AF.Copy), [w2f], [w2])
                for blk in range(CAP // 128):
                    x = xg[bi % 2]; xt = xT[bi % 2]; h1 = hs[bi % 2]; h2 = hT[bi % 2]; y = yb[bi % 2]; bi += 1
                    hf = ex // (NE // 2); row0 = (ex % (NE // 2)) * CAP + blk * 128
                    fw.dma('act', x, x[:, :], k.XS[hf], k.XS[hf][row0:row0 + 128, :])
                    transpose_tile(k, x, xt, lambda c0, c1: xt[:, c0:c1, :], k.PS[0], k.PS[1], eng2='dve')
                    for f in range(4):
                        ps2 = k.PS[2 + f]
                        for c in range(8):
                            fw.op('pe', lambda e: e.matmul(ps2[:, 0:128], w1[:, c, f * 128:(f + 1) * 128], xt[:, c, :], start=(c == 0), stop=(c == 7)), [w1, xt], [ps2])
                        if f < 2:
                            fw.op('act', lambda e: e.activation(h1[:, f, :], ps2[:, 0:128], AF.Silu), [ps2], [h1])
                        else:
                            fw.op('dve', lambda e: e.tensor_tensor(h2[:, f - 2, :], ps2[:, 0:128], h1[:, f - 2, :], ALU.mult), [ps2, h1], [h2])
                    for half in range(2):
                        ps3 = k.PS[6 + half]
                        for f in range(2):
                            fw.op('pe', lambda e: e.matmul(ps3[:, :], h2[:, f, :], w2[:, f, half * 512:(half + 1) * 512], start=(f == 0), stop=(f == 1)), [h2, w2], [ps3])
                        if half == 0:
                            fw.op('act', lambda e: e.activation(y[:, 0:512], ps3[:, :], AF.Copy), [ps3], [y])
                        else:
                            fw.op('dve', lambda e: e.tensor_copy(y[:, 512:1024], ps3[:, :]), [ps3], [y])
                    fw.dma('pool', k.YS[hf], k.YS[hf][row0:row0 + 128, :], y, y[:, :])
        with fw.scope():
            g_bc = fw.sb("c_g", [128, D]); b_bc = fw.sb("c_b", [128, D])
            load_bc_row(k, g_bc, k.W['post_ln_g'], k.W['post_ln_g'][layer, 1:2, :])
            load_bc_row(k, b_bc, k.W['post_ln_b'], k.W['post_ln_b'][layer, 1:2, :])
            gate = {}
            for kk in sorted(set(0 if t >= 2 else 1 for t in tiles)):
                gate[kk] = fw.sb("c_gate%d" % kk, [128, D]); ada_bc(k, layer, 5, kk, gate[kk])
            R = [fw.sb("c_R%d" % i, [128, D]) for i in range(4)]
            for r in R:
                fw.op('pool', lambda e: e.memset(r[:, :], 0.0), [], [r])
            acc = [fw.sb("c_acc%d" % i, [128, D]) for i in range(2)]
            hb = [fw.sb("c_h%d" % i, [128, D]) for i in range(2)]
            tmp = fw.sb("c_tmp", [128, D]); st = fw.sb("c_st", [128, 4])
            ri = 0
            for it, t in enumerate(tiles):
                a = acc[it % 2]; h = hb[it % 2]
                kk = 0 if t >= 2 else 1
                fw.dma('sp', a, a[:, :], k.SH, k.SH[t * 128:(t + 1) * 128, :])
                fw.dma('act', h, h[:, :], hin, hin[t * 128:(t + 1) * 128, :])
                for j in range(8):
                    r = R[ri % 4]; ri += 1
                    for hf in range(2):
                        ys = k.YS[hf]
                        fw.dma('pool', r, None, ys, None, fn=lambda e: e.indirect_dma_start(
                            out=r[:, :], out_offset=None, in_=ys[:, :], in_offset=bass.IndirectOffsetOnAxis(ap=IDX[:, t, hf * 8 + j:hf * 8 + j + 1], axis=0),
                            bounds_check=k.breg, oob_is_err=False), extra_reads=[IDX])
                    eng = 'dve'
                    fw.op(eng, lambda e: e.scalar_tensor_tensor(a[:, :], r[:, :], GK[:, t, j:j + 1], a[:, :], ALU.mult, ALU.add), [r, GK, a], [a])
                fw.op('dve', lambda e: e.tensor_tensor(a[:, :], a[:, :], gate[kk][:, :], ALU.mult), [a, gate[kk]], [a])
                fw.op('dve', lambda e: e.scalar_tensor_tensor(h[:, :], h[:, :], ALPHA, a[:, :], ALU.mult, ALU.add), [h, a], [h])
                layer_norm_tile(k, h, g_bc, b_bc, tmp, st)
                if final_out is not None:
                    fw.dma('sp', final_out, final_out[(t - 2) * 128:(t - 1) * 128, :], h, h[:, :])
                else:
                    fw.dma('sp', hout, hout[t * 128:(t + 1) * 128, :], h, h[:, :])


def rwkv_phase(k):
    fw = k.fw; W = k.W
    NB = 9

    def dr(name, shape):
        return fw.dram("rk_" + name, shape)
    UD = dr("U", [NTOK, D]); UT = dr("UT", [D, NTOK]); NBT = dr("NBT", [D, NTOK])
    RT = dr("RT", [D, NTOK]); KT = dr("KT", [D, NTOK]); AT = dr("AT", [D, NTOK])
    AD = [dr("AD%d" % d, [D, NTOK]) for d in range(2)]
    BT = [dr("BT%d" % d, [D, NTOK]) for d in range(2)]
    KD = [dr("KD%d" % d, [D, NTOK]) for d in range(2)]
    VT = dr("VT", [NTOK, D]); SG = [dr("SG%d" % d, [NTOK, D]) for d in range(2)]
    GT = dr("GT", [NTOK, D]); BON = dr("BON", [NTOK, 16])

    def fmv(X):
        return X[:, :].rearrange("(c p) n -> p c n", p=128)

    with fw.scope():
        sc = [fw.sb("r_sc%d" % i, [128, D]) for i in range(2)]; sh = [fw.sb("r_sh%d" % i, [128, D]) for i in range(2)]
        for kk in range(2):
            ada_bc(k, 1, 0, kk, sh[kk]); ada_bc(k, 1, 1, kk, sc[kk], plus_one=True)
        hb = [fw.sb("r_h%d" % i, [128, D]) for i in range(3)]
        for t in range(NT):
            h = hb[t % 3]; kk = 0 if t >= 2 else 1
            fw.dma('sp', h, h[:, :], k.H2, k.H2[t * 128:(t + 1) * 128, :])
            fw.op('dve', lambda e: e.tensor_tensor(h[:, :], h[:, :], sc[kk][:, :], ALU.mult), [h, sc[kk]], [h])
            fw.op('pool', lambda e: e.tensor_tensor(h[:, :], h[:, :], sh[kk][:, :], ALU.add), [h, sh[kk]], [h])
            fw.dma('act', UD, UD[t * 128:(t + 1) * 128, :], h, h[:, :])
    with fw.scope():
        ub = [fw.sb("r_u%d" % i, [128, D]) for i in range(2)]; upb = [fw.sb("r_up%d" % i, [128, D]) for i in range(2)]
        unb = [fw.sb("r_un%d" % i, [128, D]) for i in range(2)]
        uTt = [fw.sb("r_uT%d" % i, [128, 8, 128]) for i in range(2)]; nTt = [fw.sb("r_nT%d" % i, [128, 8, 128]) for i in range(2)]
        for t in range(NT):
            u = ub[t % 2]; up = upb[t % 2]; un = unb[t % 2]; uT = uTt[t % 2]; nT = nTt[t % 2]
            r0 = t * 128
            fw.dma('sp', u, u[:, :], UD, UD[r0:r0 + 128, :])
            fw.op('pool', lambda e: e.memset(up[:, :], 0.0), [], [up])
            fw.op('pool', lambda e: e.memset(un[:, :], 0.0), [], [un])
            fw.dma('sp', up, up[1:128, :], UD, UD[r0:r0 + 127, :])
            if t not in (0, 2):
                fw.dma('sp', up, up[0:1, :], UD, UD[r0 - 1:r0, :])
            fw.dma('act', un, un[0:127, :], UD, UD[r0 + 1:r0 + 128, :])
            if t not in (1, NT - 1):
                fw.dma('act', un, un[127:128, :], UD, UD[r0 + 128:r0 + 129, :])
            fw.op('dve', lambda e: e.tensor_tensor(up[:, :], up[:, :], un[:, :], ALU.add), [up, un], [up])
            transpose_tile(k, u, uT, lambda c0, c1: uT[:, c0:c1, :], k.PS[0], k.PS[1])
            transpose_tile(k, up, nT, lambda c0, c1: nT[:, c0:c1, :], k.PS[2], k.PS[3])
            fw.dma('pool', UT, fmv(UT)[:, :, r0:r0 + 128], uT, uT[:, :, :])
            fw.dma('pool', NBT, fmv(NBT)[:, :, r0:r0 + 128], nT, nT[:, :, :])
    if k.stop_after == 'r_1b':
        return
    with fw.scope():
        mu = fw.sb("r_mu", [128, 6, 8]); om = fw.sb("r_om", [128, 6, 8]); hm = fw.sb("r_hm", [128, 6, 8])
        fw.dma('sp', mu, mu[:, :, :], k.mucol, k.mucol[:, :, :])
        fw.op('dve', lambda e: e.tensor_scalar(om[:, :, :], mu[:, :, :], -1.0, 1.0, ALU.mult, ALU.add), [mu], [om])
        fw.op('dve', lambda e: e.tensor_scalar(hm[:, :, :], mu[:, :, :], 0.5, None, ALU.mult), [mu], [hm])
        ublk = [fw.sb("r_ub%d" % i, [128, 8, 256]) for i in range(2)]; nblk = [fw.sb("r_nb%d" % i, [128, 8, 256]) for i in range(2)]
        xm = [fw.sb("r_xm%d" % i, [128, 8, 256]) for i in range(2)]
        stage = [fw.sb("r_stg%d" % i, [128, 8, 256]) for i in range(2)]
        cnt = [0]

        def get_xm(b, m):
            i = cnt[0] % 2; cnt[0] += 1
            ub_, nb_, x_ = ublk[i], nblk[i], xm[i]
            fw.dma('sp', ub_, ub_[:, :, :], UT, fmv(UT)[:, :, b * 256:(b + 1) * 256])
            fw.dma('sp', nb_, nb_[:, :, :], NBT, fmv(NBT)[:, :, b * 256:(b + 1) * 256])
            for c in range(8):
                fw.op('act', lambda e: e.activation(x_[:, c, :], ub_[:, c, :], AF.Copy, scale=om[:, m, c:c + 1]), [ub_, om], [x_])
                fw.op('dve', lambda e: e.scalar_tensor_tensor(x_[:, c, :], nb_[:, c, :], hm[:, m, c:c + 1], x_[:, c, :], ALU.mult, ALU.add), [nb_, hm, x_], [x_])
            return x_

        def tokview(st):
            return st[:, :, :].rearrange("p a n -> p (a n)").rearrange("p (a n) -> p a n", a=2)

        with fw.scope():
            wt = fw.sb("r_w", [128, 8, D])
            for job, (mi, dst) in enumerate(((0, RT), (2, KT), (3, VT))):
                fw.dma('sp', wt, wt[:, 0:4, :], W['rk_w_rkv'], W['rk_w_rkv'][0, job, 0:512, :].rearrange("(c p) n -> p c n", p=128))
                fw.dma('act', wt, wt[:, 4:8, :], W['rk_w_rkv'], W['rk_w_rkv'][0, job, 512:1024, :].rearrange("(c p) n -> p c n", p=128))
                for b in range(NB):
                    x_ = get_xm(b, mi); st = stage[b % 2]
                    if job < 2:
                        for oc in range(8):
                            ps = k.PS[oc % 4]
                            for c in range(8):
                                fw.op('pe', lambda e: e.matmul(ps[:, 0:256], wt[:, c, oc * 128:(oc + 1) * 128], x_[:, c, :], start=(c == 0), stop=(c == 7)), [wt, x_], [ps])
                            if oc % 2 == 0:
                                fw.op('dve', lambda e: e.tensor_copy(st[:, oc, :], ps[:, 0:256]), [ps], [st])
                            else:
                                fw.op('act', lambda e: e.activation(st[:, oc, :], ps[:, 0:256], AF.Copy), [ps], [st])
                        fw.dma('pool', dst, fmv(dst)[:, :, b * 256:(b + 1) * 256], st, st[:, :, :])
                    else:
                        tv = tokview(st)
                        for tt in range(2):
                            for half in range(2):
                                ps = k.PS[4 + (tt * 2 + half) % 4]
                                for c in range(8):
                                    fw.op('pe', lambda e: e.matmul(ps[:, :], x_[:, c, tt * 128:(tt + 1) * 128], wt[:, c, half * 512:(half + 1) * 512], start=(c == 0), stop=(c == 7)), [x_, wt], [ps])
                                if half == 0:
                                    fw.op('dve', lambda e: e.tensor_copy(tv[:, tt, 0:512], ps[:, :]), [ps], [st])
                                else:
                                    fw.op('act', lambda e: e.activation(tv[:, tt, 512:1024], ps[:, :], AF.Copy), [ps], [st])
                        fw.dma('pool', dst, dst[b * 256:(b + 1) * 256, :].rearrange("(a p) n -> p a n", p=128), st, tv)
        with fw.scope():
            d0bc = fw.sb("r_d0", [128, D]); w1 = fw.sb("r_w1", [128, 8, 64]); w2 = fw.sb("r_w2", [64, D]); t1 = fw.sb("r_t1", [64, 256])
            i0 = fw.sb("r_i0", [128, 2, 8]); fw.dma('sp', i0, i0[:, :, :], k.icl0col, k.icl0col[:, :, :])
            for d in range(2):
                load_bc_row(k, d0bc, W['rk_decay0'], W['rk_decay0'][0, d:d + 1, :])
                fw.dma('sp', w1, w1[:, :, :], W['rk_decay1'], W['rk_decay1'][0, d].rearrange("(c p) n -> p c n", p=128))
                fw.dma('sp', w2, w2[:, :], W['rk_decay2'], W['rk_decay2'][0, d])
                for b in range(NB):
                    x_ = get_xm(b, 1); st = stage[b % 2]; tv = tokview(st)
                    ps = k.PS[0]
                    for c in range(8):
                        fw.op('pe', lambda e: e.matmul(ps[0:64, 0:256], w1[:, c, :], x_[:, c, :], start=(c == 0), stop=(c == 7)), [w1, x_], [ps])
                    fw.op('act', lambda e: e.activation(t1[:, :], ps[0:64, 0:256], AF.Tanh), [ps], [t1])
                    for tt in range(2):
                        for half in range(2):
                            ps2 = k.PS[4 + (tt * 2 + half) % 4]
                            fw.op('pe', lambda e: e.matmul(ps2[:, :], t1[0:64, tt * 128:(tt + 1) * 128], w2[0:64, half * 512:(half + 1) * 512], start=True, stop=True), [t1, w2], [ps2])
                            fw.op('dve', lambda e: e.tensor_tensor(tv[:, tt, half * 512:(half + 1) * 512], ps2[:, :], d0bc[:, half * 512:(half + 1) * 512], ALU.add), [ps2, d0bc], [st])
                    fw.op('act', lambda e: e.activation(st[:, :, :], st[:, :, :], AF.Sigmoid), [st], [st])
                    fw.dma('pool', SG[d], SG[d][b * 256:(b + 1) * 256, :].rearrange("(a p) n -> p a n", p=128), st, tv)
            for d in range(2):
                fw.dma('sp', w1, w1[:, :, :], W['rk_iclr1'], W['rk_iclr1'][0, d].rearrange("(c p) n -> p c n", p=128))
                fw.dma('sp', w2, w2[:, :], W['rk_iclr2'], W['rk_iclr2'][0, d])
                for b in range(NB):
                    x_ = get_xm(b, 4); st = stage[b % 2]
                    ps = k.PS[0]
                    for c in range(8):
                        fw.op('pe', lambda e: e.matmul(ps[0:64, 0:256], w1[:, c, :], x_[:, c, :], start=(c == 0), stop=(c == 7)), [w1, x_], [ps])
                    fw.op('act', lambda e: e.activation(t1[:, :], ps[0:64, 0:256], AF.Copy), [ps], [t1])
                    for oc in range(8):
                        ps2 = k.PS[4 + oc % 4]
                        fw.op('pe', lambda e: e.matmul(ps2[:, 0:256], w2[0:64, oc * 128:(oc + 1) * 128], t1[0:64, :], start=True, stop=True), [t1, w2], [ps2])
                        fw.op('act', lambda e: e.activation(st[:, oc, :], ps2[:, 0:256], AF.Sigmoid, bias=i0[:, d, oc:oc + 1]), [ps2, i0], [st])
                    fw.dma('pool', AD[d], fmv(AD[d])[:, :, b * 256:(b + 1) * 256], st, st[:, :, :])
        with fw.scope():
            g1 = fw.sb("r_g1", [128, 8, 160]); g2a = fw.sb("r_g2a", [128, D]); g2b = fw.sb("r_g2b", [32, D])
            sa = fw.sb("r_sa", [128, 256]); sbb = fw.sb("r_sb", [32, 256])
            fw.dma('sp', g1, g1[:, :, :], W['rk_gate1'], W['rk_gate1'][0].rearrange("(c p) n -> p c n", p=128))
            fw.dma('sp', g2a, g2a[:, :], W['rk_gate2'], W['rk_gate2'][0, 0:128, :]); fw.dma('sp', g2b, g2b[:, :], W['rk_gate2'], W['rk_gate2'][0, 128:160, :])
            for b in range(NB):
                x_ = get_xm(b, 5); st = stage[b % 2]; tv = tokview(st)
                ps = k.PS[0]; psb = k.PS[1]
                for c in range(8):
                    fw.op('pe', lambda e: e.matmul(ps[:, 0:256], g1[:, c, 0:128], x_[:, c, :], start=(c == 0), stop=(c == 7)), [g1, x_], [ps])
                for c in range(8):
                    fw.op('pe', lambda e: e.matmul(psb[0:32, 0:256], g1[:, c, 128:160], x_[:, c, :], start=(c == 0), stop=(c == 7)), [g1, x_], [psb])
                fw.op('act', lambda e: e.activation(sa[:, :], ps[:, 0:256], AF.Sigmoid), [ps], [sa])
                fw.op('act', lambda e: e.activation(sbb[:, :], psb[0:32, 0:256], AF.Sigmoid), [psb], [sbb])
                for tt in range(2):
                    for half in range(2):
                        ps2 = k.PS[4 + (tt * 2 + half) % 4]
                        fw.op('pe', lambda e: e.matmul(ps2[:, :], sa[:, tt * 128:(tt + 1) * 128], g2a[:, half * 512:(half + 1) * 512], start=True, stop=False), [sa, g2a], [ps2])
                        fw.op('pe', lambda e: e.matmul(ps2[:, :], sbb[0:32, tt * 128:(tt + 1) * 128], g2b[0:32, half * 512:(half + 1) * 512], start=False, stop=True), [sbb, g2b], [ps2])
                        if half == 0:
                            fw.op('dve', lambda e: e.tensor_copy(tv[:, tt, 0:512], ps2[:, :]), [ps2], [st])
                        else:
                            fw.op('act', lambda e: e.activation(tv[:, tt, 512:1024], ps2[:, :], AF.Copy), [ps2], [st])
                fw.dma('pool', GT, GT[b * 256:(b + 1) * 256, :].rearrange("(a p) n -> p a n", p=128), st, tv)
    if k.stop_after == 'r_1c':
        return
    with fw.scope():
        kkc = fw.sb("r_kkc", [128, 8]); kac = fw.sb("r_kac", [128, 8]); rkc = fw.sb("r_rkc", [128, 8])
        fw.dma('sp', kkc, kkc[:, :], k.kkcol, k.kkcol[:, :]); fw.dma('sp', kac, kac[:, :], k.kacol, k.kacol[:, :]); fw.dma('sp', rkc, rkc[:, :], k.rkcol, k.rkcol[:, :])
        blk = fw.sb("r_blk", [128, 128]); fw.dma('sp', blk, blk[:, :], k.C["blk64"], k.C["blk64"][:, :])
        sel2 = fw.sb("r_sel2", [128, 2]); fw.dma('sp', sel2, sel2[:, :], k.C["sel2"], k.C["sel2"][:, :])
        tiny = fw.sb("r_tiny", [128, 1]); fw.op('pool', lambda e: e.memset(tiny[:, :], 0.0), [], [tiny])
        kt = fw.sb("r2_k", [128, 8, 256]); rt = fw.sb("r2_r", [128, 8, 256]); a0 = fw.sb("r2_a0", [128, 8, 256]); a1 = fw.sb("r2_a1", [128, 8, 256])
        kkt = fw.sb("r2_kk", [128, 8, 256]); tm = fw.sb("r2_tm", [128, 8, 256]); ks = fw.sb("r2_ks", [128, 8, 256]); bon = fw.sb("r2_bon", [128, 2, 16])
        for b in range(NB):
            cs = slice(b * 256, (b + 1) * 256)
            fw.dma('sp', kt, kt[:, :, :], KT, fmv(KT)[:, :, cs]); fw.dma('act', rt, rt[:, :, :], RT, fmv(RT)[:, :, cs])
            fw.dma('sp', a0, a0[:, :, :], AD[0], fmv(AD[0])[:, :, cs]); fw.dma('act', a1, a1[:, :, :], AD[1], fmv(AD[1])[:, :, cs])
            for c in range(8):
                fw.op('act', lambda e: e.activation(kkt[:, c, :], kt[:, c, :], AF.Copy, scale=kkc[:, c:c + 1]), [kt, kkc], [kkt])
            fw.op('pool', lambda e: e.tensor_tensor(tm[:, :, :], kkt[:, :, :], kkt[:, :, :], ALU.mult), [kkt], [tm])
            for c in range(8):
                ps = k.PS[c % 4]
                fw.op('pe', lambda e: e.matmul(ps[:, 0:256], blk[:, :], tm[:, c, :], start=True, stop=True), [blk, tm], [ps])
                fw.op('dve', lambda e: e.tensor_scalar(ks[:, c, :], ps[:, 0:256], 1e-24, None, ALU.max), [ps], [ks])
            fw.op('act', lambda e: e.activation(ks[:, :, :], ks[:, :, :], AF.Sqrt, bias=tiny[:, 0:1], scale=1.0), [ks, tiny], [ks])
            fw.op('dve', lambda e: e.reciprocal(ks[:, :, :], ks[:, :, :]), [ks], [ks])
            fw.op('dve', lambda e: e.tensor_tensor(kkt[:, :, :], kkt[:, :, :], ks[:, :, :], ALU.mult), [kkt, ks], [kkt])
            fw.op('act', lambda e: e.activation(tm[:, :, :], kkt[:, :, :], AF.Copy, scale=-1.0), [kkt], [tm])
            fw.dma('pool', AT, fmv(AT)[:, :, cs], tm, tm[:, :, :])
            fw.op('pool', lambda e: e.memset(ks[:, :, :], 0.0), [], [ks])
            for d, ad in enumerate((a0, a1)):
                fw.op('dve', lambda e: e.tensor_tensor(tm[:, :, :], kkt[:, :, :], ad[:, :, :], ALU.mult), [kkt, ad], [tm])
                fw.dma('pool', BT[d], fmv(BT[d])[:, :, cs], tm, tm[:, :, :])
                for c in range(8):
                    fw.op('dve', lambda e: e.tensor_scalar(ad[:, c, :], ad[:, c, :], -1.0, kac[:, c:c + 1], ALU.add, ALU.mult), [ad, kac], [ad])
                fw.op('dve', lambda e: e.tensor_scalar(ad[:, :, :], ad[:, :, :], 1.0, None, ALU.add), [ad], [ad])
                fw.op('dve', lambda e: e.tensor_tensor(ad[:, :, :], ad[:, :, :], kt[:, :, :], ALU.mult), [ad, kt], [ad])
                fw.dma('pool', KD[d], fmv(KD[d])[:, :, cs], ad, ad[:, :, :])
                fw.op('dve', lambda e: e.tensor_tensor(ks[:, :, :], ks[:, :, :], ad[:, :, :], ALU.add), [ks, ad], [ks])
            fw.op('dve', lambda e: e.tensor_tensor(ks[:, :, :], ks[:, :, :], rt[:, :, :], ALU.mult), [ks, rt], [ks])
            for c in range(8):
                fw.op('act', lambda e: e.activation(ks[:, c, :], ks[:, c, :], AF.Copy, scale=rkc[:, c:c + 1]), [ks, rkc], [ks])
            pb = k.PS[4 + b % 2]
            for tt in range(2):
                for c in range(8):
                    fw.op('pe', lambda e: e.matmul(pb[:, tt * 16 + 2 * c:tt * 16 + 2 * c + 2], ks[:, c, tt * 128:(tt + 1) * 128], sel2[:, :], start=True, stop=True), [ks, sel2], [pb])
            fw.op('dve', lambda e: e.tensor_copy(bon[:, :, :], pb[:, 0:32].rearrange("p (a n) -> p a n", a=2)), [pb], [bon])
            fw.dma('pool', BON, BON[b * 256:(b + 1) * 256, :].rearrange("(a p) n -> p a n", p=128), bon, bon[:, :, :])
    if k.stop_after == 'r_2':
        return
    with fw.scope():
        ident = k.ident
        triI = [fw.sb("s_triI%d" % d, [128, 128]) for d in range(2)]; triE = [fw.sb("s_triE%d" % d, [128, 128]) for d in range(2)]
        msi2 = [fw.sb("s_msi%d" % d, [128, 4, 128]) for d in range(2)]; mn = [fw.sb("s_mn%d" % d, [128, 128]) for d in range(2)]
        for d, sfx in enumerate(("f", "b")):
            fw.dma('sp', triI[d], triI[d][:, :], k.C["triI_" + sfx], k.C["triI_" + sfx][:, :])
            fw.dma('sp', triE[d], triE[d][:, :], k.C["triE_" + sfx], k.C["triE_" + sfx][:, :])
            fw.dma('sp', msi2[d], msi2[d][:, 0:2, :], k.C["msi_" + sfx], k.C["msi_" + sfx][:, :, :])
            fw.dma('sp', msi2[d], msi2[d][:, 2:4, :], k.C["msi_" + sfx], k.C["msi_" + sfx][:, :, :])
            fw.dma('sp', mn[d], mn[d][:, :], k.C["mn_" + sfx], k.C["mn_" + sfx][:, :])
        lng = fw.sb("s_lng", [128, D]); lnb = fw.sb("s_lnb", [128, D])
        load_bc_row(k, lng, W['rk_lnx'], W['rk_lnx'][0, 0:1, :]); load_bc_row(k, lnb, W['rk_lnx'], W['rk_lnx'][0, 1:2, :])
        epsx = fw.sb("s_eps", [128, 1]); fw.op('pool', lambda e: e.memset(epsx[:, :], 64e-5), [], [epsx])
        U4 = range(4)
        F = [[fw.sb("s_F%d_%d" % (u, i), [64, 4, 128]) for i in range(2)] for u in U4]
        Vb = [[fw.sb("s_V%d_%d" % (u, i), [128, 64]) for i in range(2)] for u in U4]
        Sb = [[fw.sb("s_S%d_%d" % (u, i), [128, 64]) for i in range(2)] for u in U4]
        PI4 = fw.sb("s_PI", [64, 4, 128]); PE4 = fw.sb("s_PE", [64, 4, 128]); PV4 = fw.sb("s_PV", [64, 4, 128])
        AR = [fw.sb("s_AR%d" % u, [64, 2, 128]) for u in U4]; BK = [fw.sb("s_BK%d" % u, [64, 2, 128]) for u in U4]
        TK = [fw.sb("s_TK%d" % u, [128, 4, 128]) for u in U4]; NM = [fw.sb("s_NM%d" % u, [128, 128]) for u in U4]
        SQ = [[fw.sb("s_SQ%d_%d" % (u, i), [128, 2, 128]) for i in range(2)] for u in U4]
        Wsb = [fw.sb("s_W%d" % u, [128, 64]) for u in U4]; W1s = [fw.sb("s_W1%d" % u, [128, 64]) for u in U4]
        BKt = [fw.sb("s_BKt%d" % u, [128, 2, 64]) for u in U4]
        Hst = [fw.sb("s_H%d" % u, [64, 64]) for u in U4]
        Yacc = [fw.sb("s_Y%d" % hh, [128, 16, 64]) for hh in range(2)]
        ytmp = fw.sb("s_ytmp", [128, 64]); y1s = fw.sb("s_y1s", [128, 64])
        fin = fw.sb("s_fin", [128, 16, 64]); fsq = fw.sb("s_fsq", [128, 16, 64]); fst = fw.sb("s_fst", [128, 16, 4])
        vfin = fw.sb("s_vfin", [128, 16, 64]); gfin = fw.sb("s_gfin", [128, 16, 64]); bfin = fw.sb("s_bfin", [128, 16, 16])
        order = [list(range(NT)), [1, 0] + list(range(NT - 1, 1, -1))]
        for pair in range(8):
            if k.stop_after == "r_scan1" and pair > 0:
                break
            for u in U4:
                fw.op('pool', lambda e: e.memset(Hst[u][:, :], 0.0), [], [Hst[u]])
            ywritten = [set(), set()]
            for it in range(NT if k.scan_stop is None else 1):
                units = [(u, pair * 2 + u // 2, u % 2, order[u % 2][it]) for u in U4]
                i2 = it % 2
                for (u, h, d, c) in units:
                    f = F[u][i2]; rs = slice(h * 64, (h + 1) * 64); cs = slice(c * 128, (c + 1) * 128)
                    fw.dma('sp', f, f[:, 0, :], RT, RT[rs, cs]); fw.dma('sp', f, f[:, 1, :], KD[d], KD[d][rs, cs])
                    fw.dma('sp', f, f[:, 2, :], AT, AT[rs, cs]); fw.dma('sp', f, f[:, 3, :], BT[d], BT[d][rs, cs])
                    fw.dma('act', Vb[u][i2], Vb[u][i2][:, :], VT, VT[cs, rs]); fw.dma('act', Sb[u][i2], Sb[u][i2][:, :], SG[d], SG[d][cs, rs])
                if k.scan_stop is not None and k.scan_stop < 1:
                    break
                for (u, h, d, c) in units:
                    fw.op('pe', lambda e: e.matmul(k.PS[3][0:64, u * 128:(u + 1) * 128], Sb[u][i2][:, :], triI[d][:, :], start=True, stop=True), [Sb[u][i2], triI[d]], [k.PS[3]])
                    fw.op('pe', lambda e: e.matmul(k.PS[4][0:64, u * 128:(u + 1) * 128], Sb[u][i2][:, :], triE[d][:, :], start=True, stop=True), [Sb[u][i2], triE[d]], [k.PS[4]])
                fw.op('act', lambda e: e.activation(PI4[:, :, :], k.PS[3][0:64, :].rearrange("p (a n) -> p a n", a=4), AF.Exp), [k.PS[3]], [PI4])
                fw.op('act', lambda e: e.activation(PV4[:, :, :], k.PS[3][0:64, :].rearrange("p (a n) -> p a n", a=4), AF.Exp, scale=-1.0), [k.PS[3]], [PV4])
                fw.op('act', lambda e: e.activation(PE4[:, :, :], k.PS[4][0:64, :].rearrange("p (a n) -> p a n", a=4), AF.Exp), [k.PS[4]], [PE4])
                if k.scan_stop is not None and k.scan_stop < 2:
                    break
                for (u, h, d, c) in units:
                    f = F[u][i2]
                    fw.op('dve', lambda e: e.tensor_tensor(AR[u][:, 0, :], f[:, 2, :], PE4[:, u, :], ALU.mult), [f, PE4], [AR[u]])
                    fw.op('pool', lambda e: e.tensor_tensor(AR[u][:, 1, :], f[:, 0, :], PI4[:, u, :], ALU.mult), [f, PI4], [AR[u]])
                    fw.op('dve', lambda e: e.tensor_tensor(BK[u][:, 0, :], f[:, 3, :], PV4[:, u, :], ALU.mult), [f, PV4], [BK[u]])
                    fw.op('pool', lambda e: e.tensor_tensor(BK[u][:, 1, :], f[:, 1, :], PV4[:, u, :], ALU.mult), [f, PV4], [BK[u]])
                if k.scan_stop is not None and k.scan_stop < 3:
                    break
                for (u, h, d, c) in units:
                    ar2 = AR[u][:, :, :].rearrange("p a n -> p (a n)")
                    fw.op('pe', lambda e: e.matmul(k.PS[1][:, u * 128:(u + 1) * 128], AR[u][:, 0, :], BK[u][:, 0, :], start=True, stop=True), [AR[u], BK[u]], [k.PS[1]])
                    fw.op('pe', lambda e: e.matmul(k.PS[0][:, 0:256], BK[u][:, 0, :], ar2, start=True, stop=True), [AR[u], BK[u]], [k.PS[0]])
                    fw.op('pe', lambda e: e.matmul(k.PS[0][:, 256:512], BK[u][:, 1, :], ar2, start=True, stop=True), [AR[u], BK[u]], [k.PS[0]])
                    fw.op('dve', lambda e: e.tensor_tensor(TK[u][:, :, :], k.PS[0][:, :].rearrange("p (a n) -> p a n", a=4), msi2[d][:, :, :], ALU.mult), [k.PS[0], msi2[d]], [TK[u]])
                if k.scan_stop is not None and k.scan_stop < 4:
                    break
                for (u, h, d, c) in units:
                    fw.op('dve', lambda e: e.tensor_tensor(NM[u][:, :], k.PS[1][:, u * 128:(u + 1) * 128], mn[d][:, :], ALU.mult), [k.PS[1], mn[d]], [NM[u]])
                if k.scan_stop is not None and k.scan_stop < 5:
                    break
                for (u, h, d, c) in units:
                    fw.op('pe', lambda e: e.matmul(k.PS[2][:, u * 64:(u + 1) * 64], AR[u][:, 0, :], Hst[u][:, :], start=True, stop=True), [AR[u], Hst[u]], [k.PS[2]])
                    fw.op('pe', lambda e: e.matmul(k.PS[5][:, u * 64:(u + 1) * 64], TK[u][:, 2, :], Vb[u][i2][:, :], start=True, stop=True), [TK[u], Vb[u][i2]], [k.PS[5]])
                for (u, h, d, c) in units:
                    fw.op('act', lambda e: e.activation(W1s[u][:, :], k.PS[2][:, u * 64:(u + 1) * 64], AF.Copy), [k.PS[2]], [W1s[u]])
                    fw.op('dve', lambda e: e.tensor_tensor(Wsb[u][:, :], k.PS[5][:, u * 64:(u + 1) * 64], W1s[u][:, :], ALU.add), [k.PS[5], W1s[u]], [Wsb[u]])
                if k.scan_stop is not None and k.scan_stop < 6:
                    break
                Ncur = {u: (NM[u][:, :], TK[u][:, 0, :], [NM[u], TK[u]]) for u in U4}
                for lvl in range(7):
                    for (u, h, d, c) in units:
                        N_, NT_, deps = Ncur[u]
                        fw.op('pe', lambda e: e.matmul(k.PS[5][:, u * 64:(u + 1) * 64], NT_, Wsb[u][:, :], start=True, stop=True), deps + [Wsb[u]], [k.PS[5]])
                        if lvl < 6:
                            fw.op('pe', lambda e: e.matmul(k.PS[6 + u // 2][:, (u % 2) * 256:(u % 2) * 256 + 128], NT_, N_, start=True, stop=True), deps, [k.PS[6 + u // 2]])
                            fw.op('pe', lambda e: e.matmul(k.PS[6 + u // 2][:, (u % 2) * 256 + 128:(u % 2) * 256 + 256], N_, NT_, start=True, stop=True), deps, [k.PS[6 + u // 2]])
                    for (u, h, d, c) in units:
                        fw.op('dve', lambda e: e.tensor_tensor(Wsb[u][:, :], Wsb[u][:, :], k.PS[5][:, u * 64:(u + 1) * 64], ALU.add), [k.PS[5], Wsb[u]], [Wsb[u]])
                        if lvl < 6:
                            sq = SQ[u][lvl % 2]
                            fw.op('act', lambda e: e.activation(sq[:, :, :], k.PS[6 + u // 2][:, (u % 2) * 256:(u % 2) * 256 + 256].rearrange("p (a n) -> p a n", a=2), AF.Copy), [k.PS[6 + u // 2]], [sq])
                            Ncur[u] = (sq[:, 0, :], sq[:, 1, :], [sq])
                if k.scan_stop is not None and k.scan_stop < 7:
                    break
                for (u, h, d, c) in units:
                    if c < 2:
                        continue
                    hh = u // 2
                    fw.op('pe', lambda e: e.matmul(k.PS[2][:, 256 + u * 64:256 + (u + 1) * 64], AR[u][:, 1, :], Hst[u][:, :], start=True, stop=True), [AR[u], Hst[u]], [k.PS[2]])
                    fw.op('pe', lambda e: e.matmul(k.PS[5][:, 256 + u * 64:256 + (u + 1) * 64], TK[u][:, 1, :], Wsb[u][:, :], start=True, stop=False), [TK[u], Wsb[u]], [k.PS[5]])
                    fw.op('pe', lambda e: e.matmul(k.PS[5][:, 256 + u * 64:256 + (u + 1) * 64], TK[u][:, 3, :], Vb[u][i2][:, :], start=False, stop=True), [TK[u], Vb[u][i2]], [k.PS[5]])
                    fw.op('act', lambda e: e.activation(y1s[:, :], k.PS[2][:, 256 + u * 64:256 + (u + 1) * 64], AF.Copy), [k.PS[2]], [y1s])
                    if c in ywritten[hh]:
                        fw.op('dve', lambda e: e.tensor_tensor(ytmp[:, :], k.PS[5][:, 256 + u * 64:256 + (u + 1) * 64], y1s[:, :], ALU.add), [k.PS[5], y1s], [ytmp])
                        fw.op('dve', lambda e: e.tensor_tensor(Yacc[hh][:, c - 2, :], Yacc[hh][:, c - 2, :], ytmp[:, :], ALU.add), [Yacc[hh], ytmp], [Yacc[hh]])
                    else:
                        fw.op('dve', lambda e: e.tensor_tensor(Yacc[hh][:, c - 2, :], k.PS[5][:, 256 + u * 64:256 + (u + 1) * 64], y1s[:, :], ALU.add), [k.PS[5], y1s], [Yacc[hh]])
                        ywritten[hh].add(c)
                if k.scan_stop is not None and k.scan_stop < 8:
                    break
                for (u, h, d, c) in units:
                    fw.op('pe', lambda e: e.matmul(k.PS[1][:, u * 128:u * 128 + 64], BK[u][:, 0, :], ident[0:64, 0:64], start=True, stop=True), [BK[u], ident], [k.PS[1]])
                    fw.op('pe', lambda e: e.matmul(k.PS[1][:, u * 128 + 64:u * 128 + 128], BK[u][:, 1, :], ident[0:64, 0:64], start=True, stop=True), [BK[u], ident], [k.PS[1]])
                for (u, h, d, c) in units:
                    fw.op('act', lambda e: e.activation(BKt[u][:, :, :], k.PS[1][:, u * 128:(u + 1) * 128].rearrange("p (a n) -> p a n", a=2), AF.Copy), [k.PS[1]], [BKt[u]])
                for (u, h, d, c) in units:
                    fw.op('pe', lambda e: e.matmul(k.PS[4][0:64, u * 64:(u + 1) * 64], BKt[u][:, 0, :], Wsb[u][:, :], start=True, stop=False), [BKt[u], Wsb[u]], [k.PS[4]])
                    fw.op('pe', lambda e: e.matmul(k.PS[4][0:64, u * 64:(u + 1) * 64], BKt[u][:, 1, :], Vb[u][i2][:, :], start=False, stop=True), [BKt[u], Vb[u][i2]], [k.PS[4]])
                for (u, h, d, c) in units:
                    pc = PI4[:, u, 127:128] if d == 0 else PI4[:, u, 0:1]
                    fw.op('dve', lambda e: e.tensor_tensor(Hst[u][:, :], Hst[u][:, :], k.PS[4][0:64, u * 64:(u + 1) * 64], ALU.add), [k.PS[4], Hst[u]], [Hst[u]])
                    fw.op('dve', lambda e: e.tensor_scalar(Hst[u][:, :], Hst[u][:, :], pc, None, ALU.mult), [Hst[u], PI4], [Hst[u]])
            if k.scan_stop is not None:
                break
            for hh in range(2):
                h = pair * 2 + hh; rs = slice(h * 64, (h + 1) * 64)
                Y = Yacc[hh]
                fw.dma('sp', vfin, vfin[:, :, :], VT, VT[256:NTOK, rs].rearrange("(a p) n -> p a n", p=128))
                fw.dma('act', gfin, gfin[:, :, :], GT, GT[256:NTOK, rs].rearrange("(a p) n -> p a n", p=128))
                fw.dma('sp', bfin, bfin[:, :, :], BON, BON[256:NTOK, :].rearrange("(a p) n -> p a n", p=128))
                fw.op('dve', lambda e: e.tensor_reduce(fst[:, :, 0], Y[:, :, :], AX.X, ALU.add), [Y], [fst])
                fw.op('dve', lambda e: e.tensor_scalar(fst[:, :, 0], fst[:, :, 0], -1.0 / 64, None, ALU.mult), [fst], [fst])
                fw.op('dve', lambda e: e.tensor_tensor(fin[:, :, :], Y[:, :, :], fst[:, :, 0:1].to_broadcast([128, 16, 64]), ALU.add), [Y, fst], [fin])
                fw.op('pool', lambda e: e.tensor_tensor(fsq[:, :, :], fin[:, :, :], fin[:, :, :], ALU.mult), [fin], [fsq])
                fw.op('dve', lambda e: e.tensor_reduce(fst[:, :, 1], fsq[:, :, :], AX.X, ALU.add), [fsq], [fst])
                fw.op('act', lambda e: e.activation(fst[:, :, 2], fst[:, :, 1], AF.Sqrt, bias=epsx[:, 0:1], scale=1.0 / 64), [fst, epsx], [fst])
                fw.op('dve', lambda e: e.reciprocal(fst[:, :, 2], fst[:, :, 2]), [fst], [fst])
                fw.op('dve', lambda e: e.tensor_tensor(fin[:, :, :], fin[:, :, :], fst[:, :, 2:3].to_broadcast([128, 16, 64]), ALU.mult), [fin, fst], [fin])
                fw.op('dve', lambda e: e.tensor_tensor(fin[:, :, :], fin[:, :, :], lng[:, rs].unsqueeze(1).to_broadcast([128, 16, 64]), ALU.mult), [fin, lng], [fin])
                fw.op('dve', lambda e: e.tensor_tensor(fin[:, :, :], fin[:, :, :], lnb[:, rs].unsqueeze(1).to_broadcast([128, 16, 64]), ALU.add), [fin, lnb], [fin])
                fw.op('dve', lambda e: e.tensor_tensor(vfin[:, :, :], vfin[:, :, :], bfin[:, :, h:h + 1].to_broadcast([128, 16, 64]), ALU.mult), [vfin, bfin], [vfin])
                fw.op('dve', lambda e: e.tensor_tensor(fin[:, :, :], fin[:, :, :], vfin[:, :, :], ALU.add), [fin, vfin], [fin])
                fw.op('dve', lambda e: e.tensor_tensor(fin[:, :, :], fin[:, :, :], gfin[:, :, :], ALU.mult), [fin, gfin], [fin])
                fw.dma('pool', k.AO, k.AO[256:NTOK, rs].rearrange("(a p) n -> p a n", p=128), fin, fin[:, :, :])


_NC_CACHE = {}


def make_in_maps(inputs, used=None):
    consts = host_consts()
    maps = []
    shared = {n: np.ascontiguousarray(np.asarray(inputs[n], dtype=np.float32)) for n in W_SHAPES}
    def col(v):
        return np.ascontiguousarray(np.asarray(v, np.float32).reshape(8, 128).T)
    shared["mucol"] = np.ascontiguousarray(np.stack([col(shared["rk_mu"][0, m]) for m in range(6)], 1))
    shared["icl0col"] = np.ascontiguousarray(np.stack([col(shared["rk_iclr0"][0, d]) for d in range(2)], 1))
    shared["kkcol"] = col(shared["rk_k_k"][0]); shared["kacol"] = col(shared["rk_k_a"][0]); shared["rkcol"] = col(shared["rk_r_k"][0].reshape(-1))
    shared["sinkb"] = np.ascontiguousarray(np.broadcast_to(shared["att_sink"].reshape(1, 8), (128, 8)))
    for n, v in consts.items():
        shared["c_" + n] = np.ascontiguousarray(v.astype(np.float32))
    x = np.asarray(inputs['x'], np.float32); c = np.asarray(inputs['c'], np.float32)
    ctx = np.asarray(inputs['ctx'], np.float32); c_ctx = np.asarray(inputs['c_ctx'], np.float32)
    for b in range(8):
        m = dict(shared)
        m["x"] = np.ascontiguousarray(x[b]); m["ctx"] = np.ascontiguousarray(ctx[b])
        cc = np.stack([c[b], c_ctx], -1).reshape(8, 128, 2).transpose(1, 0, 2)
        m["cc"] = np.ascontiguousarray(cc)
        if used is not None:
            m = {n: m[n] for n in used}
        maps.append(m)
    return maps


def kernel(**inputs):
    if "nc" not in _NC_CACHE:
        _NC_CACHE["nc"] = build()
    nc = _NC_CACHE["nc"]
    res = run_bass_kernel_spmd(nc, make_in_maps(inputs, nc._used()), core_ids=list(range(8)))
    return np.stack([np.asarray(r["out"]) for r in res.results], 0).astype(np.float32)
```

```python
import math
import numpy as np
import concourse.bass as bass
import concourse.mybir as mybir
from concourse.bass_utils import run_bass_kernel_spmd
from contextlib import ExitStack, contextmanager

F32 = mybir.dt.float32
I32 = mybir.dt.int32
BF16 = mybir.dt.bfloat16
ALU = mybir.AluOpType
AF = mybir.ActivationFunctionType
AX = mybir.AxisListType

D = 1024
SEQ = 2048
LCTX = 256
NTOK = SEQ + LCTX
NT = NTOK // 128
NE = 256
CAP = 256
HALF_ROWS = (NE // 2) * CAP
ALPHA = 4 ** 0.25
LN_EPS = 1e-5
C0 = -math.exp(-0.5)


class T:
    def __init__(self, fw, t, name):
        self.fw = fw; self.t = t; self.name = name
        self.w = {}; self.r = {}
        self.dsem = None; self.dcnt = 0

    def __getitem__(self, k):
        return self.t[k]


class FW:
    ENG = ('pe', 'dve', 'act', 'pool', 'sp')

    def __init__(self, nc):
        self.nc = nc
        self.eng = {'pe': nc.tensor, 'dve': nc.vector, 'act': nc.scalar, 'pool': nc.gpsimd, 'sp': nc.sync}
        self.root = ExitStack()
        self.stacks = [self.root]
        self.scopeT = [[]]
        self.free_dsems = []
        self.sem = {}; self.cnt = {}; self.seen = {}
        self.nsem = 0
        for e in self.ENG:
            self._new_prog(e)
        self.bar_sem = self._alloc_sem("BAR"); self.bar_cnt = 0
        self.ninst = 0

    def _alloc_sem(self, name):
        self.nsem += 1
        return self.root.enter_context(self.nc.semaphore(name + "_%d" % self.nsem))

    def _new_prog(self, e):
        self.sem[e] = self._alloc_sem("S_" + e); self.cnt[e] = 0
        self.seen[e] = {}

    def _reg(self, b):
        self.scopeT[-1].append(b)
        return b

    def sb(self, name, shape, dtype=F32):
        self.uid = getattr(self, "uid", 0) + 1
        name = "%s_u%d" % (name, self.uid)
        t = self.stacks[-1].enter_context(self.nc.sbuf_tensor(name, list(shape), dtype))
        return self._reg(T(self, t, name))

    def ps(self, name, shape, dtype=F32):
        t = self.stacks[-1].enter_context(self.nc.psum_tensor(name, list(shape), dtype))
        return self._reg(T(self, t, name))

    def dram(self, name, shape, dtype=F32, kind="Internal"):
        t = self.nc.dram_tensor(name, list(shape), dtype, kind=kind)
        return self._reg(T(self, t.ap(), name))

    def _dsem(self, b):
        if b.dsem is None:
            if self.free_dsems:
                b.dsem, b.dcnt = self.free_dsems.pop()
            else:
                b.dsem = self._alloc_sem("D"); b.dcnt = 0
        return b.dsem

    def _waits(self, e, reads, writes):
        own = self.sem[e]
        waits = {}

        def merge(d, skip_own):
            for s, v in d.items():
                if skip_own and s is own:
                    continue
                k = id(s)
                if k not in waits or waits[k][1] < v:
                    waits[k] = (s, v)
        for b in reads:
            merge(b.w, False)
        for b in writes:
            merge(b.w, True)
            merge(b.r, True)
        seen = self.seen[e]
        for k, (s, v) in waits.items():
            if seen.get(k, 0) >= v:
                continue
            self.eng[e].wait_ge(s, v)
            seen[k] = v

    def op(self, e, fn, reads=(), writes=()):
        self._waits(e, reads, writes)
        inst = fn(self.eng[e])
        self.cnt[e] += 1; self.ninst += 1
        c = self.cnt[e]; s = self.sem[e]
        inst.then_inc(s, 1)
        for b in writes:
            b.w[s] = c; b.r = {}
        for b in reads:
            if b not in writes:
                b.r[s] = c
        return inst

    def dma(self, q, out_b, out_ap, in_b, in_ap, fn=None, extra_reads=()):
        self._waits(q, [in_b] + list(extra_reads), [out_b])
        s = self._dsem(out_b)
        if fn is None:
            inst = self.eng[q].dma_start(out=out_ap, in_=in_ap)
        else:
            inst = fn(self.eng[q])
        self.ninst += 1
        out_b.dcnt += 16
        inst.then_inc(s, 16)
        out_b.w[s] = out_b.dcnt
        out_b.r = {}
        in_b.r[s] = out_b.dcnt
        for b in extra_reads:
            b.r[s] = out_b.dcnt
        return inst

    def all_T(self):
        for lst in self.scopeT:
            for b in lst:
                yield b

    def barrier(self):
        sp = self.eng['sp']
        seen = self.seen['sp']
        for e in self.ENG:
            if e != 'sp' and self.cnt[e] > 0:
                sp.wait_ge(self.sem[e], self.cnt[e])
        for b in self.all_T():
            if b.dsem is not None and b.dcnt > 0 and seen.get(id(b.dsem), 0) < b.dcnt:
                sp.wait_ge(b.dsem, b.dcnt); seen[id(b.dsem)] = b.dcnt
        self.bar_cnt += 1
        sp.sem_inc(self.bar_sem, 1)
        for e in self.ENG:
            if e != 'sp':
                self.eng[e].wait_ge(self.bar_sem, self.bar_cnt)
        for b in self.all_T():
            b.w = {}; b.r = {}
        for e in self.ENG:
            if e != 'sp' and self.cnt[e] > 12000:
                self._new_prog(e)

    @contextmanager
    def scope(self):
        st = ExitStack(); self.stacks.append(st); self.scopeT.append([])
        yield
        self.barrier()
        for b in self.scopeT.pop():
            if b.dsem is not None:
                self.free_dsems.append((b.dsem, b.dcnt)); b.dsem = None
        self.stacks.pop().close()

    def finish(self):
        self.barrier()
        self.root.close()


def host_consts():
    c = {}
    c["ident"] = np.eye(128, dtype=np.float32)
    s = np.arange(128)[:, None]; t = np.arange(128)[None, :]
    c["tri_s"] = (s < t).astype(np.float32)
    c["ones"] = np.ones((128, 128), np.float32)
    c["base_e"] = np.broadcast_to((np.arange(NE, dtype=np.float32) * CAP + 1.0)[None, :], (128, NE)).copy()
    rows = SEQ // 64
    row = np.repeat(np.arange(rows), 64).astype(np.float32)
    col = np.tile(np.arange(64), rows).astype(np.float32)
    inv = (np.float32(10000.0) ** (-np.arange(0, 32, 2, dtype=np.float32) / np.float32(32))).astype(np.float32)
    ar = row[:, None] * inv; ac = col[:, None] * inv
    ang = np.concatenate([ar, ar, ac, ac], -1).astype(np.float32)
    cosT = np.cos(ang).T.astype(np.float32); sinT = np.sin(ang).T.astype(np.float32)
    c["cos2"] = np.concatenate([cosT, cosT], 0).copy(); c["sin2"] = np.concatenate([sinT, sinT], 0).copy()
    rm = np.zeros((64, 64), np.float32)
    for d in range(64):
        if (d % 32) < 16:
            rm[d + 16, d] = -1.0
        else:
            rm[d - 16, d] = 1.0
    rm2 = np.zeros((128, 128), np.float32); rm2[:64, :64] = rm; rm2[64:, 64:] = rm
    c["rm2"] = rm2
    kk = np.arange(128)[:, None]; qq = np.arange(128)[None, :]
    c["mask_prev"] = (qq <= kk).astype(np.float32)
    c["mask_next"] = (kk <= qq).astype(np.float32)
    c["triI_f"] = (C0 * (s <= t)).astype(np.float32); c["triE_f"] = (C0 * (s < t)).astype(np.float32)
    c["triI_b"] = (C0 * (s >= t)).astype(np.float32); c["triE_b"] = (C0 * (s > t)).astype(np.float32)
    c["msi_f"] = np.stack([(s < t), (s <= t)], 1).astype(np.float32)
    c["msi_b"] = np.stack([(s > t), (s >= t)], 1).astype(np.float32)
    c["mn_f"] = (t < s).astype(np.float32)
    c["mn_b"] = (t > s).astype(np.float32)
    c["blk64"] = ((s // 64) == (t // 64)).astype(np.float32)
    sel2 = np.zeros((128, 2), np.float32); sel2[:64, 0] = 1.0; sel2[64:, 1] = 1.0
    c["sel2"] = sel2
    return c


CONST_SHAPES = {k: v.shape for k, v in host_consts().items()}

W_SHAPES = {
    'ada_w': (2, 1024, 6144), 'ada_b': (2, 6144), 'post_ln_g': (2, 2, 1024), 'post_ln_b': (2, 2, 1024),
    'att_w_in': (1, 1024, 2304), 'att_w_out': (1, 1024, 1024), 'att_sink': (1, 2, 4),
    'diff_lambda_vecs': (1, 4, 64), 'diff_subln_g': (1, 128),
    'rk_mu': (1, 6, 1024), 'rk_w_rkv': (1, 3, 1024, 1024), 'rk_w_out': (1, 1024, 1024),
    'rk_decay0': (1, 2, 1024), 'rk_decay1': (1, 2, 1024, 64), 'rk_decay2': (1, 2, 64, 1024),
    'rk_iclr0': (1, 2, 1024), 'rk_iclr1': (1, 2, 1024, 64), 'rk_iclr2': (1, 2, 64, 1024),
    'rk_gate1': (1, 1024, 160), 'rk_gate2': (1, 160, 1024), 'rk_k_k': (1, 1024), 'rk_k_a': (1, 1024),
    'rk_r_k': (1, 16, 64), 'rk_lnx': (1, 2, 1024),
    'moe_router': (2, 1024, 256), 'moe_bias': (2, 256), 'moe_w_in': (2, 256, 1024, 512),
    'moe_w_out': (2, 256, 256, 1024), 'moe_ws_in': (2, 1024, 512), 'moe_ws_out': (2, 256, 1024),
}


class K:
    pass


def build(stop_after=None, debug=False, ne_dbg=None):
    nc = bass.Bass("TRN2", target_bir_lowering=False)
    try:
        nc.allow_low_precision("bf16 expert matmuls with fp32 accumulation")
    except Exception:
        pass
    fw = FW(nc)
    k = K(); k.fw = fw; k.nc = nc; k.debug = debug; k.stop_after = stop_after; k.ne_dbg = ne_dbg
    import os as _os
    k.scan_stop = int(_os.environ['SCAN_STOP']) if 'SCAN_STOP' in _os.environ else None
    k.breg = nc.gpsimd.to_reg(HALF_ROWS - 1)

    def din(name, shape, dt=F32):
        return fw._reg(T(fw, nc.dram_tensor(name, list(shape), dt, kind="ExternalInput").ap(), name))
    k.sinkb = din("sinkb", [128, 8])
    k.mucol = din("mucol", [128, 6, 8]); k.icl0col = din("icl0col", [128, 2, 8])
    k.kkcol = din("kkcol", [128, 8]); k.kacol = din("kacol", [128, 8]); k.rkcol = din("rkcol", [128, 8])
    k.x = din("x", [SEQ, D]); k.ctx = din("ctx", [LCTX, D]); k.cc = din("cc", [128, 8, 2])
    class Lazy(dict):
        def __init__(self, shapes, prefix):
            super().__init__(); self.shapes = shapes; self.prefix = prefix
        def __missing__(self, n):
            v = din(self.prefix + n, self.shapes[n]); self[n] = v; return v
    wsh = dict(W_SHAPES)
    if ne_dbg:
        wsh['moe_w_in'] = (1, ne_dbg, 1024, 512); wsh['moe_w_out'] = (1, ne_dbg, 256, 1024)
    k.W = Lazy(wsh, ""); k.C = Lazy(CONST_SHAPES, "c_")
    nc._used = lambda: ["x", "ctx", "cc", "sinkb", "mucol", "icl0col", "kkcol", "kacol", "rkcol"] + (["H2in"] if k.h2in else []) + list(k.W.keys()) + ["c_" + n for n in k.C.keys()]
    k.out = fw._reg(T(fw, nc.dram_tensor("out", [SEQ, D], F32, kind="ExternalOutput").ap(), "out"))
    skind = "ExternalOutput" if debug else "Internal"
    k.H1 = fw.dram("H1", [NTOK, D], kind=skind)
    k.H2 = fw.dram("H2", [NTOK, D], kind=skind)
    k.AO = fw.dram("AO", [NTOK, D], kind=skind)
    k.XS = [fw.dram("XS%d" % i, [HALF_ROWS, D]) for i in range(2)]
    k.YS = [fw.dram("YS%d" % i, [HALF_ROWS, D]) for i in range(2)]
    k.SH = fw.dram("SH", [NTOK, D])
    k.PS = [fw.ps("psb%d" % i, [128, 512]) for i in range(8)]

    k.h2in = bool(stop_after and stop_after.startswith("r_"))
    if k.h2in:
        k.H2 = din("H2in", [NTOK, D])
    with fw.scope():
        k.ident = fw.sb("ident", [128, 128]); fw.dma('sp', k.ident, k.ident[:, :], k.C["ident"], k.C["ident"][:, :])
        k.ones = fw.sb("ones", [128, 128]); fw.dma('sp', k.ones, k.ones[:, :], k.C["ones"], k.C["ones"][:, :])
        k.epsln = fw.sb("epsln", [128, 1]); fw.op('pool', lambda e: e.memset(k.epsln[:, :], LN_EPS), [], [k.epsln])
        k.cact = fw.sb("cact", [128, 8, 2])
        fw.dma('sp', k.cact, k.cact[:, :, :], k.cc, k.cc[:, :, :])
        fw.op('act', lambda e: e.activation(k.cact[:, :, :], k.cact[:, :, :], AF.Silu), [k.cact], [k.cact])
        k.cbc = []
        for kk in range(2):
            t = fw.sb("cbc%d" % kk, [128, 8, 128])
            fw.op('dve', lambda e: e.tensor_copy(t[:, :, :], k.cact[:, :, kk:kk + 1].to_broadcast([128, 8, 128])), [k.cact], [t])
            k.cbc.append(t)
        if not k.h2in:
            zero_fill(k)
        print("nsem", fw.nsem, "ninst", fw.ninst, flush=True)
        if stop_after == "zero":
            fw.finish(); return nc
        if k.h2in:
            rwkv_phase(k)
            if stop_after != "r_all":
                fw.finish(); return nc
            outproj_ln_phase(k, 1, k.W['rk_w_out'], [k.H2, k.H2], k.H1, range(2, NT))
            fw.finish(); return nc
        attn_phase(k)
        if stop_after in ("a0", "a1", "b0", "b1", "b2", "w0", "w1", "w2"):
            fw.finish(); return nc
        print("nsem", fw.nsem, "ninst", fw.ninst, flush=True)
        if stop_after == "attn":
            fw.finish(); return nc
        outproj_ln_phase(k, 0, k.W['att_w_out'], [k.ctx, k.x], k.H1, range(NT))
        if stop_after == "mix0":
            fw.finish(); return nc
        moe_phase(k, 0, k.H1, k.H2, list(range(NT)), None)
        if stop_after == "moe0":
            fw.finish(); return nc
        rwkv_phase(k)
        if stop_after == "rwkv":
            fw.finish(); return nc
        outproj_ln_phase(k, 1, k.W['rk_w_out'], [k.H2, k.H2], k.H1, range(2, NT))
        if stop_after == "mix1":
            fw.finish(); return nc
        moe_phase(k, 1, k.H1, None, list(range(2, NT)), k.out)
    fw.finish()
    return nc


def src_rows(k, srcs, t):
    if srcs[0] is srcs[1]:
        return srcs[0], srcs[0][t * 128:(t + 1) * 128, :]
    if t < 2:
        return srcs[0], srcs[0][t * 128:(t + 1) * 128, :]
    return srcs[1], srcs[1][(t - 2) * 128:(t - 1) * 128, :]


def zero_fill(k):
    fw = k.fw
    with fw.scope():
        z = fw.sb("zf", [128, 4, D])
        fw.op('pool', lambda e: e.memset(z[:, :, :], 0.0), [], [z])
        for hf in range(2):
            for i in range(HALF_ROWS // 512):
                q = 'sp' if i % 2 == 0 else 'act'
                fw.dma(q, k.XS[hf], k.XS[hf][i * 512:(i + 1) * 512, :].rearrange("(a p) n -> p a n", p=128), z, z[:, :, :])
                if k.ne_dbg:
                    fw.dma(q, k.YS[hf], k.YS[hf][i * 512:(i + 1) * 512, :].rearrange("(a p) n -> p a n", p=128), z, z[:, :, :])


def ada_bc(k, layer, j, kk, dst, plus_one=False):
    fw = k.fw
    aw = k.W['ada_w']; ab = k.W['ada_b']
    with fw.scope():
        bb = fw.sb("ada_bb", [128, D])
        fw.dma('act', bb, bb[:, :], ab, ab[layer:layer + 1, j * D:(j + 1) * D].partition_broadcast(128))
        for half in range(2):
            wt = fw.sb("ada_wt%d" % half, [128, 8, 512])
            c0 = j * D + half * 512
            fw.dma('sp', wt, wt[:, :, :], aw, aw[layer, :, c0:c0 + 512].rearrange("(c p) n -> p c n", p=128))
            ps = k.PS[half]
            for c in range(8):
                fw.op('pe', lambda e: e.matmul(ps[:, :], k.cbc[kk][:, c, :], wt[:, c, :], start=(c == 0), stop=(c == 7)),
                      [k.cbc[kk], wt], [ps])
            fw.op('dve', lambda e: e.tensor_tensor(dst[:, half * 512:(half + 1) * 512], ps[:, :], bb[:, half * 512:(half + 1) * 512], ALU.add),
                  [ps, bb], [dst])
        if plus_one:
            fw.op('dve', lambda e: e.tensor_scalar(dst[:, :], dst[:, :], 1.0, None, ALU.add), [dst], [dst])


def load_bc_row(k, dst, src, src_ap, q='act'):
    k.fw.dma(q, dst, dst[:, :], src, src_ap.partition_broadcast(128))


def transpose_tile(k, src, dstT, dst_ap_fn, psa, psb, eng2='act'):
    fw = k.fw
    for half, ps in ((0, psa), (1, psb)):
        for c in range(4):
            cc = half * 4 + c
            fw.op('pe', lambda e: e.transpose(ps[:, c * 128:(c + 1) * 128], src[:, cc * 128:(cc + 1) * 128], k.ident[:, :]),
                  [src, k.ident], [ps])
        if half == 0:
            fw.op('dve', lambda e: e.tensor_copy(dst_ap_fn(0, 4), ps[:, :].rearrange("p (c n) -> p c n", c=4)), [ps], [dstT])
        else:
            fw.op(eng2, (lambda e: e.activation(dst_ap_fn(4, 8), ps[:, :].rearrange("p (c n) -> p c n", c=4), AF.Copy)) if eng2 == 'act'
                  else (lambda e: e.tensor_copy(dst_ap_fn(4, 8), ps[:, :].rearrange("p (c n) -> p c n", c=4))), [ps], [dstT])


def layer_norm_tile(k, z, g_bc, b_bc, tmp, st):
    fw = k.fw
    fw.op('dve', lambda e: e.tensor_reduce(st[:, 0:1], z[:, :], AX.X, ALU.add), [z], [st])
    fw.op('dve', lambda e: e.tensor_scalar(st[:, 1:2], st[:, 0:1], -1.0 / D, None, ALU.mult), [st], [st])
    fw.op('dve', lambda e: e.tensor_scalar(z[:, :], z[:, :], st[:, 1:2], None, ALU.add), [z, st], [z])
    fw.op('act', lambda e: e.activation(tmp[:, :], z[:, :], AF.Square, accum_out=st[:, 2:3]), [z], [tmp, st])
    fw.op('act', lambda e: e.activation(st[:, 3:4], st[:, 2:3], AF.Sqrt, bias=k.epsln[:, 0:1], scale=1.0 / D), [st, k.epsln], [st])
    fw.op('dve', lambda e: e.reciprocal(st[:, 3:4], st[:, 3:4]), [st], [st])
    fw.op('dve', lambda e: e.scalar_tensor_tensor(z[:, :], z[:, :], st[:, 3:4], g_bc[:, :], ALU.mult, ALU.mult), [z, st, g_bc], [z])
    fw.op('pool', lambda e: e.tensor_tensor(z[:, :], z[:, :], b_bc[:, :], ALU.add), [z, b_bc], [z])


def attn_phase(k):
    fw = k.fw
    W = k.W['att_w_in']
    with fw.scope():
        uT = fw.sb("uT", [128, 8, NTOK])
        with fw.scope():
            if k.stop_after == "a0":
                return
            sc = [fw.sb("a_sc%d" % i, [128, D]) for i in range(2)]
            sh = [fw.sb("a_sh%d" % i, [128, D]) for i in range(2)]
            for kk in range(2):
                ada_bc(k, 0, 0, kk, sh[kk]); ada_bc(k, 0, 1, kk, sc[kk], plus_one=True)
            hb = [fw.sb("a_h%d" % i, [128, D]) for i in range(2)]
            for t in range(NT):
                h = hb[t % 2]
                sT, sap = src_rows(k, [k.ctx, k.x], t)
                fw.dma('sp', h, h[:, :], sT, sap)
                kk = 0 if t >= 2 else 1
                fw.op('dve', lambda e: e.tensor_tensor(h[:, :], h[:, :], sc[kk][:, :], ALU.mult), [h, sc[kk]], [h])
                fw.op('pool', lambda e: e.tensor_tensor(h[:, :], h[:, :], sh[kk][:, :], ALU.add), [h, sh[kk]], [h])
                transpose_tile(k, h, uT, lambda a, b: uT[:, a:b, t * 128:(t + 1) * 128], k.PS[(t % 2) * 2], k.PS[(t % 2) * 2 + 1])
        if k.stop_after == "a1":
            for c in range(8):
                fw.dma('sp', k.AO, k.AO[c * 128:(c + 1) * 128, 0:NTOK // 4].rearrange("p (a n) -> p a n", a=1)[:, 0, :], uT, uT[:, c, 0:NTOK // 4])
            return
        cos2 = fw.sb("cos2", [128, SEQ]); sin2 = fw.sb("sin2", [128, SEQ]); rm2 = fw.sb("rm2", [128, 128])
        fw.dma('sp', cos2, cos2[:, :], k.C["cos2"], k.C["cos2"][:, :])
        fw.dma('act', sin2, sin2[:, :], k.C["sin2"], k.C["sin2"][:, :])
        fw.dma('sp', rm2, rm2[:, :], k.C["rm2"], k.C["rm2"][:, :])
        TB = [(0, 256)] + [(256 + i * 512, 512) for i in range(4)]

        def proj_fm(dst, wt, M=128):
            for bi, (t0, n) in enumerate(TB):
                ps = k.PS[4 + bi % 2]
                for c in range(8):
                    fw.op('pe', lambda e: e.matmul(ps[0:M, 0:n], wt[:, c, 0:M], uT[:, c, t0:t0 + n], start=(c == 0), stop=(c == 7)),
                          [wt, uT], [ps])
                if bi % 2 == 0:
                    fw.op('dve', lambda e: e.tensor_copy(dst[0:M, t0:t0 + n], ps[0:M, 0:n]), [ps], [dst])
                else:
                    fw.op('act', lambda e: e.activation(dst[0:M, t0:t0 + n], ps[0:M, 0:n], AF.Copy), [ps], [dst])

        def proj_tok(dst, wt, M, ncol):
            for g in range(0, NT, 3):
                ps = k.PS[6 + (g // 3) % 2]
                for i in range(3):
                    t = g + i
                    for c in range(8):
                        fw.op('pe', lambda e: e.matmul(ps[:, i * 128:i * 128 + M], uT[:, c, t * 128:(t + 1) * 128], wt[:, c, 0:M],
                                                       start=(c == 0), stop=(c == 7)), [uT, wt], [ps])
                fw.op('dve', lambda e: e.tensor_copy(dst[:, g:g + 3, 0:M], ps[:, 0:384].rearrange("p (a n) -> p a n", a=3)[:, :, 0:M]), [ps], [dst])

        def rope(dst, tmp):
            for bi in range(4):
                t0 = 256 + bi * 512
                ps = k.PS[4 + bi % 2]
                fw.op('pe', lambda e: e.matmul(ps[:, :], rm2[:, :], dst[:, t0:t0 + 512], start=True, stop=True), [rm2, dst], [ps])
                fw.op('dve', lambda e: e.tensor_tensor(tmp[:, :], ps[:, :], sin2[:, bi * 512:(bi + 1) * 512], ALU.mult), [ps, sin2], [tmp])
                fw.op('pool', lambda e: e.tensor_tensor(dst[:, t0:t0 + 512], dst[:, t0:t0 + 512], cos2[:, bi * 512:(bi + 1) * 512], ALU.mult), [dst, cos2], [dst])
                fw.op('dve', lambda e: e.tensor_tensor(dst[:, t0:t0 + 512], dst[:, t0:t0 + 512], tmp[:, :], ALU.add), [dst, tmp], [dst])

        def load_w(wt, col0, M, off=0, q='sp'):
            fw.dma(q, wt, wt[:, :, off:off + M], W, W[0, :, col0:col0 + M].rearrange("(c p) n -> p c n", p=128))

        with fw.scope():
          if k.stop_after not in ("w0", "w1", "w2"):
                lam_init = 0.8 - 0.6 * math.exp(0.0)
                lvf = fw.sb("lv", [128, 256]); lsm = fw.sb("lsm", [128, 4]); lam = fw.sb("lam", [128, 2])
                fw.dma('act', lvf, lvf[:, :], k.W['diff_lambda_vecs'], k.W['diff_lambda_vecs'][0].rearrange("(o b) c -> o (b c)", o=1).partition_broadcast(128))
                fw.op('dve', lambda e: e.tensor_tensor(lvf[:, 0:64], lvf[:, 0:64], lvf[:, 64:128], ALU.mult), [lvf], [lvf])
                fw.op('dve', lambda e: e.tensor_tensor(lvf[:, 128:192], lvf[:, 128:192], lvf[:, 192:256], ALU.mult), [lvf], [lvf])
                fw.op('dve', lambda e: e.tensor_reduce(lsm[:, :], lvf[:, :].rearrange("p (b c) -> p b c", b=4), AX.X, ALU.add), [lvf], [lsm])
                fw.op('act', lambda e: e.activation(lsm[:, :], lsm[:, :], AF.Exp), [lsm], [lsm])
                fw.op('dve', lambda e: e.tensor_tensor(lam[:, 0:1], lsm[:, 0:1], lsm[:, 2:3], ALU.subtract), [lsm], [lam])
                fw.op('dve', lambda e: e.tensor_scalar(lam[:, 1:2], lam[:, 0:1], lam_init, -1.0, ALU.add, ALU.mult), [lam], [lam])
                gsc = fw.sb("gsc", [128, 128])
                load_bc_row(k, gsc, k.W['diff_subln_g'], k.W['diff_subln_g'][0:1, :])
                fw.op('dve', lambda e: e.tensor_scalar(gsc[:, :], gsc[:, :], 1.0 - lam_init, None, ALU.mult), [gsc], [gsc])
                epsb = fw.sb("epsb", [128, 1]); fw.op('pool', lambda e: e.memset(epsb[:, :], 1e-5), [], [epsb])
                qT = fw.sb("b_qT", [128, NTOK]); kT = fw.sb("b_kT", [128, NTOK]); vt = fw.sb("b_v", [128, NT, 132])
                tmp = fw.sb("b_tmp", [128, 512])
                wts = [fw.sb("b_w%d" % i, [128, 8, 128]) for i in range(3)]
                Eb = [fw.sb("b_E%d" % i, [128, 512]) for i in range(2)]
                osb = [fw.sb("b_o%d" % i, [128, 128]) for i in range(2)]
                t0b = fw.sb("b_t0", [128, 128]); sq = fw.sb("b_sq", [128, 128]); st = fw.sb("b_st", [128, 8])
                fw.op('pool', lambda e: e.memset(vt[:, :, 128:129], 1.0), [], [vt])
                for h in range(4):
                    load_w(wts[0], 768 + h * 128, 128); load_w(wts[1], 1280 + h * 128, 128, q='act'); load_w(wts[2], 1792 + h * 128, 128)
                    proj_fm(qT, wts[0]); proj_fm(kT, wts[1]); proj_tok(vt, wts[2], 128, 132)
                    rope(qT, tmp); rope(kT, tmp)
                    if k.stop_after == "b0":
                        fw.dma('sp', k.AO, k.AO[0:128, :], qT, qT[:, 0:1024]); fw.dma('sp', k.AO, k.AO[128:256, :], kT, kT[:, 256:1280])
                        fw.dma('sp', k.AO, k.AO[256:384, 0:129 * 7].rearrange("p (a n) -> p a n", a=7), vt, vt[:, 0:7, 0:129])
                        break
                    for qb in range(NT):
                        if k.stop_after == "b1" and (h > 0 or qb > 3):
                            break
                        keys = range(NT) if qb >= 2 else range(2)
                        kgroups = [list(keys)[i:i + 4] for i in range(0, len(keys), 4)]
                        po = [k.PS[2], k.PS[3]]
                        it = 0
                        for m in range(2):
                            pr = slice(m * 64, (m + 1) * 64)
                            for gi, kg in enumerate(kgroups):
                                ps = k.PS[it % 2]; E = Eb[it % 2]; it += 1
                                n = len(kg)
                                for i, kt in enumerate(kg):
                                    fw.op('pe', lambda e: e.matmul(ps[:, i * 128:(i + 1) * 128], kT[pr, kt * 128:(kt + 1) * 128],
                                                                   qT[pr, qb * 128:(qb + 1) * 128], start=True, stop=True), [kT, qT], [ps])
                                fw.op('act', lambda e: e.activation(E[:, 0:n * 128], ps[:, 0:n * 128], AF.Exp, scale=0.125), [ps], [E])
                                for i, kt in enumerate(kg):
                                    fw.op('pe', lambda e: e.matmul(po[m][:, 0:129], E[:, i * 128:(i + 1) * 128], vt[:, kt, 0:129],
                                                                   start=(gi == 0 and i == 0), stop=(gi == len(kgroups) - 1 and i == n - 1)),
                                          [E, vt], [po[m]])
                        o = osb[qb % 2]
                        fw.op('dve', lambda e: e.reciprocal(st[:, 0:1], po[0][:, 128:129]), [po[0]], [st])
                        fw.op('dve', lambda e: e.reciprocal(st[:, 1:2], po[1][:, 128:129]), [po[1]], [st])
                        fw.op('dve', lambda e: e.tensor_tensor(st[:, 1:2], st[:, 1:2], lam[:, 1:2], ALU.mult), [st, lam], [st])
                        fw.op('dve', lambda e: e.tensor_scalar(t0b[:, :], po[0][:, 0:128], st[:, 0:1], None, ALU.mult), [po[0], st], [t0b])
                        fw.op('dve', lambda e: e.scalar_tensor_tensor(t0b[:, :], po[1][:, 0:128], st[:, 1:2], t0b[:, :], ALU.mult, ALU.add), [po[1], st, t0b], [t0b])
                        fw.op('act', lambda e: e.activation(sq[:, :], t0b[:, :], AF.Square, accum_out=st[:, 2:3]), [t0b], [sq, st])
                        fw.op('act', lambda e: e.activation(st[:, 3:4], st[:, 2:3], AF.Sqrt, bias=epsb[:, 0:1], scale=1.0 / 128), [st, epsb], [st])
                        fw.op('dve', lambda e: e.reciprocal(st[:, 3:4], st[:, 3:4]), [st], [st])
                        fw.op('dve', lambda e: e.scalar_tensor_tensor(o[:, :], t0b[:, :], st[:, 3:4], gsc[:, :], ALU.mult, ALU.mult), [t0b, st, gsc], [o])
                        fw.dma('pool', k.AO, k.AO[qb * 128:(qb + 1) * 128, 512 + h * 128:512 + (h + 1) * 128], o, o[:, :])
        if k.stop_after in ("b0", "b1", "b2"):
            return
        with fw.scope():
            snk = fw.sb("snk", [128, 8])
            fw.dma('sp', snk, snk[:, :], k.sinkb, k.sinkb[:, :])
            fw.op('act', lambda e: e.activation(snk[:, :], snk[:, :], AF.Exp), [snk], [snk])
            mp = fw.sb("mprev", [128, 128]); mn = fw.sb("mnext", [128, 128])
            fw.dma('sp', mp, mp[:, :], k.C["mask_prev"], k.C["mask_prev"][:, :])
            fw.dma('sp', mn, mn[:, :], k.C["mask_next"], k.C["mask_next"][:, :])
            qTs = [fw.sb("a_qT%d" % i, [128, NTOK]) for i in range(2)]
            kd = fw.sb("a_kd", [128, NTOK]); vt = fw.sb("a_v", [128, NT, 68])
            tmp = fw.sb("a_tmp", [128, 512])
            wq = [fw.sb("a_wq%d" % i, [128, 8, 128]) for i in range(2)]
            wk = fw.sb("a_wk", [128, 8, 128]); wv = fw.sb("a_wv", [128, 8, 64])
            Eall = fw.sb("a_E", [128, 5, 512])
            osb = [fw.sb("a_o%d" % i, [128, 256]) for i in range(2)]
            st = fw.sb("a_st", [128, 8])
            fw.op('pool', lambda e: e.memset(vt[:, :, 64:65], 1.0), [], [vt])
            for g in range(2):
                load_w(wq[0], g * 256, 128); load_w(wq[1], g * 256 + 128, 128, q='act')
                load_w(wk, 512 + g * 64, 64, off=0); load_w(wk, 512 + g * 64, 64, off=64, q='act')
                load_w(wv, 640 + g * 64, 64)
                proj_fm(qTs[0], wq[0]); proj_fm(qTs[1], wq[1]); proj_fm(kd, wk); proj_tok(vt, wv, 64, 68)
                rope(qTs[0], tmp); rope(qTs[1], tmp); rope(kd, tmp)
                if k.stop_after == "w0":
                    break
                for qb in range(NT):
                    if k.stop_after == "w1" and (g > 0 or qb > 4):
                        break
                    if qb < 2:
                        kts = [(0, None), (1, None)]
                    else:
                        kts = [(0, None), (1, None)]
                        if qb - 1 >= 2:
                            kts.append((qb - 1, mp))
                        kts.append((qb, None))
                        if qb + 1 < NT:
                            kts.append((qb + 1, mn))
                    for i, (kt, msk) in enumerate(kts):
                        for par in range(2):
                            ps = k.PS[(i % 2) * 2 + par]
                            pr = slice(par * 64, par * 64 + 64)
                            for jj in range(2):
                                r = jj * 2 + par
                                fw.op('pe', lambda e: e.matmul(ps[:, jj * 128:(jj + 1) * 128], kd[pr, kt * 128:(kt + 1) * 128],
                                                               qTs[r // 2][pr, qb * 128:(qb + 1) * 128], start=True, stop=True), [kd, qTs[r // 2]], [ps])
                            fw.op('act', lambda e: e.activation(Eall[:, i, par * 256:(par + 1) * 256], ps[:, 0:256], AF.Exp, scale=0.125), [ps], [Eall])
                        if msk is not None:
                            fw.op('dve', lambda e: e.tensor_tensor(Eall[:, i, :].rearrange("p (r q) -> p r q", r=4), Eall[:, i, :].rearrange("p (r q) -> p r q", r=4),
                                                                   msk[:, :].unsqueeze(1).to_broadcast([128, 4, 128]), ALU.mult), [Eall, msk], [Eall])
                    o = osb[qb % 2]
                    for r in range(4):
                        po = k.PS[4 + r % 2]
                        jr = (r % 2) * 2 + r // 2
                        for i, (kt, msk) in enumerate(kts):
                            fw.op('pe', lambda e: e.matmul(po[:, 0:65], Eall[:, i, jr * 128:(jr + 1) * 128], vt[:, kt, 0:65],
                                                           start=(i == 0), stop=(i == len(kts) - 1)), [Eall, vt], [po])
                        fw.op('dve', lambda e: e.tensor_tensor(st[:, r:r + 1], po[:, 64:65], snk[:, g * 4 + r:g * 4 + r + 1], ALU.add), [po, snk], [st])
                        fw.op('dve', lambda e: e.reciprocal(st[:, r:r + 1], st[:, r:r + 1]), [st], [st])
                        fw.op('dve', lambda e: e.tensor_scalar(o[:, r * 64:(r + 1) * 64], po[:, 0:64], st[:, r:r + 1], None, ALU.mult), [po, st], [o])
                    fw.dma('pool', k.AO, k.AO[qb * 128:(qb + 1) * 128, g * 256:(g + 1) * 256], o, o[:, :])


def outproj_ln_phase(k, layer, wout, hsrc, hdst, tiles):
    fw = k.fw
    with fw.scope():
        g_bc = fw.sb("o_g", [128, D]); b_bc = fw.sb("o_b", [128, D])
        load_bc_row(k, g_bc, k.W['post_ln_g'], k.W['post_ln_g'][layer, 0:1, :])
        load_bc_row(k, b_bc, k.W['post_ln_b'], k.W['post_ln_b'][layer, 0:1, :])
        kks = sorted(set(0 if t >= 2 else 1 for t in tiles))
        gate = {}
        for kk in kks:
            gate[kk] = fw.sb("o_gate%d" % kk, [128, D]); ada_bc(k, layer, 2, kk, gate[kk])
        wt = fw.sb("o_w", [128, 8, D])
        fw.dma('sp', wt, wt[:, 0:4, :], wout, wout[0, 0:512, :].rearrange("(c p) n -> p c n", p=128))
        fw.dma('act', wt, wt[:, 4:8, :], wout, wout[0, 512:1024, :].rearrange("(c p) n -> p c n", p=128))
        ao = [fw.sb("o_ao%d" % i, [128, D]) for i in range(2)]
        hb = [fw.sb("o_h%d" % i, [128, D]) for i in range(2)]
        aoT = [fw.sb("o_aoT%d" % i, [128, 8, 128]) for i in range(2)]
        tmp = fw.sb("o_tmp", [128, D]); st = fw.sb("o_st", [128, 4])
        for it, t in enumerate(tiles):
            a = ao[it % 2]; h = hb[it % 2]; aT = aoT[it % 2]
            fw.dma('sp', a, a[:, :], k.AO, k.AO[t * 128:(t + 1) * 128, :])
            sT, sap = src_rows(k, hsrc, t)
            fw.dma('act', h, h[:, :], sT, sap)
            transpose_tile(k, a, aT, lambda c0, c1: aT[:, c0:c1, :], k.PS[0], k.PS[1])
            kk = 0 if t >= 2 else 1
            for half in range(2):
                ps = k.PS[2 + half]
                for c in range(8):
                    fw.op('pe', lambda e: e.matmul(ps[:, :], aT[:, c, :], wt[:, c, half * 512:(half + 1) * 512], start=(c == 0), stop=(c == 7)), [aT, wt], [ps])
                fw.op('dve', lambda e: e.tensor_tensor(tmp[:, half * 512:(half + 1) * 512], ps[:, :], gate[kk][:, half * 512:(half + 1) * 512], ALU.mult), [ps, gate[kk]], [tmp])
            fw.op('dve', lambda e: e.scalar_tensor_tensor(h[:, :], h[:, :], ALPHA, tmp[:, :], ALU.mult, ALU.add), [h, tmp], [h])
            layer_norm_tile(k, h, g_bc, b_bc, tmp, st)
            fw.dma('pool', hdst, hdst[t * 128:(t + 1) * 128, :], h, h[:, :])


def moe_phase(k, layer, hin, hout, tiles, final_out):
    fw = k.fw
    ntl = len(tiles)
    with fw.scope():
        IDX = fw.sb("m_idx", [128, NT, 16], I32); GK = fw.sb("m_gk", [128, NT, 8])
        with fw.scope():
            sc = {}; sh = {}
            for kk in sorted(set(0 if t >= 2 else 1 for t in tiles)):
                sc[kk] = fw.sb("m_sc%d" % kk, [128, D]); sh[kk] = fw.sb("m_sh%d" % kk, [128, D])
                ada_bc(k, layer, 3, kk, sh[kk]); ada_bc(k, layer, 4, kk, sc[kk], plus_one=True)
            wr = fw.sb("m_wr", [128, 8, NE])
            fw.dma('sp', wr, wr[:, :, :], k.W['moe_router'], k.W['moe_router'][layer].rearrange("(c p) n -> p c n", p=128))
            bias = fw.sb("m_bias", [128, NE]); load_bc_row(k, bias, k.W['moe_bias'], k.W['moe_bias'][layer:layer + 1, :])
            base_e = fw.sb("m_base", [128, NE]); fw.dma('sp', base_e, base_e[:, :], k.C["base_e"], k.C["base_e"][:, :])
            tri = fw.sb("m_tri", [128, 128]); fw.dma('sp', tri, tri[:, :], k.C["tri_s"], k.C["tri_s"][:, :])
            wsi = fw.sb("m_wsi", [128, 8, 512]); wso = fw.sb("m_wso", [128, 2, D])
            fw.dma('act', wsi, wsi[:, :, :], k.W['moe_ws_in'], k.W['moe_ws_in'][layer].rearrange("(c p) n -> p c n", p=128))
            fw.dma('act', wso, wso[:, :, :], k.W['moe_ws_out'], k.W['moe_ws_out'][layer].rearrange("(c p) n -> p c n", p=128))
            cnt = fw.sb("m_cnt", [128, NE]); fw.op('pool', lambda e: e.memset(cnt[:, :], 0.0), [], [cnt])
            ub = [fw.sb("m_u%d" % i, [128, D]) for i in range(3)]
            uTb = [fw.sb("m_uT%d" % i, [128, 8, 128]) for i in range(2)]
            scs = fw.sb("m_scs", [128, NE]); sel = fw.sb("m_sel", [128, NE]); selm = fw.sb("m_selm", [128, NE])
            M = fw.sb("m_M", [128, NE]); G = fw.sb("m_G", [128, NE]); enc = fw.sb("m_enc", [128, NE])
            g8 = fw.sb("m_g8", [128, 8, 8]); gs = fw.sb("m_gs", [128, 8]); v8 = fw.sb("m_v8", [128, 8]); gm = fw.sb("m_gm", [128, 8]); pen = fw.sb("m_pen", [128, 8])
            e8 = fw.sb("m_e8", [128, 8]); oh = fw.sb("m_oh", [128, 8, NE]); st = fw.sb("m_st", [128, 4])
            hs = fw.sb("m_hs", [128, 4, 128]); hT = fw.sb("m_hT", [128, 2, 128]); shb = [fw.sb("m_shb%d" % i, [128, D]) for i in range(2)]
            for it, t in enumerate(tiles):
                u = ub[it % 3]; uT = uTb[it % 2]
                kk = 0 if t >= 2 else 1
                fw.dma('sp', u, u[:, :], hin, hin[t * 128:(t + 1) * 128, :])
                fw.op('dve', lambda e: e.tensor_tensor(u[:, :], u[:, :], sc[kk][:, :], ALU.mult), [u, sc[kk]], [u])
                fw.op('pool', lambda e: e.tensor_tensor(u[:, :], u[:, :], sh[kk][:, :], ALU.add), [u, sh[kk]], [u])
                transpose_tile(k, u, uT, lambda c0, c1: uT[:, c0:c1, :], k.PS[0], k.PS[1])
                ps = k.PS[2]
                for c in range(8):
                    fw.op('pe', lambda e: e.matmul(ps[:, 0:NE], uT[:, c, :], wr[:, c, :], start=(c == 0), stop=(c == 7)), [uT, wr], [ps])
                fw.op('act', lambda e: e.activation(scs[:, :], ps[:, 0:NE], AF.Sigmoid), [ps], [scs])
                fw.op('dve', lambda e: e.tensor_tensor(sel[:, :], scs[:, :], bias[:, :], ALU.add), [scs, bias], [sel])
                for gI in range(8):
                    fw.op('dve', lambda e: e.max(g8[:, gI, :], sel[:, gI * 32:(gI + 1) * 32]), [sel], [g8])
                fw.op('dve', lambda e: e.tensor_tensor(gs[:, :], g8[:, :, 0], g8[:, :, 1], ALU.add), [g8], [gs])
                fw.op('dve', lambda e: e.max(v8[:, :], gs[:, :]), [gs], [v8])
                fw.op('dve', lambda e: e.tensor_scalar(gm[:, :], gs[:, :], v8[:, 3:4], None, ALU.is_ge), [gs, v8], [gm])
                fw.op('dve', lambda e: e.tensor_scalar(pen[:, :], gm[:, :], -1.0, 1e30, ALU.add, ALU.mult), [gm], [pen])
                fw.op('dve', lambda e: e.tensor_tensor(selm[:, :].rearrange("p (g j) -> p g j", g=8), sel[:, :].rearrange("p (g j) -> p g j", g=8),
                                                       gm[:, :].unsqueeze(2).to_broadcast([128, 8, 32]), ALU.mult), [sel, gm], [selm])
                fw.op('dve', lambda e: e.tensor_tensor(selm[:, :].rearrange("p (g j) -> p g j", g=8), selm[:, :].rearrange("p (g j) -> p g j", g=8),
                                                       pen[:, :].unsqueeze(2).to_broadcast([128, 8, 32]), ALU.add), [selm, pen], [selm])
                fw.op('dve', lambda e: e.max(v8[:, :], selm[:, :]), [selm], [v8])
                fw.op('dve', lambda e: e.tensor_scalar(M[:, :], selm[:, :], v8[:, 7:8], None, ALU.is_ge), [selm, v8], [M])
                fw.op('dve', lambda e: e.tensor_tensor(G[:, :], M[:, :], scs[:, :], ALU.mult), [M, scs], [G])
                fw.op('dve', lambda e: e.tensor_reduce(st[:, 0:1], G[:, :], AX.X, ALU.add), [G], [st])
                fw.op('dve', lambda e: e.reciprocal(st[:, 0:1], st[:, 0:1]), [st], [st])
                fw.op('dve', lambda e: e.tensor_scalar(G[:, :], G[:, :], st[:, 0:1], 2.5, ALU.mult, ALU.mult), [G, st], [G])
                pp = k.PS[3]
                fw.op('pe', lambda e: e.matmul(pp[:, 0:NE], tri[:, :], M[:, :], start=True, stop=True), [tri, M], [pp])
                fw.op('pe', lambda e: e.matmul(pp[:, NE:2 * NE], k.ones[:, :], M[:, :], start=True, stop=True), [k.ones, M], [pp])
                fw.op('dve', lambda e: e.tensor_tensor(enc[:, :], pp[:, 0:NE], cnt[:, :], ALU.add), [pp, cnt], [enc])
                fw.op('dve', lambda e: e.tensor_tensor(cnt[:, :], cnt[:, :], pp[:, NE:2 * NE], ALU.add), [pp, cnt], [cnt])
                fw.op('dve', lambda e: e.tensor_scalar(selm[:, :], enc[:, :], float(CAP) - 0.5, None, ALU.is_lt), [enc], [selm])
                fw.op('dve', lambda e: e.tensor_tensor(G[:, :], G[:, :], selm[:, :], ALU.mult), [G, selm], [G])
                fw.op('dve', lambda e: e.tensor_tensor(enc[:, :], enc[:, :], base_e[:, :], ALU.add), [enc, base_e], [enc])
                fw.op('dve', lambda e: e.tensor_tensor(enc[:, :], enc[:, :], M[:, :], ALU.mult), [enc, M], [enc])
                fw.op('dve', lambda e: e.tensor_tensor(enc[:, :], enc[:, :], selm[:, :], ALU.mult), [enc, selm], [enc])
                fw.op('dve', lambda e: e.max(e8[:, :], enc[:, :]), [enc], [e8])
                fw.op('dve', lambda e: e.tensor_scalar(IDX[:, t, 0:8], e8[:, :], -1.0, None, ALU.add), [e8], [IDX])
                fw.op('dve', lambda e: e.tensor_scalar(IDX[:, t, 8:16], e8[:, :], -1.0 - HALF_ROWS, None, ALU.add), [e8], [IDX])
                fw.op('dve', lambda e: e.tensor_tensor(oh[:, :, :], enc[:, :].unsqueeze(1).to_broadcast([128, 8, NE]),
                                                       e8[:, :].unsqueeze(2).to_broadcast([128, 8, NE]), ALU.is_equal), [enc, e8], [oh])
                fw.op('pool', lambda e: e.tensor_tensor(oh[:, :, :], oh[:, :, :], G[:, :].unsqueeze(1).to_broadcast([128, 8, NE]), ALU.mult), [oh, G], [oh])
                fw.op('dve', lambda e: e.tensor_reduce(GK[:, t, :], oh[:, :, :], AX.X, ALU.add), [oh], [GK])
                for j in range(16):
                    xs = k.XS[j // 8]
                    fw.dma('pool', xs, None, u, None, fn=lambda e: e.indirect_dma_start(
                        out=xs[:, :], out_offset=bass.IndirectOffsetOnAxis(ap=IDX[:, t, j:j + 1], axis=0), in_=u[:, :], in_offset=None,
                        bounds_check=k.breg, oob_is_err=False), extra_reads=[IDX])
                for f in range(4):
                    ps2 = k.PS[4 + f % 2]
                    for c in range(8):
                        fw.op('pe', lambda e: e.matmul(ps2[:, 0:128], wsi[:, c, f * 128:(f + 1) * 128], uT[:, c, :], start=(c == 0), stop=(c == 7)), [wsi, uT], [ps2])
                    if f < 2:
                        fw.op('act', lambda e: e.activation(hs[:, f, :], ps2[:, 0:128], AF.Silu), [ps2], [hs])
                    else:
                        fw.op('dve', lambda e: e.tensor_tensor(hT[:, f - 2, :], ps2[:, 0:128], hs[:, f - 2, :], ALU.mult), [ps2, hs], [hT])
                so = shb[it % 2]
                for half in range(2):
                    ps3 = k.PS[6 + half]
                    for f in range(2):
                        fw.op('pe', lambda e: e.matmul(ps3[:, :], hT[:, f, :], wso[:, f, half * 512:(half + 1) * 512], start=(f == 0), stop=(f == 1)), [hT, wso], [ps3])
                    fw.op('act', lambda e: e.activation(so[:, half * 512:(half + 1) * 512], ps3[:, :], AF.Copy), [ps3], [so])
                fw.dma('act', k.SH, k.SH[t * 128:(t + 1) * 128, :], so, so[:, :])
        with fw.scope():
            NB = 3
            wi = [fw.sb("e_wi%d" % i, [128, 8, 512]) for i in range(NB)]
            wo = [fw.sb("e_wo%d" % i, [128, 2, D]) for i in range(NB)]
            xg = [fw.sb("e_x%d" % i, [128, D]) for i in range(2)]
            xT = [fw.sb("e_xT%d" % i, [128, 8, 128], BF16) for i in range(2)]
            hs = [fw.sb("e_hs%d" % i, [128, 2, 128]) for i in range(2)]
            hT = [fw.sb("e_hT%d" % i, [128, 2, 128], BF16) for i in range(2)]
            yb = [fw.sb("e_y%d" % i, [128, D]) for i in range(2)]
            hsl = [fw.sb("e_hsl%d" % i, [128, 256]) for i in range(2)]; hfl = [fw.sb("e_hfl%d" % i, [128, 256]) for i in range(2)]
            wib = [fw.sb("e_wib%d" % i, [128, 8, 512], BF16) for i in range(2)]
            wob = [fw.sb("e_wob%d" % i, [128, 2, D], BF16) for i in range(2)]
            win = k.W['moe_w_in']; wout = k.W['moe_w_out']
            lw = 0 if k.ne_dbg else layer
            bi = 0
            for ex in range(k.ne_dbg or NE):
                w1f = wi[ex % NB]; w2f = wo[ex % NB]; w1 = wib[ex % 2]; w2 = wob[ex % 2]
                fw.dma('sp', w1f, w1f[:, 0:4, :], win, win[lw, ex, 0:512, :].rearrange("(c p) n -> p c n", p=128))
                fw.dma('act', w1f, w1f[:, 4:8, :], win, win[lw, ex, 512:1024, :].rearrange("(c p) n -> p c n", p=128))
                fw.dma('sp', w2f, w2f[:, :, :], wout, wout[lw, ex].rearrange("(c p) n -> p c n", p=128))
                fw.op('pool', lambda e: e.tensor_copy(w1[:, :, :], w1f[:, :, :]), [w1f], [w1])
                fw.op('act', lambda e: e.activation(w2[:, :, :], w2f[:, :, :], AF.Copy), [w2f], [w2])
                for blk in range(CAP // 128):
                    x = xg[bi % 2]; xt = xT[bi % 2]; h1 = hs[bi % 2]; h2 = hT[bi % 2]; y = yb[bi % 2]; bi += 1
                    hf = ex // (NE // 2); row0 = (ex % (NE // 2)) * CAP + blk * 128
                    fw.dma('act', x, x[:, :], k.XS[hf], k.XS[hf][row0:row0 + 128, :])
                    transpose_tile(k, x, xt, lambda c0, c1: xt[:, c0:c1, :], k.PS[0], k.PS[1], eng2='dve')
                    ps2 = k.PS[2 + bi % 2]; pst = k.PS[4 + bi % 2]; hsil = hsl[bi % 2]; hf32 = hfl[bi % 2]
                    for c in range(8):
                        fw.op('pe', lambda e: e.matmul(ps2[:, :], xt[:, c, :], w1[:, c, :], start=(c == 0), stop=(c == 7)), [w1, xt], [ps2])
                    fw.op('act', lambda e: e.activation(hsil[:, :], ps2[:, 0:256], AF.Silu), [ps2], [hsil])
                    fw.op('dve', lambda e: e.tensor_tensor(hf32[:, :], ps2[:, 256:512], hsil[:, :], ALU.mult), [ps2, hsil], [hf32])
                    for f in range(2):
                        fw.op('pe', lambda e: e.transpose(pst[:, f * 128:(f + 1) * 128], hf32[:, f * 128:(f + 1) * 128], k.ident[:, :]), [hf32, k.ident], [pst])
                    fw.op('act', lambda e: e.activation(h2[:, :, :], pst[:, 0:256].rearrange("p (a n) -> p a n", a=2), AF.Copy), [pst], [h2])
                    for half in range(2):
                        ps3 = k.PS[6 + half]
                        for f in range(2):
                            fw.op('pe', lambda e: e.matmul(ps3[:, :], h2[:, f, :], w2[:, f, half * 512:(half + 1) * 512], start=(f == 0), stop=(f == 1)), [h2, w2], [ps3])
                        if half == 0:
                            fw.op('act', lambda e: e.activation(y[:, 0:512], ps3[:, :], AF.Copy), [ps3], [y])
                        else:
                            fw.op('dve', lambda e: e.tensor_copy(y[:, 512:1024], ps3[:, :]), [ps3], [y])
                    fw.dma('pool', k.YS[hf], k.YS[hf][row0:row0 + 128, :], y, y[:, :])
        with fw.scope():
            g_bc = fw.sb("c_g", [128, D]); b_bc = fw.sb("c_b", [128, D])
            load_bc_row(k, g_bc, k.W['post_ln_g'], k.W['post_ln_g'][layer, 1:2, :])
            load_bc_row(k, b_bc, k.W['post_ln_b'], k.W['post_ln_b'][layer, 1:2, :])
            gate = {}
            for kk in sorted(set(0 if t >= 2 else 1 for t in tiles)):
                gate[kk] = fw.sb("c_gate%d" % kk, [128, D]); ada_bc(k, layer, 5, kk, gate[kk])
            R = [fw.sb("c_R%d" % i, [128, D]) for i in range(4)]
            for r in R:
                fw.op('pool', lambda e: e.memset(r[:, :], 0.0), [], [r])
            acc = [fw.sb("c_acc%d" % i, [128, D]) for i in range(2)]
            hb = [fw.sb("c_h%d" % i, [128, D]) for i in range(2)]
            tmp = fw.sb("c_tmp", [128, D]); st = fw.sb("c_st", [128, 4])
            ri = 0
            for it, t in enumerate(tiles):
                a = acc[it % 2]; h = hb[it % 2]
                kk = 0 if t >= 2 else 1
                fw.dma('sp', a, a[:, :], k.SH, k.SH[t * 128:(t + 1) * 128, :])
                fw.dma('act', h, h[:, :], hin, hin[t * 128:(t + 1) * 128, :])
                for j in range(8):
                    r = R[ri % 4]; ri += 1
                    for hf in range(2):
                        ys = k.YS[hf]
                        fw.dma('pool', r, None, ys, None, fn=lambda e: e.indirect_dma_start(
                            out=r[:, :], out_offset=None, in_=ys[:, :], in_offset=bass.IndirectOffsetOnAxis(ap=IDX[:, t, hf * 8 + j:hf * 8 + j + 1], axis=0),
                            bounds_check=k.breg, oob_is_err=False), extra_reads=[IDX])
                    eng = 'dve'
                    fw.op(eng, lambda e: e.scalar_tensor_tensor(a[:, :], r[:, :], GK[:, t, j:j + 1], a[:, :], ALU.mult, ALU.add), [r, GK, a], [a])
                fw.op('dve', lambda e: e.tensor_tensor(a[:, :], a[:, :], gate[kk][:, :], ALU.mult), [a, gate[kk]], [a])
                fw.op('dve', lambda e: e.scalar_tensor_tensor(h[:, :], h[:, :], ALPHA, a[:, :], ALU.mult, ALU.add), [h, a], [h])
                layer_norm_tile(k, h, g_bc, b_bc, tmp, st)
                if final_out is not None:
                    fw.dma('sp', final_out, final_out[(t - 2) * 128:(t - 1) * 128, :], h, h[:, :])
                else:
                    fw.dma('sp', hout, hout[t * 128:(t + 1) * 128, :], h, h[:, :])


def rwkv_phase(k):
    fw = k.fw; W = k.W
    NB = 9

    def dr(name, shape):
        return fw.dram("rk_" + name, shape)
    UD = dr("U", [NTOK, D]); UT = dr("UT", [D, NTOK]); NBT = dr("NBT", [D, NTOK])
    RT = dr("RT", [D, NTOK]); KT = dr("KT", [D, NTOK]); AT = dr("AT", [D, NTOK])
    AD = [dr("AD%d" % d, [D, NTOK]) for d in range(2)]
    BT = [dr("BT%d" % d, [D, NTOK]) for d in range(2)]
    KD = [dr("KD%d" % d, [D, NTOK]) for d in range(2)]
    VT = dr("VT", [NTOK, D]); SG = [dr("SG%d" % d, [NTOK, D]) for d in range(2)]
    GT = dr("GT", [NTOK, D]); BON = dr("BON", [NTOK, 16])

    def fmv(X):
        return X[:, :].rearrange("(c p) n -> p c n", p=128)

    with fw.scope():
        sc = [fw.sb("r_sc%d" % i, [128, D]) for i in range(2)]; sh = [fw.sb("r_sh%d" % i, [128, D]) for i in range(2)]
        for kk in range(2):
            ada_bc(k, 1, 0, kk, sh[kk]); ada_bc(k, 1, 1, kk, sc[kk], plus_one=True)
        hb = [fw.sb("r_h%d" % i, [128, D]) for i in range(3)]
        for t in range(NT):
            h = hb[t % 3]; kk = 0 if t >= 2 else 1
            fw.dma('sp', h, h[:, :], k.H2, k.H2[t * 128:(t + 1) * 128, :])
            fw.op('dve', lambda e: e.tensor_tensor(h[:, :], h[:, :], sc[kk][:, :], ALU.mult), [h, sc[kk]], [h])
            fw.op('pool', lambda e: e.tensor_tensor(h[:, :], h[:, :], sh[kk][:, :], ALU.add), [h, sh[kk]], [h])
            fw.dma('act', UD, UD[t * 128:(t + 1) * 128, :], h, h[:, :])
    with fw.scope():
        ub = [fw.sb("r_u%d" % i, [128, D]) for i in range(2)]; upb = [fw.sb("r_up%d" % i, [128, D]) for i in range(2)]
        unb = [fw.sb("r_un%d" % i, [128, D]) for i in range(2)]
        uTt = [fw.sb("r_uT%d" % i, [128, 8, 128]) for i in range(2)]; nTt = [fw.sb("r_nT%d" % i, [128, 8, 128]) for i in range(2)]
        for t in range(NT):
            u = ub[t % 2]; up = upb[t % 2]; un = unb[t % 2]; uT = uTt[t % 2]; nT = nTt[t % 2]
            r0 = t * 128
            fw.dma('sp', u, u[:, :], UD, UD[r0:r0 + 128, :])
            fw.op('pool', lambda e: e.memset(up[:, :], 0.0), [], [up])
            fw.op('pool', lambda e: e.memset(un[:, :], 0.0), [], [un])
            fw.dma('sp', up, up[1:128, :], UD, UD[r0:r0 + 127, :])
            if t not in (0, 2):
                fw.dma('sp', up, up[0:1, :], UD, UD[r0 - 1:r0, :])
            fw.dma('act', un, un[0:127, :], UD, UD[r0 + 1:r0 + 128, :])
            if t not in (1, NT - 1):
                fw.dma('act', un, un[127:128, :], UD, UD[r0 + 128:r0 + 129, :])
            fw.op('dve', lambda e: e.tensor_tensor(up[:, :], up[:, :], un[:, :], ALU.add), [up, un], [up])
            transpose_tile(k, u, uT, lambda c0, c1: uT[:, c0:c1, :], k.PS[0], k.PS[1])
            transpose_tile(k, up, nT, lambda c0, c1: nT[:, c0:c1, :], k.PS[2], k.PS[3])
            fw.dma('pool', UT, fmv(UT)[:, :, r0:r0 + 128], uT, uT[:, :, :])
            fw.dma('pool', NBT, fmv(NBT)[:, :, r0:r0 + 128], nT, nT[:, :, :])
    if k.stop_after == 'r_1b':
        return
    with fw.scope():
        mu = fw.sb("r_mu", [128, 6, 8]); om = fw.sb("r_om", [128, 6, 8]); hm = fw.sb("r_hm", [128, 6, 8])
        fw.dma('sp', mu, mu[:, :, :], k.mucol, k.mucol[:, :, :])
        fw.op('dve', lambda e: e.tensor_scalar(om[:, :, :], mu[:, :, :], -1.0, 1.0, ALU.mult, ALU.add), [mu], [om])
        fw.op('dve', lambda e: e.tensor_scalar(hm[:, :, :], mu[:, :, :], 0.5, None, ALU.mult), [mu], [hm])
        ublk = [fw.sb("r_ub%d" % i, [128, 8, 256]) for i in range(2)]; nblk = [fw.sb("r_nb%d" % i, [128, 8, 256]) for i in range(2)]
        xm = [fw.sb("r_xm%d" % i, [128, 8, 256]) for i in range(2)]
        stage = [fw.sb("r_stg%d" % i, [128, 8, 256]) for i in range(2)]
        cnt = [0]

        def get_xm(b, m):
            i = cnt[0] % 2; cnt[0] += 1
            ub_, nb_, x_ = ublk[i], nblk[i], xm[i]
            fw.dma('sp', ub_, ub_[:, :, :], UT, fmv(UT)[:, :, b * 256:(b + 1) * 256])
            fw.dma('sp', nb_, nb_[:, :, :], NBT, fmv(NBT)[:, :, b * 256:(b + 1) * 256])
            for c in range(8):
                fw.op('act', lambda e: e.activation(x_[:, c, :], ub_[:, c, :], AF.Copy, scale=om[:, m, c:c + 1]), [ub_, om], [x_])
                fw.op('dve', lambda e: e.scalar_tensor_tensor(x_[:, c, :], nb_[:, c, :], hm[:, m, c:c + 1], x_[:, c, :], ALU.mult, ALU.add), [nb_, hm, x_], [x_])
            return x_

        def tokview(st):
            return st[:, :, :].rearrange("p a n -> p (a n)").rearrange("p (a n) -> p a n", a=2)

        with fw.scope():
            wt = fw.sb("r_w", [128, 8, D])
            for job, (mi, dst) in enumerate(((0, RT), (2, KT), (3, VT))):
                fw.dma('sp', wt, wt[:, 0:4, :], W['rk_w_rkv'], W['rk_w_rkv'][0, job, 0:512, :].rearrange("(c p) n -> p c n", p=128))
                fw.dma('act', wt, wt[:, 4:8, :], W['rk_w_rkv'], W['rk_w_rkv'][0, job, 512:1024, :].rearrange("(c p) n -> p c n", p=128))
                for b in range(NB):
                    x_ = get_xm(b, mi); st = stage[b % 2]
                    if job < 2:
                        for oc in range(8):
                            ps = k.PS[oc % 4]
                            for c in range(8):
                                fw.op('pe', lambda e: e.matmul(ps[:, 0:256], wt[:, c, oc * 128:(oc + 1) * 128], x_[:, c, :], start=(c == 0), stop=(c == 7)), [wt, x_], [ps])
                            if oc % 2 == 0:
                                fw.op('dve', lambda e: e.tensor_copy(st[:, oc, :], ps[:, 0:256]), [ps], [st])
                            else:
                                fw.op('act', lambda e: e.activation(st[:, oc, :], ps[:, 0:256], AF.Copy), [ps], [st])
                        fw.dma('pool', dst, fmv(dst)[:, :, b * 256:(b + 1) * 256], st, st[:, :, :])
                    else:
                        tv = tokview(st)
                        for tt in range(2):
                            for half in range(2):
                                ps = k.PS[4 + (tt * 2 + half) % 4]
                                for c in range(8):
                                    fw.op('pe', lambda e: e.matmul(ps[:, :], x_[:, c, tt * 128:(tt + 1) * 128], wt[:, c, half * 512:(half + 1) * 512], start=(c == 0), stop=(c == 7)), [x_, wt], [ps])
                                if half == 0:
                                    fw.op('dve', lambda e: e.tensor_copy(tv[:, tt, 0:512], ps[:, :]), [ps], [st])
                                else:
                                    fw.op('act', lambda e: e.activation(tv[:, tt, 512:1024], ps[:, :], AF.Copy), [ps], [st])
                        fw.dma('pool', dst, dst[b * 256:(b + 1) * 256, :].rearrange("(a p) n -> p a n", p=128), st, tv)
        with fw.scope():
            d0bc = fw.sb("r_d0", [128, D]); w1 = fw.sb("r_w1", [128, 8, 64]); w2 = fw.sb("r_w2", [64, D]); t1 = fw.sb("r_t1", [64, 256])
            i0 = fw.sb("r_i0", [128, 2, 8]); fw.dma('sp', i0, i0[:, :, :], k.icl0col, k.icl0col[:, :, :])
            for d in range(2):
                load_bc_row(k, d0bc, W['rk_decay0'], W['rk_decay0'][0, d:d + 1, :])
                fw.dma('sp', w1, w1[:, :, :], W['rk_decay1'], W['rk_decay1'][0, d].rearrange("(c p) n -> p c n", p=128))
                fw.dma('sp', w2, w2[:, :], W['rk_decay2'], W['rk_decay2'][0, d])
                for b in range(NB):
                    x_ = get_xm(b, 1); st = stage[b % 2]; tv = tokview(st)
                    ps = k.PS[0]
                    for c in range(8):
                        fw.op('pe', lambda e: e.matmul(ps[0:64, 0:256], w1[:, c, :], x_[:, c, :], start=(c == 0), stop=(c == 7)), [w1, x_], [ps])
                    fw.op('act', lambda e: e.activation(t1[:, :], ps[0:64, 0:256], AF.Tanh), [ps], [t1])
                    for tt in range(2):
                        for half in range(2):
                            ps2 = k.PS[4 + (tt * 2 + half) % 4]
                            fw.op('pe', lambda e: e.matmul(ps2[:, :], t1[0:64, tt * 128:(tt + 1) * 128], w2[0:64, half * 512:(half + 1) * 512], start=True, stop=True), [t1, w2], [ps2])
                            fw.op('dve', lambda e: e.tensor_tensor(tv[:, tt, half * 512:(half + 1) * 512], ps2[:, :], d0bc[:, half * 512:(half + 1) * 512], ALU.add), [ps2, d0bc], [st])
                    fw.op('act', lambda e: e.activation(st[:, :, :], st[:, :, :], AF.Sigmoid), [st], [st])
                    fw.dma('pool', SG[d], SG[d][b * 256:(b + 1) * 256, :].rearrange("(a p) n -> p a n", p=128), st, tv)
            for d in range(2):
                fw.dma('sp', w1, w1[:, :, :], W['rk_iclr1'], W['rk_iclr1'][0, d].rearrange("(c p) n -> p c n", p=128))
                fw.dma('sp', w2, w2[:, :], W['rk_iclr2'], W['rk_iclr2'][0, d])
                for b in range(NB):
                    x_ = get_xm(b, 4); st = stage[b % 2]
                    ps = k.PS[0]
                    for c in range(8):
                        fw.op('pe', lambda e: e.matmul(ps[0:64, 0:256], w1[:, c, :], x_[:, c, :], start=(c == 0), stop=(c == 7)), [w1, x_], [ps])
                    fw.op('act', lambda e: e.activation(t1[:, :], ps[0:64, 0:256], AF.Copy), [ps], [t1])
                    for oc in range(8):
                        ps2 = k.PS[4 + oc % 4]
                        fw.op('pe', lambda e: e.matmul(ps2[:, 0:256], w2[0:64, oc * 128:(oc + 1) * 128], t1[0:64, :], start=True, stop=True), [t1, w2], [ps2])
                        fw.op('act', lambda e: e.activation(st[:, oc, :], ps2[:, 0:256], AF.Sigmoid, bias=i0[:, d, oc:oc + 1]), [ps2, i0], [st])
                    fw.dma('pool', AD[d], fmv(AD[d])[:, :, b * 256:(b + 1) * 256], st, st[:, :, :])
        with fw.scope():
            g1 = fw.sb("r_g1", [128, 8, 160]); g2a = fw.sb("r_g2a", [128, D]); g2b = fw.sb("r_g2b", [32, D])
            sa = fw.sb("r_sa", [128, 256]); sbb = fw.sb("r_sb", [32, 256])
            fw.dma('sp', g1, g1[:, :, :], W['rk_gate1'], W['rk_gate1'][0].rearrange("(c p) n -> p c n", p=128))
            fw.dma('sp', g2a, g2a[:, :], W['rk_gate2'], W['rk_gate2'][0, 0:128, :]); fw.dma('sp', g2b, g2b[:, :], W['rk_gate2'], W['rk_gate2'][0, 128:160, :])
            for b in range(NB):
                x_ = get_xm(b, 5); st = stage[b % 2]; tv = tokview(st)
                ps = k.PS[0]; psb = k.PS[1]
                for c in range(8):
                    fw.op('pe', lambda e: e.matmul(ps[:, 0:256], g1[:, c, 0:128], x_[:, c, :], start=(c == 0), stop=(c == 7)), [g1, x_], [ps])
                for c in range(8):
                    fw.op('pe', lambda e: e.matmul(psb[0:32, 0:256], g1[:, c, 128:160], x_[:, c, :], start=(c == 0), stop=(c == 7)), [g1, x_], [psb])
                fw.op('act', lambda e: e.activation(sa[:, :], ps[:, 0:256], AF.Sigmoid), [ps], [sa])
                fw.op('act', lambda e: e.activation(sbb[:, :], psb[0:32, 0:256], AF.Sigmoid), [psb], [sbb])
                for tt in range(2):
                    for half in range(2):
                        ps2 = k.PS[4 + (tt * 2 + half) % 4]
                        fw.op('pe', lambda e: e.matmul(ps2[:, :], sa[:, tt * 128:(tt + 1) * 128], g2a[:, half * 512:(half + 1) * 512], start=True, stop=False), [sa, g2a], [ps2])
                        fw.op('pe', lambda e: e.matmul(ps2[:, :], sbb[0:32, tt * 128:(tt + 1) * 128], g2b[0:32, half * 512:(half + 1) * 512], start=False, stop=True), [sbb, g2b], [ps2])
                        if half == 0:
                            fw.op('dve', lambda e: e.tensor_copy(tv[:, tt, 0:512], ps2[:, :]), [ps2], [st])
                        else:
                            fw.op('act', lambda e: e.activation(tv[:, tt, 512:1024], ps2[:, :], AF.Copy), [ps2], [st])
                fw.dma('pool', GT, GT[b * 256:(b + 1) * 256, :].rearrange("(a p) n -> p a n", p=128), st, tv)
    if k.stop_after == 'r_1c':
        return
    with fw.scope():
        kkc = fw.sb("r_kkc", [128, 8]); kac = fw.sb("r_kac", [128, 8]); rkc = fw.sb("r_rkc", [128, 8])
        fw.dma('sp', kkc, kkc[:, :], k.kkcol, k.kkcol[:, :]); fw.dma('sp', kac, kac[:, :], k.kacol, k.kacol[:, :]); fw.dma('sp', rkc, rkc[:, :], k.rkcol, k.rkcol[:, :])
        blk = fw.sb("r_blk", [128, 128]); fw.dma('sp', blk, blk[:, :], k.C["blk64"], k.C["blk64"][:, :])
        sel2 = fw.sb("r_sel2", [128, 2]); fw.dma('sp', sel2, sel2[:, :], k.C["sel2"], k.C["sel2"][:, :])
        tiny = fw.sb("r_tiny", [128, 1]); fw.op('pool', lambda e: e.memset(tiny[:, :], 0.0), [], [tiny])
        kt = fw.sb("r2_k", [128, 8, 256]); rt = fw.sb("r2_r", [128, 8, 256]); a0 = fw.sb("r2_a0", [128, 8, 256]); a1 = fw.sb("r2_a1", [128, 8, 256])
        kkt = fw.sb("r2_kk", [128, 8, 256]); tm = fw.sb("r2_tm", [128, 8, 256]); ks = fw.sb("r2_ks", [128, 8, 256]); bon = fw.sb("r2_bon", [128, 2, 16])
        for b in range(NB):
            cs = slice(b * 256, (b + 1) * 256)
            fw.dma('sp', kt, kt[:, :, :], KT, fmv(KT)[:, :, cs]); fw.dma('act', rt, rt[:, :, :], RT, fmv(RT)[:, :, cs])
            fw.dma('sp', a0, a0[:, :, :], AD[0], fmv(AD[0])[:, :, cs]); fw.dma('act', a1, a1[:, :, :], AD[1], fmv(AD[1])[:, :, cs])
            for c in range(8):
                fw.op('act', lambda e: e.activation(kkt[:, c, :], kt[:, c, :], AF.Copy, scale=kkc[:, c:c + 1]), [kt, kkc], [kkt])
            fw.op('pool', lambda e: e.tensor_tensor(tm[:, :, :], kkt[:, :, :], kkt[:, :, :], ALU.mult), [kkt], [tm])
            for c in range(8):
                ps = k.PS[c % 4]
                fw.op('pe', lambda e: e.matmul(ps[:, 0:256], blk[:, :], tm[:, c, :], start=True, stop=True), [blk, tm], [ps])
                fw.op('dve', lambda e: e.tensor_scalar(ks[:, c, :], ps[:, 0:256], 1e-24, None, ALU.max), [ps], [ks])
            fw.op('act', lambda e: e.activation(ks[:, :, :], ks[:, :, :], AF.Sqrt, bias=tiny[:, 0:1], scale=1.0), [ks, tiny], [ks])
            fw.op('dve', lambda e: e.reciprocal(ks[:, :, :], ks[:, :, :]), [ks], [ks])
            fw.op('dve', lambda e: e.tensor_tensor(kkt[:, :, :], kkt[:, :, :], ks[:, :, :], ALU.mult), [kkt, ks], [kkt])
            fw.op('act', lambda e: e.activation(tm[:, :, :], kkt[:, :, :], AF.Copy, scale=-1.0), [kkt], [tm])
            fw.dma('pool', AT, fmv(AT)[:, :, cs], tm, tm[:, :, :])
            fw.op('pool', lambda e: e.memset(ks[:, :, :], 0.0), [], [ks])
            for d, ad in enumerate((a0, a1)):
                fw.op('dve', lambda e: e.tensor_tensor(tm[:, :, :], kkt[:, :, :], ad[:, :, :], ALU.mult), [kkt, ad], [tm])
                fw.dma('pool', BT[d], fmv(BT[d])[:, :, cs], tm, tm[:, :, :])
                for c in range(8):
                    fw.op('dve', lambda e: e.tensor_scalar(ad[:, c, :], ad[:, c, :], -1.0, kac[:, c:c + 1], ALU.add, ALU.mult), [ad, kac], [ad])
                fw.op('dve', lambda e: e.tensor_scalar(ad[:, :, :], ad[:, :, :], 1.0, None, ALU.add), [ad], [ad])
                fw.op('dve', lambda e: e.tensor_tensor(ad[:, :, :], ad[:, :, :], kt[:, :, :], ALU.mult), [ad, kt], [ad])
                fw.dma('pool', KD[d], fmv(KD[d])[:, :, cs], ad, ad[:, :, :])
                fw.op('dve', lambda e: e.tensor_tensor(ks[:, :, :], ks[:, :, :], ad[:, :, :], ALU.add), [ks, ad], [ks])
            fw.op('dve', lambda e: e.tensor_tensor(ks[:, :, :], ks[:, :, :], rt[:, :, :], ALU.mult), [ks, rt], [ks])
            for c in range(8):
                fw.op('act', lambda e: e.activation(ks[:, c, :], ks[:, c, :], AF.Copy, scale=rkc[:, c:c + 1]), [ks, rkc], [ks])
            pb = k.PS[4 + b % 2]
            for tt in range(2):
                for c in range(8):
                    fw.op('pe', lambda e: e.matmul(pb[:, tt * 16 + 2 * c:tt * 16 + 2 * c + 2], ks[:, c, tt * 128:(tt + 1) * 128], sel2[:, :], start=True, stop=True), [ks, sel2], [pb])
            fw.op('dve', lambda e: e.tensor_copy(bon[:, :, :], pb[:, 0:32].rearrange("p (a n) -> p a n", a=2)), [pb], [bon])
            fw.dma('pool', BON, BON[b * 256:(b + 1) * 256, :].rearrange("(a p) n -> p a n", p=128), bon, bon[:, :, :])
    if k.stop_after == 'r_2':
        return
    with fw.scope():
        ident = k.ident
        triI = [fw.sb("s_triI%d" % d, [128, 128]) for d in range(2)]; triE = [fw.sb("s_triE%d" % d, [128, 128]) for d in range(2)]
        msi2 = [fw.sb("s_msi%d" % d, [128, 4, 128]) for d in range(2)]; mn = [fw.sb("s_mn%d" % d, [128, 128]) for d in range(2)]
        for d, sfx in enumerate(("f", "b")):
            fw.dma('sp', triI[d], triI[d][:, :], k.C["triI_" + sfx], k.C["triI_" + sfx][:, :])
            fw.dma('sp', triE[d], triE[d][:, :], k.C["triE_" + sfx], k.C["triE_" + sfx][:, :])
            fw.dma('sp', msi2[d], msi2[d][:, 0:2, :], k.C["msi_" + sfx], k.C["msi_" + sfx][:, :, :])
            fw.dma('sp', msi2[d], msi2[d][:, 2:4, :], k.C["msi_" + sfx], k.C["msi_" + sfx][:, :, :])
            fw.dma('sp', mn[d], mn[d][:, :], k.C["mn_" + sfx], k.C["mn_" + sfx][:, :])
        lng = fw.sb("s_lng", [128, D]); lnb = fw.sb("s_lnb", [128, D])
        load_bc_row(k, lng, W['rk_lnx'], W['rk_lnx'][0, 0:1, :]); load_bc_row(k, lnb, W['rk_lnx'], W['rk_lnx'][0, 1:2, :])
        epsx = fw.sb("s_eps", [128, 1]); fw.op('pool', lambda e: e.memset(epsx[:, :], 64e-5), [], [epsx])
        U4 = range(4)
        F = [[fw.sb("s_F%d_%d" % (u, i), [64, 4, 128]) for i in range(2)] for u in U4]
        Vb = [[fw.sb("s_V%d_%d" % (u, i), [128, 64]) for i in range(2)] for u in U4]
        Sb = [[fw.sb("s_S%d_%d" % (u, i), [128, 64]) for i in range(2)] for u in U4]
        PI4 = fw.sb("s_PI", [64, 4, 128]); PE4 = fw.sb("s_PE", [64, 4, 128]); PV4 = fw.sb("s_PV", [64, 4, 128])
        AR = [fw.sb("s_AR%d" % u, [64, 2, 128]) for u in U4]; BK = [fw.sb("s_BK%d" % u, [64, 2, 128]) for u in U4]
        TK = [fw.sb("s_TK%d" % u, [128, 4, 128]) for u in U4]; NM = [fw.sb("s_NM%d" % u, [128, 128]) for u in U4]
        SQ = [[fw.sb("s_SQ%d_%d" % (u, i), [128, 2, 128]) for i in range(2)] for u in U4]
        Wsb = [fw.sb("s_W%d" % u, [128, 64]) for u in U4]; W1s = [fw.sb("s_W1%d" % u, [128, 64]) for u in U4]
        BKt = [fw.sb("s_BKt%d" % u, [128, 2, 64]) for u in U4]
        Hst = [fw.sb("s_H%d" % u, [64, 64]) for u in U4]
        Yacc = [fw.sb("s_Y%d" % hh, [128, 16, 64]) for hh in range(2)]
        ytmp = fw.sb("s_ytmp", [128, 64]); y1s = fw.sb("s_y1s", [128, 64])
        fin = fw.sb("s_fin", [128, 16, 64]); fsq = fw.sb("s_fsq", [128, 16, 64]); fst = fw.sb("s_fst", [128, 16, 4])
        vfin = fw.sb("s_vfin", [128, 16, 64]); gfin = fw.sb("s_gfin", [128, 16, 64]); bfin = fw.sb("s_bfin", [128, 16, 16])
        order = [list(range(NT)), [1, 0] + list(range(NT - 1, 1, -1))]
        for pair in range(8):
            if k.stop_after == "r_scan1" and pair > 0:
                break
            for u in U4:
                fw.op('pool', lambda e: e.memset(Hst[u][:, :], 0.0), [], [Hst[u]])
            ywritten = [set(), set()]
            for it in range(NT if k.scan_stop is None else 1):
                units = [(u, pair * 2 + u // 2, u % 2, order[u % 2][it]) for u in U4]
                i2 = it % 2
                for (u, h, d, c) in units:
                    f = F[u][i2]; rs = slice(h * 64, (h + 1) * 64); cs = slice(c * 128, (c + 1) * 128)
                    fw.dma('sp', f, f[:, 0, :], RT, RT[rs, cs]); fw.dma('sp', f, f[:, 1, :], KD[d], KD[d][rs, cs])
                    fw.dma('sp', f, f[:, 2, :], AT, AT[rs, cs]); fw.dma('sp', f, f[:, 3, :], BT[d], BT[d][rs, cs])
                    fw.dma('act', Vb[u][i2], Vb[u][i2][:, :], VT, VT[cs, rs]); fw.dma('act', Sb[u][i2], Sb[u][i2][:, :], SG[d], SG[d][cs, rs])
                if k.scan_stop is not None and k.scan_stop < 1:
                    break
                for (u, h, d, c) in units:
                    fw.op('pe', lambda e: e.matmul(k.PS[3][0:64, u * 128:(u + 1) * 128], Sb[u][i2][:, :], triI[d][:, :], start=True, stop=True), [Sb[u][i2], triI[d]], [k.PS[3]])
                    fw.op('pe', lambda e: e.matmul(k.PS[4][0:64, u * 128:(u + 1) * 128], Sb[u][i2][:, :], triE[d][:, :], start=True, stop=True), [Sb[u][i2], triE[d]], [k.PS[4]])
                fw.op('act', lambda e: e.activation(PI4[:, :, :], k.PS[3][0:64, :].rearrange("p (a n) -> p a n", a=4), AF.Exp), [k.PS[3]], [PI4])
                fw.op('act', lambda e: e.activation(PV4[:, :, :], k.PS[3][0:64, :].rearrange("p (a n) -> p a n", a=4), AF.Exp, scale=-1.0), [k.PS[3]], [PV4])
                fw.op('act', lambda e: e.activation(PE4[:, :, :], k.PS[4][0:64, :].rearrange("p (a n) -> p a n", a=4), AF.Exp), [k.PS[4]], [PE4])
                if k.scan_stop is not None and k.scan_stop < 2:
                    break
                for (u, h, d, c) in units:
                    f = F[u][i2]
                    fw.op('dve', lambda e: e.tensor_tensor(AR[u][:, 0, :], f[:, 2, :], PE4[:, u, :], ALU.mult), [f, PE4], [AR[u]])
                    fw.op('pool', lambda e: e.tensor_tensor(AR[u][:, 1, :], f[:, 0, :], PI4[:, u, :], ALU.mult), [f, PI4], [AR[u]])
                    fw.op('dve', lambda e: e.tensor_tensor(BK[u][:, 0, :], f[:, 3, :], PV4[:, u, :], ALU.mult), [f, PV4], [BK[u]])
                    fw.op('pool', lambda e: e.tensor_tensor(BK[u][:, 1, :], f[:, 1, :], PV4[:, u, :], ALU.mult), [f, PV4], [BK[u]])
                if k.scan_stop is not None and k.scan_stop < 3:
                    break
                for (u, h, d, c) in units:
                    ar2 = AR[u][:, :, :].rearrange("p a n -> p (a n)")
                    fw.op('pe', lambda e: e.matmul(k.PS[1][:, u * 128:(u + 1) * 128], AR[u][:, 0, :], BK[u][:, 0, :], start=True, stop=True), [AR[u], BK[u]], [k.PS[1]])
                    fw.op('pe', lambda e: e.matmul(k.PS[0][:, 0:256], BK[u][:, 0, :], ar2, start=True, stop=True), [AR[u], BK[u]], [k.PS[0]])
                    fw.op('pe', lambda e: e.matmul(k.PS[0][:, 256:512], BK[u][:, 1, :], ar2, start=True, stop=True), [AR[u], BK[u]], [k.PS[0]])
                    fw.op('dve', lambda e: e.tensor_tensor(TK[u][:, :, :], k.PS[0][:, :].rearrange("p (a n) -> p a n", a=4), msi2[d][:, :, :], ALU.mult), [k.PS[0], msi2[d]], [TK[u]])
                if k.scan_stop is not None and k.scan_stop < 4:
                    break
                for (u, h, d, c) in units:
                    fw.op('dve', lambda e: e.tensor_tensor(NM[u][:, :], k.PS[1][:, u * 128:(u + 1) * 128], mn[d][:, :], ALU.mult), [k.PS[1], mn[d]], [NM[u]])
                if k.scan_stop is not None and k.scan_stop < 5:
                    break
                for (u, h, d, c) in units:
                    fw.op('pe', lambda e: e.matmul(k.PS[2][:, u * 64:(u + 1) * 64], AR[u][:, 0, :], Hst[u][:, :], start=True, stop=True), [AR[u], Hst[u]], [k.PS[2]])
                    fw.op('pe', lambda e: e.matmul(k.PS[5][:, u * 64:(u + 1) * 64], TK[u][:, 2, :], Vb[u][i2][:, :], start=True, stop=True), [TK[u], Vb[u][i2]], [k.PS[5]])
                for (u, h, d, c) in units:
                    fw.op('act', lambda e: e.activation(W1s[u][:, :], k.PS[2][:, u * 64:(u + 1) * 64], AF.Copy), [k.PS[2]], [W1s[u]])
                    fw.op('dve', lambda e: e.tensor_tensor(Wsb[u][:, :], k.PS[5][:, u * 64:(u + 1) * 64], W1s[u][:, :], ALU.add), [k.PS[5], W1s[u]], [Wsb[u]])
                if k.scan_stop is not None and k.scan_stop < 6:
                    break
                Ncur = {u: (NM[u][:, :], TK[u][:, 0, :], [NM[u], TK[u]]) for u in U4}
                for lvl in range(7):
                    for (u, h, d, c) in units:
                        N_, NT_, deps = Ncur[u]
                        fw.op('pe', lambda e: e.matmul(k.PS[5][:, u * 64:(u + 1) * 64], NT_, Wsb[u][:, :], start=True, stop=True), deps + [Wsb[u]], [k.PS[5]])
                        if lvl < 6:
                            fw.op('pe', lambda e: e.matmul(k.PS[6 + u // 2][:, (u % 2) * 256:(u % 2) * 256 + 128], NT_, N_, start=True, stop=True), deps, [k.PS[6 + u // 2]])
                            fw.op('pe', lambda e: e.matmul(k.PS[6 + u // 2][:, (u % 2) * 256 + 128:(u % 2) * 256 + 256], N_, NT_, start=True, stop=True), deps, [k.PS[6 + u // 2]])
                    for (u, h, d, c) in units:
                        fw.op('dve', lambda e: e.tensor_tensor(Wsb[u][:, :], Wsb[u][:, :], k.PS[5][:, u * 64:(u + 1) * 64], ALU.add), [k.PS[5], Wsb[u]], [Wsb[u]])
                        if lvl < 6:
                            sq = SQ[u][lvl % 2]
                            fw.op('act', lambda e: e.activation(sq[:, :, :], k.PS[6 + u // 2][:, (u % 2) * 256:(u % 2) * 256 + 256].rearrange("p (a n) -> p a n", a=2), AF.Copy), [k.PS[6 + u // 2]], [sq])
                            Ncur[u] = (sq[:, 0, :], sq[:, 1, :], [sq])
                if k.scan_stop is not None and k.scan_stop < 7:
                    break
                for (u, h, d, c) in units:
                    if c < 2:
                        continue
                    hh = u // 2
                    fw.op('pe', lambda e: e.matmul(k.PS[2][:, 256 + u * 64:256 + (u + 1) * 64], AR[u][:, 1, :], Hst[u][:, :], start=True, stop=True), [AR[u], Hst[u]], [k.PS[2]])
                    fw.op('pe', lambda e: e.matmul(k.PS[5][:, 256 + u * 64:256 + (u + 1) * 64], TK[u][:, 1, :], Wsb[u][:, :], start=True, stop=False), [TK[u], Wsb[u]], [k.PS[5]])
                    fw.op('pe', lambda e: e.matmul(k.PS[5][:, 256 + u * 64:256 + (u + 1) * 64], TK[u][:, 3, :], Vb[u][i2][:, :], start=False, stop=True), [TK[u], Vb[u][i2]], [k.PS[5]])
                    fw.op('act', lambda e: e.activation(y1s[:, :], k.PS[2][:, 256 + u * 64:256 + (u + 1) * 64], AF.Copy), [k.PS[2]], [y1s])
                    if c in ywritten[hh]:
                        fw.op('dve', lambda e: e.tensor_tensor(ytmp[:, :], k.PS[5][:, 256 + u * 64:256 + (u + 1) * 64], y1s[:, :], ALU.add), [k.PS[5], y1s], [ytmp])
                        fw.op('dve', lambda e: e.tensor_tensor(Yacc[hh][:, c - 2, :], Yacc[hh][:, c - 2, :], ytmp[:, :], ALU.add), [Yacc[hh], ytmp], [Yacc[hh]])
                    else:
                        fw.op('dve', lambda e: e.tensor_tensor(Yacc[hh][:, c - 2, :], k.PS[5][:, 256 + u * 64:256 + (u + 1) * 64], y1s[:, :], ALU.add), [k.PS[5], y1s], [Yacc[hh]])
                        ywritten[hh].add(c)
                if k.scan_stop is not None and k.scan_stop < 8:
                    break
                for (u, h, d, c) in units:
                    fw.op('pe', lambda e: e.matmul(k.PS[1][:, u * 128:u * 128 + 64], BK[u][:, 0, :], ident[0:64, 0:64], start=True, stop=True), [BK[u], ident], [k.PS[1]])
                    fw.op('pe', lambda e: e.matmul(k.PS[1][:, u * 128 + 64:u * 128 + 128], BK[u][:, 1, :], ident[0:64, 0:64], start=True, stop=True), [BK[u], ident], [k.PS[1]])
                for (u, h, d, c) in units:
                    fw.op('act', lambda e: e.activation(BKt[u][:, :, :], k.PS[1][:, u * 128:(u + 1) * 128].rearrange("p (a n) -> p a n", a=2), AF.Copy), [k.PS[1]], [BKt[u]])
                for (u, h, d, c) in units:
                    fw.op('pe', lambda e: e.matmul(k.PS[4][0:64, u * 64:(u + 1) * 64], BKt[u][:, 0, :], Wsb[u][:, :], start=True, stop=False), [BKt[u], Wsb[u]], [k.PS[4]])
                    fw.op('pe', lambda e: e.matmul(k.PS[4][0:64, u * 64:(u + 1) * 64], BKt[u][:, 1, :], Vb[u][i2][:, :], start=False, stop=True), [BKt[u], Vb[u][i2]], [k.PS[4]])
                for (u, h, d, c) in units:
                    pc = PI4[:, u, 127:128] if d == 0 else PI4[:, u, 0:1]
                    fw.op('dve', lambda e: e.tensor_tensor(Hst[u][:, :], Hst[u][:, :], k.PS[4][0:64, u * 64:(u + 1) * 64], ALU.add), [k.PS[4], Hst[u]], [Hst[u]])
                    fw.op('dve', lambda e: e.tensor_scalar(Hst[u][:, :], Hst[u][:, :], pc, None, ALU.mult), [Hst[u], PI4], [Hst[u]])
            if k.scan_stop is not None:
                break
            for hh in range(2):
                h = pair * 2 + hh; rs = slice(h * 64, (h + 1) * 64)
                Y = Yacc[hh]
                fw.dma('sp', vfin, vfin[:, :, :], VT, VT[256:NTOK, rs].rearrange("(a p) n -> p a n", p=128))
                fw.dma('act', gfin, gfin[:, :, :], GT, GT[256:NTOK, rs].rearrange("(a p) n -> p a n", p=128))
                fw.dma('sp', bfin, bfin[:, :, :], BON, BON[256:NTOK, :].rearrange("(a p) n -> p a n", p=128))
                fw.op('dve', lambda e: e.tensor_reduce(fst[:, :, 0], Y[:, :, :], AX.X, ALU.add), [Y], [fst])
                fw.op('dve', lambda e: e.tensor_scalar(fst[:, :, 0], fst[:, :, 0], -1.0 / 64, None, ALU.mult), [fst], [fst])
                fw.op('dve', lambda e: e.tensor_tensor(fin[:, :, :], Y[:, :, :], fst[:, :, 0:1].to_broadcast([128, 16, 64]), ALU.add), [Y, fst], [fin])
                fw.op('pool', lambda e: e.tensor_tensor(fsq[:, :, :], fin[:, :, :], fin[:, :, :], ALU.mult), [fin], [fsq])
                fw.op('dve', lambda e: e.tensor_reduce(fst[:, :, 1], fsq[:, :, :], AX.X, ALU.add), [fsq], [fst])
                fw.op('act', lambda e: e.activation(fst[:, :, 2], fst[:, :, 1], AF.Sqrt, bias=epsx[:, 0:1], scale=1.0 / 64), [fst, epsx], [fst])
                fw.op('dve', lambda e: e.reciprocal(fst[:, :, 2], fst[:, :, 2]), [fst], [fst])
                fw.op('dve', lambda e: e.tensor_tensor(fin[:, :, :], fin[:, :, :], fst[:, :, 2:3].to_broadcast([128, 16, 64]), ALU.mult), [fin, fst], [fin])
                fw.op('dve', lambda e: e.tensor_tensor(fin[:, :, :], fin[:, :, :], lng[:, rs].unsqueeze(1).to_broadcast([128, 16, 64]), ALU.mult), [fin, lng], [fin])
                fw.op('dve', lambda e: e.tensor_tensor(fin[:, :, :], fin[:, :, :], lnb[:, rs].unsqueeze(1).to_broadcast([128, 16, 64]), ALU.add), [fin, lnb], [fin])
                fw.op('dve', lambda e: e.tensor_tensor(vfin[:, :, :], vfin[:, :, :], bfin[:, :, h:h + 1].to_broadcast([128, 16, 64]), ALU.mult), [vfin, bfin], [vfin])
                fw.op('dve', lambda e: e.tensor_tensor(fin[:, :, :], fin[:, :, :], vfin[:, :, :], ALU.add), [fin, vfin], [fin])
                fw.op('dve', lambda e: e.tensor_tensor(fin[:, :, :], fin[:, :, :], gfin[:, :, :], ALU.mult), [fin, gfin], [fin])
                fw.dma('pool', k.AO, k.AO[256:NTOK, rs].rearrange("(a p) n -> p a n", p=128), fin, fin[:, :, :])


_NC_CACHE = {}


def make_in_maps(inputs, used=None):
    consts = host_consts()
    maps = []
    shared = {n: np.ascontiguousarray(np.asarray(inputs[n], dtype=np.float32)) for n in W_SHAPES}
    def col(v):
        return np.ascontiguousarray(np.asarray(v, np.float32).reshape(8, 128).T)
    shared["mucol"] = np.ascontiguousarray(np.stack([col(shared["rk_mu"][0, m]) for m in range(6)], 1))
    shared["icl0col"] = np.ascontiguousarray(np.stack([col(shared["rk_iclr0"][0, d]) for d in range(2)], 1))
    shared["kkcol"] = col(shared["rk_k_k"][0]); shared["kacol"] = col(shared["rk_k_a"][0]); shared["rkcol"] = col(shared["rk_r_k"][0].reshape(-1))
    shared["sinkb"] = np.ascontiguousarray(np.broadcast_to(shared["att_sink"].reshape(1, 8), (128, 8)))
    for n, v in consts.items():
        shared["c_" + n] = np.ascontiguousarray(v.astype(np.float32))
    x = np.asarray(inputs['x'], np.float32); c = np.asarray(inputs['c'], np.float32)
    ctx = np.asarray(inputs['ctx'], np.float32); c_ctx = np.asarray(inputs['c_ctx'], np.float32)
    for b in range(8):
        m = dict(shared)
        m["x"] = np.ascontiguousarray(x[b]); m["ctx"] = np.ascontiguousarray(ctx[b])
        cc = np.stack([c[b], c_ctx], -1).reshape(8, 128, 2).transpose(1, 0, 2)
        m["cc"] = np.ascontiguousarray(cc)
        if used is not None:
            m = {n: m[n] for n in used}
        maps.append(m)
    return maps


def kernel(**inputs):
    if "nc" not in _NC_CACHE:
        _NC_CACHE["nc"] = build()
    nc = _NC_CACHE["nc"]
    res = run_bass_kernel_spmd(nc, make_in_maps(inputs, nc._used()), core_ids=list(range(8)))
    return np.stack([np.asarray(r["out"]) for r in res.results], 0).astype(np.float32)
```
